# Optimizing a Trainium2 kernel written in Bass

```python
import math
import jax, jax.numpy as jnp
from jax import lax
import numpy as np

D_MODEL = 1024
BATCH = 8
SEQ = 4096
DEPTH = 2

N_META = 16
MLA_HEADS = 8
QK_NOPE = 128
QK_ROPE = 64
V_HEAD = 128
Q_RANK = 256
KV_RANK = 128
ROPE_THETA = 10000.0
Q_BLOCK = 128
LRU_WIDTH = D_MODEL
LRU_BLOCKS = 8
LRU_BLOCK_DIM = LRU_WIDTH // LRU_BLOCKS
LRU_CONV = 4
LRU_C = 8.0
D_FF = 2816
FFN_CONV = 3
DN_ALPHA = (2.0 * DEPTH) ** 0.25
DN_BETA = (8.0 * DEPTH) ** -0.25
LN_EPS = 1e-5
RMS_EPS = 1e-6

IN_PARTS = [Q_RANK, KV_RANK, QK_ROPE, LRU_WIDTH, LRU_WIDTH, D_MODEL, D_MODEL]
IN_COLS = sum(IN_PARTS)
IN_SPLITS = [int(v) for v in np.cumsum(IN_PARTS)[:-1]]

kernel_name = 'hybrid_mla_rglru_convffn_encoder'


def layer_norm(x, g, b):
    xf = x.astype(jnp.float32)
    mu = jnp.mean(xf, axis=-1, keepdims=True)
    xc = xf - mu
    var = jnp.mean(jnp.square(xc), axis=-1, keepdims=True)
    return (xc * lax.rsqrt(var + LN_EPS) * g.astype(jnp.float32) + b.astype(jnp.float32)).astype(x.dtype)


def rms_norm(x, g):
    xf = x.astype(jnp.float32)
    ms = jnp.mean(jnp.square(xf), axis=-1, keepdims=True)
    return (xf * lax.rsqrt(ms + RMS_EPS) * g.astype(jnp.float32)).astype(x.dtype)


def apply_rope(x, cos, sin):
    half = x.shape[-1] // 2
    x1, x2 = x[..., :half], x[..., half:]
    return jnp.concatenate([x1 * cos - x2 * sin, x2 * cos + x1 * sin], axis=-1)


def dwconv(x, w, b, pad_left):
    k, c = w.shape
    y = lax.conv_general_dilated(x, w[:, None, :].astype(x.dtype), window_strides=(1,),
                                 padding=[(pad_left, k - 1 - pad_left)],
                                 dimension_numbers=('NWC', 'WIO', 'NWC'), feature_group_count=c)
    return y + b.astype(x.dtype)


def mla_branch(cq, ckv, kr, q_norm, kv_norm, w_uq, w_uk, w_uv, cos, sin):
    bsz, t = cq.shape[0], cq.shape[1]
    cq = rms_norm(cq, q_norm)
    ckv = rms_norm(ckv, kv_norm)
    q = jnp.einsum('btr,rhd->bthd', cq, w_uq)
    q_nope = q[..., :QK_NOPE]
    q_rope = apply_rope(q[..., QK_NOPE:], cos[:, None, :], sin[:, None, :])
    k_nope = jnp.einsum('btr,rhd->bthd', ckv, w_uk)
    v = jnp.einsum('btr,rhd->bthd', ckv, w_uv)
    k_rope = apply_rope(kr, cos, sin)
    scale = 1.0 / math.sqrt(QK_NOPE + QK_ROPE)
    n_blk = -(-t // Q_BLOCK)
    pad = n_blk * Q_BLOCK - t

    def to_blocks(a):
        a = jnp.pad(a * scale, ((0, 0), (0, pad), (0, 0), (0, 0)))
        a = a.reshape(bsz, n_blk, Q_BLOCK, a.shape[2], a.shape[3])
        return jnp.moveaxis(a, 1, 0)

    def attend(blk):
        qn, qr = blk
        s = jnp.einsum('bqhd,bkhd->bhqk', qn, k_nope) + jnp.einsum('bqhr,bkr->bhqk', qr, k_rope)
        p = jax.nn.softmax(s.astype(jnp.float32), axis=-1).astype(v.dtype)
        return jnp.einsum('bhqk,bkhd->bqhd', p, v)

    o = lax.map(attend, (to_blocks(q_nope), to_blocks(q_rope)))
    o = jnp.moveaxis(o, 0, 1).reshape(bsz, n_blk * Q_BLOCK, MLA_HEADS * V_HEAD)
    return o[:, :t]


def _lin_comb(c1, c2):
    a1, b1 = c1
    a2, b2 = c2
    return a1 * a2, a2 * b1 + b2


def rg_lru(xc, w_rg, b_rg, w_ig, b_ig, lam):
    bsz, t, w = xc.shape
    xg = xc.reshape(bsz, t, LRU_BLOCKS, LRU_BLOCK_DIM)
    r = jax.nn.sigmoid(jnp.einsum('btgi,dgij->dbtgj', xg, w_rg).reshape(2, bsz, t, w) + b_rg[:, None, None, :])
    i = jax.nn.sigmoid(jnp.einsum('btgi,dgij->dbtgj', xg, w_ig).reshape(2, bsz, t, w) + b_ig[:, None, None, :])
    log_a = -LRU_C * r.astype(jnp.float32) * jax.nn.softplus(-lam.astype(jnp.float32))[:, None, None, :]
    a = jnp.exp(log_a)
    u = jnp.sqrt(-jnp.expm1(2.0 * log_a)) * (i * xc[None]).astype(jnp.float32)
    h_fwd = lax.associative_scan(_lin_comb, (a[0], u[0]), axis=1)[1]
    h_bwd = lax.associative_scan(_lin_comb, (a[1], u[1]), axis=1, reverse=True)[1]
    return (h_fwd + h_bwd).astype(xc.dtype)


def _normal(key, shape, fan_in, scale=1.0):
    return jax.random.normal(key, shape, jnp.float32) * (scale * fan_in ** -0.5)


def setup_inputs(seed: int = 0) -> dict:
    key = jax.random.key(seed)
    ks = jax.random.split(key, 32)
    L = DEPTH
    u = jax.random.uniform(ks[17], (L, 2, LRU_WIDTH), jnp.float32, 0.9, 0.999)
    s = u ** (1.0 / LRU_C)
    lam = jnp.log(s) - jnp.log1p(-s)
    return {
        'x': jax.random.normal(ks[0], (BATCH, SEQ, D_MODEL), jnp.float32),
        'meta_tokens': jax.random.normal(ks[1], (N_META, D_MODEL), jnp.float32),
        'ln0_g': 1.0 + 0.01 * jax.random.normal(ks[2], (D_MODEL,), jnp.float32),
        'ln0_b': 0.01 * jax.random.normal(ks[3], (D_MODEL,), jnp.float32),
        'w_in': _normal(ks[4], (L, D_MODEL, IN_COLS), D_MODEL),
        'q_norm': 1.0 + 0.01 * jax.random.normal(ks[5], (L, Q_RANK), jnp.float32),
        'kv_norm': 1.0 + 0.01 * jax.random.normal(ks[6], (L, KV_RANK), jnp.float32),
        'w_uq': _normal(ks[7], (L, Q_RANK, MLA_HEADS, QK_NOPE + QK_ROPE), Q_RANK),
        'w_uk': _normal(ks[8], (L, KV_RANK, MLA_HEADS, QK_NOPE), KV_RANK),
        'w_uv': _normal(ks[9], (L, KV_RANK, MLA_HEADS, V_HEAD), KV_RANK, DN_BETA),
        'w_o_mla': _normal(ks[10], (L, MLA_HEADS * V_HEAD, D_MODEL), MLA_HEADS * V_HEAD, DN_BETA),
        'lru_conv_w': _normal(ks[11], (L, LRU_CONV, LRU_WIDTH), LRU_CONV),
        'lru_conv_b': 0.01 * jax.random.normal(ks[12], (L, LRU_WIDTH), jnp.float32),
        'w_rg': _normal(ks[13], (L, 2, LRU_BLOCKS, LRU_BLOCK_DIM, LRU_BLOCK_DIM), LRU_BLOCK_DIM),
        'b_rg': 0.01 * jax.random.normal(ks[14], (L, 2, LRU_WIDTH), jnp.float32),
        'w_ig': _normal(ks[15], (L, 2, LRU_BLOCKS, LRU_BLOCK_DIM, LRU_BLOCK_DIM), LRU_BLOCK_DIM),
        'b_ig': 0.01 * jax.random.normal(ks[16], (L, 2, LRU_WIDTH), jnp.float32),
        'lru_lambda': lam,
        'w_o_lru': _normal(ks[18], (L, LRU_WIDTH, D_MODEL), LRU_WIDTH, DN_BETA),
        'w_out': _normal(ks[19], (L, D_MODEL, D_MODEL), D_MODEL, DN_BETA),
        'ln1_g': 1.0 + 0.01 * jax.random.normal(ks[20], (L, D_MODEL), jnp.float32),
        'ln1_b': 0.01 * jax.random.normal(ks[21], (L, D_MODEL), jnp.float32),
        'w_up': _normal(ks[22], (L, D_MODEL, 2 * D_FF), D_MODEL),
        'ffn_conv_w': _normal(ks[23], (L, FFN_CONV, 2 * D_FF), FFN_CONV),
        'ffn_conv_b': 0.01 * jax.random.normal(ks[24], (L, 2 * D_FF), jnp.float32),
        'w_down': _normal(ks[25], (L, D_FF, D_MODEL), D_FF, DN_BETA),
        'ln2_g': 1.0 + 0.01 * jax.random.normal(ks[26], (L, D_MODEL), jnp.float32),
        'ln2_b': 0.01 * jax.random.normal(ks[27], (L, D_MODEL), jnp.float32),
    }


def reference(x, meta_tokens, ln0_g, ln0_b, w_in, q_norm, kv_norm, w_uq, w_uk, w_uv, w_o_mla,
              lru_conv_w, lru_conv_b, w_rg, b_rg, w_ig, b_ig, lru_lambda, w_o_lru, w_out,
              ln1_g, ln1_b, w_up, ffn_conv_w, ffn_conv_b, w_down, ln2_g, ln2_b):
    bsz = x.shape[0]
    meta = jnp.broadcast_to(meta_tokens[None].astype(x.dtype), (bsz, N_META, x.shape[-1]))
    h = layer_norm(jnp.concatenate([meta, x], axis=1), ln0_g, ln0_b)
    t = h.shape[1]
    half = QK_ROPE // 2
    inv_freq = jnp.exp(-math.log(ROPE_THETA) * jnp.arange(half, dtype=jnp.float32) / half)
    ang = jnp.arange(t, dtype=jnp.float32)[:, None] * inv_freq[None, :]
    cos = jnp.cos(ang).astype(h.dtype)
    sin = jnp.sin(ang).astype(h.dtype)

    for l in range(DEPTH):
        proj = h @ w_in[l]
        cq, ckv, kr, lru_g, lru_x, g_mla, g_lru = jnp.split(proj, IN_SPLITS, axis=-1)
        y_mla = mla_branch(cq, ckv, kr, q_norm[l], kv_norm[l], w_uq[l], w_uk[l], w_uv[l], cos, sin) @ w_o_mla[l]
        xc = dwconv(lru_x, lru_conv_w[l], lru_conv_b[l], LRU_CONV // 2)
        y_lru = (jax.nn.gelu(lru_g) * rg_lru(xc, w_rg[l], b_rg[l], w_ig[l], b_ig[l], lru_lambda[l])) @ w_o_lru[l]
        z = jax.nn.sigmoid(g_mla) * y_mla + jax.nn.sigmoid(g_lru) * y_lru
        h = layer_norm(DN_ALPHA * h + z @ w_out[l], ln1_g[l], ln1_b[l])
        up = dwconv(h @ w_up[l], ffn_conv_w[l], ffn_conv_b[l], FFN_CONV // 2)
        gate, val = jnp.split(up, [D_FF], axis=-1)
        f = (jax.nn.gelu(gate) * val) @ w_down[l]
        h = layer_norm(DN_ALPHA * h + f, ln2_g[l], ln2_b[l])

    return h[:, N_META:]
```

```python
import math
import numpy as np
import concourse.bass as bass
import concourse.mybir as mybir
from concourse.bass_utils import run_bass_kernel_spmd

F32 = mybir.dt.float32
BF16 = mybir.dt.bfloat16
AF = mybir.ActivationFunctionType
ALU = mybir.AluOpType

D = 1024
KC = 8
NMETA = 16
H = 8
L = 2
DFF = 2816
NFC = 44
NFP = 22
INCOLS = 4544
C_CQ, C_CKV, C_KR, C_LG, C_LX, C_GM, C_GL = 0, 256, 384, 448, 1472, 2496, 3520
DN_ALPHA = (2.0 * L) ** 0.25
LN_EPS = 1e-5
RMS_EPS = 1e-6
ATT_SCALE = 1.0 / math.sqrt(192.0)
GELU_C = math.sqrt(2.0 / math.pi)
GELU_SQA = math.sqrt(0.044715)

SAME_ENGINE_SYNC = True
EPOCH = 16000
N_DMA_SEM = {'sp': 20, 'pool': 8, 'act': 4, 'pe': 1, 'dve': 1}
DMA_SEM_MAX_USES = 900


class Buf:
    __slots__ = ("name", "w", "r")

    def __init__(self, name=""):
        self.name = name
        self.w = None
        self.r = {}


class Op:
    __slots__ = ("eng", "fn", "deps", "dma", "needed", "ev", "seq", "prev_on_sem")

    def __init__(self, eng, fn, dma):
        self.eng = eng
        self.fn = fn
        self.dma = dma
        self.deps = []
        self.needed = bool(dma)
        self.ev = None
        self.seq = -1
        self.prev_on_sem = None


class Prog:
    ENGS = ("pe", "act", "dve", "pool", "sp")

    def __init__(self, nc):
        self.nc = nc
        self.ops = {e: [] for e in self.ENGS}
        self.known = {e: {f: -1 for f in self.ENGS} for e in self.ENGS}
        self.known_dma = {e: set() for e in self.ENGS}
        self.pending_dma = []
        self.last_real = {e: None for e in self.ENGS}

    def add(self, eng, fn, reads=(), writes=(), dma=False, extra_deps=(), force_same=False, track=True):
        op = Op(eng, fn, dma)
        op.seq = len(self.ops[eng])
        deps = {}
        for b in reads:
            if b.w is not None:
                deps[id(b.w)] = b.w
        for b in writes:
            if b.w is not None:
                deps[id(b.w)] = b.w
            for r in b.r.values():
                deps[id(r)] = r
        for d in extra_deps:
            if d is not None:
                deps[id(d)] = d
        kn = self.known[eng]
        for d in deps.values():
            if d is op:
                continue
            if d.dma:
                if id(d) in self.known_dma[eng]:
                    continue
                self.known_dma[eng].add(id(d))
                op.deps.append(d)
            else:
                if d.eng == eng and not dma and not force_same:
                    if d.eng == "pe" or not SAME_ENGINE_SYNC:
                        continue
                if d.seq <= kn[d.eng]:
                    continue
                kn[d.eng] = d.seq
                d.needed = True
                op.deps.append(d)
        for b in reads:
            key = ("dma", id(op)) if dma else eng
            b.r[key] = op
        for b in writes:
            b.w = op
            b.r = {}
        self.ops[eng].append(op)
        if dma:
            if track:
                self.pending_dma.append(op)
        else:
            self.last_real[eng] = op
        return op

    def barrier(self):
        pend = list(self.pending_dma)
        self.pending_dma = []
        lasts = [self.last_real[e] for e in self.ENGS]
        spop = self.add("sp", lambda e: e.nop(), extra_deps=pend + lasts, force_same=True)
        for e in ("pe", "act", "dve", "pool"):
            self.add(e, lambda g: g.nop(), extra_deps=[spop])

    def finalize_and_emit(self, block):
        nc = self.nc
        eng_sems = {e: [] for e in self.ENGS}
        for e in self.ENGS:
            c = 0
            for op in self.ops[e]:
                if not op.needed or op.dma:
                    continue
                c += 1
                ep = (c - 1) // EPOCH
                while len(eng_sems[e]) <= ep:
                    eng_sems[e].append(nc.alloc_semaphore(f"s_{e}_{len(eng_sems[e])}"))
                op.ev = (eng_sems[e][ep], (c - 1) % EPOCH + 1, 1)
        for e in self.ENGS:
            dma_sems = []
            dma_state = []
            dma_rr = 0
            for op in self.ops[e]:
                if not op.dma:
                    continue
                if len(dma_sems) < N_DMA_SEM[e]:
                    dma_sems.append(nc.alloc_semaphore(f"s_dma_{e}_{len(dma_sems)}"))
                    dma_state.append([0, None])
                k = dma_rr % len(dma_sems)
                if dma_state[k][0] >= DMA_SEM_MAX_USES:
                    dma_sems[k] = nc.alloc_semaphore(f"s_dma_{e}_{k}_{dma_rr}")
                    dma_state[k] = [0, None]
                dma_rr += 1
                dma_state[k][0] += 1
                op.prev_on_sem = dma_state[k][1]
                dma_state[k][1] = op
                op.ev = (dma_sems[k], 16 * dma_state[k][0], 16)

        def emit(e, engobj):
            waited = {}

            def w(ev):
                sem, val, _ = ev
                key = id(sem)
                if waited.get(key, 0) >= val:
                    return
                waited[key] = val
                engobj.wait_ge(sem, val)

            for op in self.ops[e]:
                for d in op.deps:
                    w(d.ev)
                if op.prev_on_sem is not None:
                    w(op.prev_on_sem.ev)
                ins = op.fn(engobj)
                if op.ev is not None:
                    ins.then_inc(op.ev[0], op.ev[2])

        block.sync(lambda eng: emit("sp", eng))
        block.tensor(lambda eng: emit("pe", eng))
        block.scalar(lambda eng: emit("act", eng))
        block.vector(lambda eng: emit("dve", eng))
        block.gpsimd(lambda eng: emit("pool", eng))


def MM(out, lhsT, rhs, start, stop):
    return lambda e: e.matmul(out, lhsT, rhs, start=start, stop=stop)


def TR(out, in_, ident):
    return lambda e: e.transpose(out, in_, ident)


def ACTF(out, in_, func, scale=None, bias=None):
    kw = {}
    if scale is not None:
        kw["scale"] = scale
    if bias is not None:
        kw["bias"] = bias
    return lambda e: e.activation(out, in_, func, **kw)


def TT(out, a, b, op):
    return lambda e: e.tensor_tensor(out, a, b, op)


def TS(out, a, s1, s2, op0, op1):
    return lambda e: e.tensor_scalar(out, a, s1, s2, op0, op1)


def TS1(out, a, s1, op0):
    return lambda e: e.tensor_scalar(out, a, s1, None, op0)


def STT(out, a, s, b, op0, op1):
    return lambda e: e.scalar_tensor_tensor(out, a, s, b, op0, op1)


def CP(out, in_):
    return lambda e: e.tensor_copy(out, in_)


def RCP(out, in_):
    return lambda e: e.reciprocal(out, in_)


def MSET(ap, v):
    return lambda e: e.memset(ap, v)


def DMA(out, in_):
    return lambda e: e.dma_start(out=out, in_=in_)


def SCAN(out, a, u, init):
    return lambda e: e.tensor_tensor_scan(out, a, u, init, ALU.mult, ALU.add)


def chunks(total, size):
    out = []
    t = 0
    while t < total:
        n = min(size, total - t)
        out.append((t, n))
        t += n
    return out


def col_layout():
    lay = {}
    off = 0

    def put(name, n):
        nonlocal off
        lay[name] = (off, n)
        off += n

    put("ln0_g", 8)
    put("ln0_b", 8)
    for l in range(L):
        put(f"qn{l}", 2)
        put(f"kvn{l}", 1)
        put(f"lcw{l}", 32)
        put(f"lcb{l}", 8)
        put(f"brg{l}", 16)
        put(f"big{l}", 16)
        put(f"lam{l}", 16)
        put(f"l1g{l}", 8)
        put(f"l1b{l}", 8)
        put(f"fcw{l}", 3 * NFC)
        put(f"fcb{l}", NFC)
        put(f"l2g{l}", 8)
        put(f"l2b{l}", 8)
    return lay, off


COL_LAY, NCOL = col_layout()


class Arena:
    def __init__(self, nc, units):
        self.t = nc.alloc_sbuf_tensor("arena", [128, units], BF16)
        self.units = units
        self.off = 0

    def mark(self):
        return self.off

    def reset(self, m):
        self.off = m

    def alloc(self, cols, dt):
        u = cols * (2 if dt == F32 else 1)
        u = (u + 1) // 2 * 2
        a = self.off
        self.off += u
        assert self.off <= self.units, f"SBUF arena overflow {self.off} > {self.units}"
        ap = self.t[:, a:a + u]
        if dt == F32:
            ap = ap.bitcast(F32)
        return ap


class Tl:
    def __init__(self, arena, K, n, dt, name="", perk=False):
        self.ap = arena.alloc(K * n, dt)
        self.K, self.n = K, n
        self.b = [Buf(f"{name}{k}") for k in range(K)] if perk else [Buf(name)] * K

    def s(self, k=0, a=0, b=None, p0=0, p1=128):
        if b is None:
            b = self.n
        return self.ap[p0:p1, k * self.n + a:k * self.n + b]

    def flat(self, a=0, b=None, p0=0, p1=128):
        if b is None:
            b = self.K * self.n
        return self.ap[p0:p1, a:b]

    def v3(self, n=None):
        if n is None or n == self.n:
            return self.ap.rearrange("p (k n) -> p k n", k=self.K)
        return self.ap.rearrange("p (k n) -> p k n", k=self.K)[:, :, 0:n]


def build(SEQ, dbg=False):
    T = SEQ + NMETA
    TCH = chunks(T, 512)
    KT = chunks(T, 128)
    FCH = chunks(T, 510)
    nc = bass.Bass("TRN2", target_bir_lowering=False)
    P = Prog(nc)

    def din(name, shape, dt=F32):
        return nc.dram_tensor(name, shape, dt, kind="ExternalInput").ap()

    def dscr(name, shape, dt):
        return nc.dram_tensor(name, shape, dt).ap()

    xT = din("xT", [D, SEQ])
    metaT = din("metaT", [D, NMETA])
    cols_d = din("cols", [128, NCOL])
    cs2_d = din("cs2", [2, 64, T])
    ident_d = din("ident", [128, 128])
    w_in = din("w_in", [L, D, INCOLS])
    w_uq = din("w_uq", [L, 256, 1536])
    w_uk = din("w_uk", [L, 128, 1024])
    w_uv = din("w_uv", [L, 128, 1024])
    w_o_mla = din("w_o_mla", [L, D, D])
    w_rg = din("w_rg", [L, 2048, 128])
    w_ig = din("w_ig", [L, 2048, 128])
    w_o_lru = din("w_o_lru", [L, D, D])
    w_out = din("w_out", [L, D, D])
    w_up = din("w_up", [L, D, 2 * DFF])
    w_down = din("w_down", [L, DFF, D])
    outT = nc.dram_tensor("outT", [D, SEQ], F32, kind="ExternalOutput").ap()

    b_w_in = dscr("b_w_in", [L, D, INCOLS], BF16)
    b_w_uq = dscr("b_w_uq", [L, 256, 1536], BF16)
    b_w_uv = dscr("b_w_uv", [L, 128, 1024], BF16)
    b_w_o_mla = dscr("b_w_o_mla", [L, D, D], BF16)
    b_w_rg = dscr("b_w_rg", [L, 2048, 128], BF16)
    b_w_ig = dscr("b_w_ig", [L, 2048, 128], BF16)
    b_w_o_lru = dscr("b_w_o_lru", [L, D, D], BF16)
    b_w_out = dscr("b_w_out", [L, D, D], BF16)
    b_w_up = dscr("b_w_up", [L, NFC, 128, KC * 128], BF16)
    b_w_down = dscr("b_w_down", [L, DFF, D], BF16)
    hres_d = dscr("hres", [D, T], F32)
    hbfA_d = dscr("hbfA", [D, T], BF16)
    hbfB_d = dscr("hbfB", [D, T], BF16)
    zm_d = dscr("zm", [D, T], BF16)
    ylin_d = dscr("ylin", [D, T], BF16)

    def fm(ap):
        return ap.rearrange("(k p) t -> p k t", p=128)

    dbg_outs = {}
    if dbg:
        for nm in ["d_h0", "d_hA1", "d_hE1"]:
            dbg_outs[nm] = nc.dram_tensor(nm, [D, T], F32, kind="ExternalOutput").ap()
        for nm in ["d_zm", "d_ylin"]:
            dbg_outs[nm] = nc.dram_tensor(nm, [D, T], BF16, kind="ExternalOutput").ap()

    WC = {}

    def cast2d(key, dst, src, R, C):
        for r0, rn in chunks(R, 1024):
            for c0, cn in chunks(C, 2048):
                b = Buf(key)
                WC.setdefault(key, []).append(b)
                P.add("pool", DMA(dst[r0:r0 + rn, c0:c0 + cn], src[r0:r0 + rn, c0:c0 + cn]), writes=[b], dma=True, track=False)

    def cast_layer_small(l):
        cast2d(("in", l), b_w_in[l], w_in[l], D, INCOLS)
        cast2d(("uq", l), b_w_uq[l], w_uq[l], 256, 1536)
        cast2d(("uv", l), b_w_uv[l], w_uv[l], 128, 1024)
        cast2d(("om", l), b_w_o_mla[l], w_o_mla[l], D, D)
        cast2d(("rg", l), b_w_rg[l], w_rg[l], 2048, 128)
        cast2d(("ig", l), b_w_ig[l], w_ig[l], 2048, 128)
        cast2d(("ol", l), b_w_o_lru[l], w_o_lru[l], D, D)
        cast2d(("out", l), b_w_out[l], w_out[l], D, D)

    def cast_layer_ffn(l):
        cast2d(("dn", l), b_w_down[l], w_down[l], DFF, D)
        for m in range(NFC):
            for k0 in range(0, KC, 4):
                src = w_up[l][k0 * 128:(k0 + 4) * 128, m * 128:(m + 1) * 128].rearrange("(k p) c -> p k c", p=128)
                dst = b_w_up[l][m][:, k0 * 128:(k0 + 4) * 128].rearrange("p (k c) -> p k c", c=128)
                b = Buf("up")
                WC.setdefault(("up", l, m), []).append(b)
                P.add("pool", DMA(dst, src), writes=[b], dma=True, track=False)

    cast_layer_small(0)

    AR = Arena(nc, 106400)
    psW = [nc.alloc_psum_tensor(f"psw{i}", [128, 1024], F32) for i in range(4)]
    ps = [psW[i // 2][:, (i % 2) * 512:(i % 2 + 1) * 512] for i in range(8)]
    psb = [Buf(f"ps{i}") for i in range(8)]

    colsT = Tl(AR, 1, NCOL, F32, "cols")
    dcols = Tl(AR, 1, L * 64, F32, "dcols")
    identF = Tl(AR, 1, 128, F32, "ident")
    onesD = Tl(AR, 1, 128, BF16, "onesD")
    ones256 = Tl(AR, 1, 128, BF16, "ones256")
    ones128 = Tl(AR, 1, 128, BF16, "ones128")
    ones1 = Tl(AR, 1, 128, BF16, "ones1")
    consts = Buf("consts")
    onecol = Tl(AR, 1, 2, F32, "onecol")

    def col(name, i=0, p0=0, p1=128):
        o, n = COL_LAY[name]
        return colsT.ap[p0:p1, o + i:o + i + 1]

    def dcol(l, i, p0=0, p1=128):
        return dcols.ap[p0:p1, l * 64 + i:l * 64 + i + 1]

    P.add("sp", DMA(colsT.ap, cols_d), writes=[consts], dma=True)
    P.add("sp", DMA(identF.ap, ident_d), writes=[consts], dma=True)
    P.add("pool", MSET(onesD.ap, 1.0 / 1024), writes=[consts])
    P.add("pool", MSET(ones256.ap, 1.0 / 256), writes=[consts])
    P.add("pool", MSET(ones128.ap, 1.0 / 128), writes=[consts])
    P.add("pool", MSET(ones1.ap, 1.0), writes=[consts])
    P.add("pool", MSET(onecol.ap, 1.0), writes=[consts])
    for l in range(L):
        o, _ = COL_LAY[f"lam{l}"]
        lam = colsT.ap[:, o:o + 16]
        d0 = dcols.ap[:, l * 64:l * 64 + 16]
        d1 = dcols.ap[:, l * 64 + 16:l * 64 + 32]
        P.add("act", ACTF(d0, lam, AF.Exp, scale=-1.0), reads=[consts], writes=[consts])
        P.add("dve", TS1(d0, d0, 1.0, ALU.add), reads=[consts], writes=[consts])
        P.add("act", ACTF(d0, d0, AF.Ln), reads=[consts], writes=[consts])
        P.add("dve", TS1(d1, d0, -4.0, ALU.mult), reads=[consts], writes=[consts])
        P.add("dve", TS1(d0, d0, -8.0, ALU.mult), reads=[consts], writes=[consts])
        o, _ = COL_LAY[f"brg{l}"]
        P.add("dve", TS1(dcols.ap[:, l * 64 + 32:l * 64 + 48], colsT.ap[:, o:o + 16], 0.5, ALU.mult), reads=[consts], writes=[consts])
        o, _ = COL_LAY[f"big{l}"]
        P.add("dve", TS1(dcols.ap[:, l * 64 + 48:l * 64 + 64], colsT.ap[:, o:o + 16], 0.5, ALU.mult), reads=[consts], writes=[consts])

    PERSIST = AR.mark()

    def ln_parts(y, n, gname, bname, ybf, ysq, st, psA, psB, hbf):
        yb = y.b[0]
        if isinstance(psA, int):
            apA, bA, apB, bB = ps[psA][:, 0:n], psb[psA], ps[psB][:, 0:n], psb[psB]
        else:
            (apA, bA), (apB, bB) = psA, psB

        def pre():
            P.add("act", ACTF(ybf.v3(n), y.v3(n), AF.Identity), reads=[yb], writes=[ybf.b[0]])
            P.add("act", ACTF(ysq.v3(n), y.v3(n), AF.Square), reads=[yb], writes=[ysq.b[0]])

        def pe():
            for k in range(KC):
                P.add("pe", MM(apA, onesD.ap, ybf.s(k, 0, n), k == 0, k == KC - 1), reads=[ybf.b[0], consts], writes=[bA])
            for k in range(KC):
                P.add("pe", MM(apB, onesD.ap, ysq.s(k, 0, n), k == 0, k == KC - 1), reads=[ysq.b[0], consts], writes=[bB])

        def post():
            mean, msq, var, rstd = (st.s(i, 0, n) for i in range(4))
            sb = st.b[0]
            P.add("dve", CP(mean, apA), reads=[bA], writes=[sb])
            P.add("dve", TT(msq, mean, mean, ALU.mult), reads=[sb], writes=[sb])
            P.add("dve", STT(var, apB, LN_EPS, msq, ALU.add, ALU.subtract), reads=[bB, sb], writes=[sb])
            P.add("act", ACTF(var, var, AF.Sqrt), reads=[sb], writes=[sb])
            P.add("dve", RCP(rstd, var), reads=[sb], writes=[sb])
            mean3 = mean.unsqueeze(1).broadcast_to([128, KC, n])
            rstd3 = rstd.unsqueeze(1).broadcast_to([128, KC, n])
            P.add("dve", TT(y.v3(n), y.v3(n), mean3, ALU.subtract), reads=[sb], writes=[yb])
            P.add("dve", TT(y.v3(n), y.v3(n), rstd3, ALU.mult), reads=[sb], writes=[yb])
            for k in range(KC):
                if hbf is not None:
                    P.add("act", ACTF(hbf.s(k, 0, n), y.s(k, 0, n), AF.Identity, scale=col(gname, k), bias=col(bname, k)), reads=[consts, yb], writes=[hbf.b[0]])
                P.add("act", ACTF(y.s(k, 0, n), y.s(k, 0, n), AF.Identity, scale=col(gname, k), bias=col(bname, k)), reads=[consts], writes=[yb])

        return pre, pe, post

    def ln_chunk(*a):
        pre, pe, post = ln_parts(*a)
        pre()
        pe()
        post()

    ys = [Tl(AR, KC, 512, F32, f"y{i}") for i in range(2)]
    hb = [Tl(AR, KC, 512, BF16, f"hb{i}") for i in range(2)]
    ybf = Tl(AR, KC, 512, BF16, "ybf")
    ysq = Tl(AR, KC, 512, BF16, "ysq")
    st = Tl(AR, 4, 512, F32, "st")
    for ci, (t0, n) in enumerate(TCH):
        y = ys[ci % 2]
        hbt = hb[ci % 2]
        if t0 == 0:
            P.add("sp", DMA(y.v3()[:, :, 0:NMETA], fm(metaT)), writes=[y.b[0]], dma=True)
            P.add("sp", DMA(y.v3()[:, :, NMETA:n], fm(xT)[:, :, 0:n - NMETA]), writes=[y.b[0]], dma=True)
        else:
            P.add("sp", DMA(y.v3(n), fm(xT)[:, :, t0 - NMETA:t0 - NMETA + n]), writes=[y.b[0]], dma=True)
        ln_chunk(y, n, "ln0_g", "ln0_b", ybf, ysq, st, 0, 1, hbt)
        P.add("sp", DMA(fm(hres_d)[:, :, t0:t0 + n], y.v3(n)), reads=[y.b[0]], dma=True)
        P.add("sp", DMA(fm(hbfA_d)[:, :, t0:t0 + n], hbt.v3(n)), reads=[hbt.b[0]], dma=True)
        if dbg:
            P.add("sp", DMA(fm(dbg_outs["d_h0"])[:, :, t0:t0 + n], y.v3(n)), reads=[y.b[0]], dma=True)
    cast_layer_ffn(0)
    P.barrier()
    AR.reset(PERSIST)

    for l in range(L):
        last = l == L - 1
        hT = Tl(AR, KC, T, BF16, "hT", perk=False)
        hTb = [Buf(f"hT{c}") for c in range(len(TCH))]
        for ci, (t0, n) in enumerate(TCH):
            P.add("sp", DMA(hT.v3()[:, :, t0:t0 + n], fm(hbfA_d)[:, :, t0:t0 + n]), writes=[hTb[ci]], dma=True)
        LAYER = AR.mark()

        cqn = Tl(AR, 2, T, BF16, "cqn")
        ckvn = Tl(AR, 1, T, BF16, "ckvn")
        krope = Tl(AR, 1, T, BF16, "krope")
        Vt = Tl(AR, len(KT), 128, BF16, "V")
        latb = [Buf(f"lat{c}") for c in range(len(TCH))]
        Vb = Buf("V")
        kpad = Buf("kpad")
        P.add("pool", MSET(krope.s(0, 0, T, 64, 128), 0.0), writes=[kpad])
        STA = AR.mark()
        wA = Tl(AR, KC, 448, BF16, "wA")
        P.add("sp", DMA(wA.v3(), b_w_in[l][:, 0:448].rearrange("(k p) n -> p k n", p=128)), reads=WC[("in", l)], writes=[wA.b[0]], dma=True)
        cqf = Tl(AR, 2, 512, F32, "cqf")
        cqs = Tl(AR, 2, 512, BF16, "cqs")
        ckf = Tl(AR, 1, 512, F32, "ckf")
        cks = Tl(AR, 1, 512, BF16, "cks")
        ckn = Tl(AR, 1, 512, F32, "ckn")
        krf = Tl(AR, 1, 512, F32, "krf")
        rot = Tl(AR, 1, 512, F32, "rot")
        cst = Tl(AR, 2, 512, F32, "cst")
        rq = Tl(AR, 1, 512, F32, "rq")
        rk = Tl(AR, 1, 512, F32, "rk")
        for ci, (t0, n) in enumerate(TCH):
            P.add("sp", DMA(cst.v3(n)[0:64], cs2_d[:, :, t0:t0 + n].rearrange("a p t -> p a t")), writes=[cst.b[0]], dma=True)
            for m in range(2):
                for k in range(KC):
                    P.add("pe", MM(ps[m][:, 0:n], wA.s(k, m * 128, (m + 1) * 128), hT.s(k, t0, t0 + n), k == 0, k == KC - 1),
                          reads=[wA.b[0], hTb[ci]], writes=[psb[m]])
            for k in range(KC):
                P.add("pe", MM(ps[2][:, 0:n], wA.s(k, 256, 384), hT.s(k, t0, t0 + n), k == 0, k == KC - 1), reads=[wA.b[0], hTb[ci]], writes=[psb[2]])
            for k in range(KC):
                P.add("pe", MM(ps[3][0:64, 0:n], wA.s(k, 384, 448), hT.s(k, t0, t0 + n), k == 0, k == KC - 1), reads=[wA.b[0], hTb[ci]], writes=[psb[3]])
            for m in range(2):
                P.add("act", ACTF(cqf.s(m, 0, n), ps[m][:, 0:n], AF.Identity), reads=[psb[m]], writes=[cqf.b[0]])
                P.add("act", ACTF(cqs.s(m, 0, n), ps[m][:, 0:n], AF.Square), reads=[psb[m]], writes=[cqs.b[0]])
            P.add("act", ACTF(ckf.s(0, 0, n), ps[2][:, 0:n], AF.Identity), reads=[psb[2]], writes=[ckf.b[0]])
            P.add("act", ACTF(cks.s(0, 0, n), ps[2][:, 0:n], AF.Square), reads=[psb[2]], writes=[cks.b[0]])
            P.add("dve", CP(krf.s(0, 0, n, 0, 64), ps[3][0:64, 0:n]), reads=[psb[3]], writes=[krf.b[0]])
            for m in range(2):
                P.add("pe", MM(ps[4][:, 0:n], ones256.ap, cqs.s(m, 0, n), m == 0, m == 1), reads=[cqs.b[0], consts], writes=[psb[4]])
            P.add("pe", MM(ps[5][:, 0:n], ones128.ap, cks.s(0, 0, n), True, True), reads=[cks.b[0], consts], writes=[psb[5]])
            P.add("dve", TS1(rq.s(0, 0, n), ps[4][:, 0:n], RMS_EPS, ALU.add), reads=[psb[4]], writes=[rq.b[0]])
            P.add("dve", TS1(rk.s(0, 0, n), ps[5][:, 0:n], RMS_EPS, ALU.add), reads=[psb[5]], writes=[rk.b[0]])
            P.add("act", ACTF(rq.s(0, 0, n), rq.s(0, 0, n), AF.Sqrt), reads=[], writes=[rq.b[0]])
            P.add("act", ACTF(rk.s(0, 0, n), rk.s(0, 0, n), AF.Sqrt), reads=[], writes=[rk.b[0]])
            P.add("dve", RCP(rq.s(0, 0, n), rq.s(0, 0, n)), reads=[], writes=[rq.b[0]])
            P.add("dve", RCP(rk.s(0, 0, n), rk.s(0, 0, n)), reads=[], writes=[rk.b[0]])
            for m in range(2):
                P.add("dve", STT(cqn.s(m, t0, t0 + n), cqf.s(m, 0, n), col(f"qn{l}", m), rq.s(0, 0, n), ALU.mult, ALU.mult),
                      reads=[cqf.b[0], rq.b[0], consts], writes=[latb[ci]])
            P.add("dve", STT(ckn.s(0, 0, n), ckf.s(0, 0, n), col(f"kvn{l}", 0), rk.s(0, 0, n), ALU.mult, ALU.mult),
                  reads=[ckf.b[0], rk.b[0], consts], writes=[ckn.b[0]])
            P.add("act", ACTF(ckvn.s(0, t0, t0 + n), ckn.s(0, 0, n), AF.Identity), reads=[ckn.b[0]], writes=[latb[ci]])
            P.add("act", ACTF(rot.s(0, 0, n, 0, 32), krf.s(0, 0, n, 32, 64), AF.Identity, scale=-1.0), reads=[krf.b[0]], writes=[rot.b[0]])
            P.add("act", ACTF(rot.s(0, 0, n, 32, 64), krf.s(0, 0, n, 0, 32), AF.Identity), reads=[krf.b[0]], writes=[rot.b[0]])
            P.add("dve", TT(krf.s(0, 0, n, 0, 64), krf.s(0, 0, n, 0, 64), cst.s(0, 0, n, 0, 64), ALU.mult), reads=[cst.b[0], rot.b[0]], writes=[krf.b[0]])
            P.add("dve", TT(rot.s(0, 0, n, 0, 64), rot.s(0, 0, n, 0, 64), cst.s(1, 0, n, 0, 64), ALU.mult), reads=[cst.b[0]], writes=[rot.b[0]])
            P.add("dve", TT(krope.s(0, t0, t0 + n, 0, 64), krf.s(0, 0, n, 0, 64), rot.s(0, 0, n, 0, 64), ALU.add), reads=[krf.b[0], rot.b[0]], writes=[latb[ci]])
            for j, (kt0, kn) in enumerate(KT):
                if kt0 < t0 or kt0 >= t0 + n:
                    continue
                a = kt0 - t0
                P.add("pe", TR(ps[6][0:kn, 0:128], ckn.s(0, a, a + kn), identF.ap), reads=[ckn.b[0], consts], writes=[psb[6]])
                P.add("act", ACTF(Vt.s(j, 0, 128, 0, kn), ps[6][0:kn, 0:128], AF.Identity), reads=[psb[6]], writes=[Vb])
        if l == 0:
            cast_layer_small(1)
            cast_layer_ffn(1)
        P.barrier()
        AR.reset(STA)

        wuq = Tl(AR, 2, 1536, BF16, "wuq")
        wukT = Tl(AR, H, 128, BF16, "wukT")
        wuv = Tl(AR, 1, 1024, BF16, "wuv")
        wom = Tl(AR, KC, 1024, BF16, "wom")
        wB = Buf("wB")
        P.add("sp", DMA(wuq.v3(), b_w_uq[l].rearrange("(k p) n -> p k n", p=128)), reads=WC[("uq", l)], writes=[wB], dma=True)
        P.add("sp", DMA(wuv.ap, b_w_uv[l]), reads=WC[("uv", l)], writes=[wB], dma=True)
        P.add("sp", DMA(wom.v3(), b_w_o_mla[l].rearrange("(k p) n -> p k n", p=128)), reads=WC[("om", l)], writes=[wB], dma=True)
        qn_t = [Tl(AR, 1, 512, BF16, f"qn{i}") for i in range(2)]
        qabs = Tl(AR, H, 512, BF16, "qabs", perk=True)
        qrf = [Tl(AR, 1, 512, F32, f"qrf{i}") for i in range(2)]
        qrot = [Tl(AR, 1, 512, F32, f"qrot{i}") for i in range(2)]
        qrope = Tl(AR, H, 512, BF16, "qrope", perk=True)
        cstq = Tl(AR, 2, 512, F32, "cstq")
        qpad = Buf("qpad")
        P.add("pool", MSET(qrope.flat(0, H * 512, 64, 128), 0.0), writes=[qpad])
        NPT = 3
        PT = [Tl(AR, 2, 512, BF16, f"PT{i}") for i in range(NPT)]
        rl = [Tl(AR, 1, 512, F32, f"rl{i}") for i in range(2)]
        olat = [Tl(AR, 1, 512, BF16, f"olat{i}") for i in range(2)]
        oT = qabs
        zm1 = Tl(AR, KC, 512, BF16, "zmT")
        wukF_ap = zm1.ap.bitcast(F32)[:, 0:1024]
        P.add("sp", DMA(wukF_ap, w_uk[l]), writes=[zm1.b[0]], dma=True)
        wukTb = Buf("wukT")
        for h in range(H):
            P.add("pe", TR(ps[5][:, 0:128], wukF_ap[:, h * 128:(h + 1) * 128], identF.ap), reads=[zm1.b[0], consts], writes=[psb[5]])
            P.add("act", ACTF(wukT.s(h), ps[5][:, 0:128], AF.Identity), reads=[psb[5]], writes=[wukTb])
        KG = []
        full = [(j, kt0, kn) for j, (kt0, kn) in enumerate(KT) if kn == 128]
        for i in range(0, len(full), 2):
            KG.append(full[i:i + 2])
        for j, (kt0, kn) in enumerate(KT):
            if kn != 128:
                KG.append([(j, kt0, kn)])
        NG = len(KG)
        sgb = [Buf("sg0"), Buf("sg1")]
        B_O, B_L, B_MA, B_MB = 4, 5, 6, 7
        gi = 0
        for ci, (t0, n) in enumerate(TCH):
            P.add("sp", DMA(cstq.v3(n)[0:64], cs2_d[:, :, t0:t0 + n].rearrange("a p t -> p a t")), writes=[cstq.b[0]], dma=True)
            for h in range(H):
                pa, pb_ = B_MA, B_MB
                qn = qn_t[h % 2]
                qf = qrf[h % 2]
                qo = qrot[h % 2]
                for k in range(2):
                    P.add("pe", MM(ps[pa][:, 0:n], wuq.s(k, h * 192, h * 192 + 128), cqn.s(k, t0, t0 + n), k == 0, k == 1),
                          reads=[wB, latb[ci]], writes=[psb[pa]])
                P.add("act", ACTF(qn.s(0, 0, n), ps[pa][:, 0:n], AF.Identity), reads=[psb[pa]], writes=[qn.b[0]])
                for k in range(2):
                    P.add("pe", MM(ps[pb_][0:64, 0:n], wuq.s(k, h * 192 + 128, h * 192 + 192), cqn.s(k, t0, t0 + n), k == 0, k == 1),
                          reads=[wB, latb[ci]], writes=[psb[pb_]])
                P.add("dve", CP(qf.s(0, 0, n, 0, 64), ps[pb_][0:64, 0:n]), reads=[psb[pb_]], writes=[qf.b[0]])
                P.add("pe", MM(ps[pa][:, 0:n], wukT.s(h), qn.s(0, 0, n), True, True), reads=[wukTb, qn.b[0]], writes=[psb[pa]])
                P.add("act", ACTF(qabs.s(h, 0, n), ps[pa][:, 0:n], AF.Identity), reads=[psb[pa]], writes=[qabs.b[h]])
                P.add("act", ACTF(qo.s(0, 0, n, 0, 32), qf.s(0, 0, n, 32, 64), AF.Identity, scale=-1.0), reads=[qf.b[0]], writes=[qo.b[0]])
                P.add("act", ACTF(qo.s(0, 0, n, 32, 64), qf.s(0, 0, n, 0, 32), AF.Identity), reads=[qf.b[0]], writes=[qo.b[0]])
                P.add("dve", TT(qf.s(0, 0, n, 0, 64), qf.s(0, 0, n, 0, 64), cstq.s(0, 0, n, 0, 64), ALU.mult), reads=[cstq.b[0], qo.b[0]], writes=[qf.b[0]])
                P.add("dve", TT(qo.s(0, 0, n, 0, 64), qo.s(0, 0, n, 0, 64), cstq.s(1, 0, n, 0, 64), ALU.mult), reads=[cstq.b[0]], writes=[qo.b[0]])
                P.add("dve", TT(qrope.s(h, 0, n, 0, 64), qf.s(0, 0, n, 0, 64), qo.s(0, 0, n, 0, 64), ALU.add), reads=[qf.b[0], qo.b[0]], writes=[qrope.b[h]])
            pending_vup = None

            def emit_S(h, g, gidx):
                sg = psW[gidx % 2]
                for ti, (j, kt0, kn) in enumerate(KG[g]):
                    kci = kt0 // 512
                    o_ = sg[0:kn, ti * 512:ti * 512 + n]
                    P.add("pe", MM(o_, ckvn.s(0, kt0, kt0 + kn), qabs.s(h, 0, n), True, False), reads=[latb[kci], qabs.b[h]], writes=[sgb[gidx % 2]])
                    P.add("pe", MM(o_, krope.s(0, kt0, kt0 + kn), qrope.s(h, 0, n), False, True), reads=[latb[kci], qrope.b[h], kpad, qpad], writes=[sgb[gidx % 2]])

            def emit_exp(g, gidx):
                sg = psW[gidx % 2]
                pt = PT[gidx % NPT]
                nt = len(KG[g])
                kn = KG[g][0][2]
                src = sg.rearrange("p (t c) -> p t c", t=2)[0:kn, 0:nt, 0:n]
                dst = pt.v3()[0:kn, 0:nt, 0:n]
                P.add("act", ACTF(dst, src, AF.Exp, scale=ATT_SCALE), reads=[sgb[gidx % 2]], writes=[pt.b[0]])

            def emit_PV(g, gidx):
                pt = PT[gidx % NPT]
                for ti, (j, kt0, kn) in enumerate(KG[g]):
                    first = (g == 0 and ti == 0)
                    lastt = (g == NG - 1 and ti == len(KG[g]) - 1)
                    P.add("pe", MM(ps[B_O][:, 0:n], Vt.s(j, 0, 128, 0, kn), pt.s(ti, 0, n, 0, kn), first, lastt), reads=[Vb, pt.b[0]], writes=[psb[B_O]])
                    P.add("pe", MM(ps[B_L][:, 0:n], ones1.s(0, 0, 128, 0, kn), pt.s(ti, 0, n, 0, kn), first, lastt), reads=[consts, pt.b[0]], writes=[psb[B_L]])

            def emit_vup(h):
                ol = olat[h % 2]
                P.add("pe", MM(ps[B_MA][:, 0:n], wuv.s(0, h * 128, (h + 1) * 128), ol.s(0, 0, n), True, True), reads=[wB, ol.b[0]], writes=[psb[B_MA]])
                P.add("act", ACTF(oT.s(h, 0, n), ps[B_MA][:, 0:n], AF.Identity), reads=[psb[B_MA]], writes=[oT.b[h]])

            for h in range(H):
                gbase = gi
                emit_S(h, 0, gbase)
                emit_exp(0, gbase)
                for g in range(1, NG):
                    emit_S(h, g, gbase + g)
                    emit_exp(g, gbase + g)
                    if g == 2 and pending_vup is not None:
                        emit_vup(pending_vup)
                        pending_vup = None
                    emit_PV(g - 1, gbase + g - 1)
                emit_PV(NG - 1, gbase + NG - 1)
                gi += NG
                r_ = rl[h % 2]
                ol = olat[h % 2]
                P.add("dve", RCP(r_.s(0, 0, n), ps[B_L][:, 0:n]), reads=[psb[B_L]], writes=[r_.b[0]])
                P.add("dve", TT(ol.s(0, 0, n), ps[B_O][:, 0:n], r_.s(0, 0, n), ALU.mult), reads=[psb[B_O], r_.b[0]], writes=[ol.b[0]])
                if pending_vup is not None:
                    emit_vup(pending_vup)
                pending_vup = h
            emit_vup(pending_vup)
            zo = zm1
            for m in range(KC):
                pa = B_MA + (m % 2)
                for hh in range(H):
                    P.add("pe", MM(ps[pa][:, 0:n], wom.s(hh, m * 128, (m + 1) * 128), oT.s(hh, 0, n), hh == 0, hh == H - 1),
                          reads=[wB] + [oT.b[hh]], writes=[psb[pa]])
                P.add("act", ACTF(zo.s(m, 0, n), ps[pa][:, 0:n], AF.Identity, scale=0.5), reads=[psb[pa]], writes=[zo.b[0]])
            P.add("sp", DMA(fm(zm_d)[:, :, t0:t0 + n], zo.v3(n)), reads=[zo.b[0]], dma=True)
            if dbg and l == 0:
                P.add("sp", DMA(fm(dbg_outs["d_zm"])[:, :, t0:t0 + n], zo.v3(n)), reads=[zo.b[0]], dma=True)
        P.barrier()
        AR.reset(LAYER)

        TP = T + 4
        xpad = Tl(AR, 1, TP, F32, "xpad")
        xc = Tl(AR, 1, T, F32, "xc")
        xcb = Tl(AR, 1, T, BF16, "xcb")
        t_a = Tl(AR, 1, T, F32, "t_a")
        t_i = Tl(AR, 1, T, F32, "t_i")
        t_g = Tl(AR, 1, T, F32, "t_g")
        hf = Tl(AR, 1, T, F32, "hf")
        hbk = t_g
        wlx = Tl(AR, KC, 128, BF16, "wlx")
        wlg = Tl(AR, KC, 128, BF16, "wlg")
        wgt = Tl(AR, 4, 128, BF16, "wgt")
        lg_f = [Tl(AR, 1, 512, F32, f"lgf{i}") for i in range(2)]
        lg_s = [Tl(AR, 1, 512, F32, f"lgs{i}") for i in range(2)]
        ylo = Tl(AR, 1, T, BF16, "ylo")
        for cc in range(KC):
            wC = Buf("wC")
            P.add("sp", DMA(wlx.v3(), b_w_in[l][:, C_LX + cc * 128:C_LX + (cc + 1) * 128].rearrange("(k p) n -> p k n", p=128)), reads=WC[("in", l)], writes=[wlx.b[0]], dma=True)
            P.add("sp", DMA(wlg.v3(), b_w_in[l][:, C_LG + cc * 128:C_LG + (cc + 1) * 128].rearrange("(k p) n -> p k n", p=128)), reads=WC[("in", l)], writes=[wlg.b[0]], dma=True)
            for d in range(2):
                P.add("sp", DMA(wgt.s(d), b_w_rg[l][(d * 8 + cc) * 128:(d * 8 + cc + 1) * 128, :]), reads=WC[("rg", l)], writes=[wgt.b[0]], dma=True)
                P.add("sp", DMA(wgt.s(2 + d), b_w_ig[l][(d * 8 + cc) * 128:(d * 8 + cc + 1) * 128, :]), reads=WC[("ig", l)], writes=[wgt.b[0]], dma=True)
            P.add("pool", MSET(xpad.s(0, 0, 2), 0.0), writes=[xpad.b[0]])
            P.add("pool", MSET(xpad.s(0, T + 2, T + 4), 0.0), writes=[xpad.b[0]])
            for ci, (t0, n) in enumerate(TCH):
                pa = ci % 2
                for k in range(KC):
                    P.add("pe", MM(ps[pa][:, 0:n], wlx.s(k), hT.s(k, t0, t0 + n), k == 0, k == KC - 1), reads=[wlx.b[0], hTb[ci]], writes=[psb[pa]])
                P.add("act", ACTF(xpad.s(0, 2 + t0, 2 + t0 + n), ps[pa][:, 0:n], AF.Identity), reads=[psb[pa]], writes=[xpad.b[0]])
            P.add("act", ACTF(xc.s(0), xpad.s(0, 0, T), AF.Identity, scale=col(f"lcw{l}", 0 * 8 + cc), bias=col(f"lcb{l}", cc)),
                  reads=[xpad.b[0], consts], writes=[xc.b[0]])
            for j in range(1, 4):
                P.add("dve", STT(xc.s(0), xpad.s(0, j, j + T), col(f"lcw{l}", j * 8 + cc), xc.s(0), ALU.mult, ALU.add),
                      reads=[xpad.b[0], consts], writes=[xc.b[0]])
            P.add("act", ACTF(xcb.s(0), xc.s(0), AF.Identity), reads=[xc.b[0]], writes=[xcb.b[0]])
            for d in range(2):
                hdst = hf if d == 0 else hbk
                for ci, (t0, n) in enumerate(TCH):
                    pa, pb_ = 2 + (ci % 2), 4 + (ci % 2)
                    P.add("pe", MM(ps[pa][:, 0:n], wgt.s(d), xcb.s(0, t0, t0 + n), True, True), reads=[wgt.b[0], xcb.b[0]], writes=[psb[pa]])
                    P.add("pe", MM(ps[pb_][:, 0:n], wgt.s(2 + d), xcb.s(0, t0, t0 + n), True, True), reads=[wgt.b[0], xcb.b[0]], writes=[psb[pb_]])
                    P.add("act", ACTF(t_a.s(0, t0, t0 + n), ps[pa][:, 0:n], AF.Tanh, scale=0.5, bias=dcol(l, 32 + d * 8 + cc)),
                          reads=[psb[pa], consts], writes=[t_a.b[0]])
                    P.add("act", ACTF(t_i.s(0, t0, t0 + n), ps[pb_][:, 0:n], AF.Tanh, scale=0.5, bias=dcol(l, 48 + d * 8 + cc)),
                          reads=[psb[pb_], consts], writes=[t_i.b[0]])
                P.add("act", ACTF(t_g.s(0), t_a.s(0), AF.Exp, scale=dcol(l, d * 8 + cc), bias=dcol(l, d * 8 + cc)), reads=[t_a.b[0], consts], writes=[t_g.b[0]])
                P.add("act", ACTF(t_a.s(0), t_a.s(0), AF.Exp, scale=dcol(l, 16 + d * 8 + cc), bias=dcol(l, 16 + d * 8 + cc)), reads=[consts], writes=[t_a.b[0]])
                P.add("act", ACTF(t_g.s(0), t_g.s(0), AF.Sqrt, scale=-1.0, bias=onecol.ap[:, 0:1]), reads=[consts], writes=[t_g.b[0]])
                P.add("dve", STT(t_i.s(0), t_i.s(0), 1.0, xc.s(0), ALU.add, ALU.mult), reads=[xc.b[0]], writes=[t_i.b[0]])
                P.add("dve", STT(t_i.s(0), t_g.s(0), 0.5, t_i.s(0), ALU.mult, ALU.mult), reads=[t_g.b[0]], writes=[t_i.b[0]])
                if d == 0:
                    P.add("dve", SCAN(hdst.s(0), t_a.s(0), t_i.s(0), 0.0), reads=[t_a.b[0], t_i.b[0]], writes=[hdst.b[0]])
                else:
                    P.add("dve", SCAN(hdst.ap[:, ::-1], t_a.ap[:, ::-1], t_i.ap[:, ::-1], 0.0), reads=[t_a.b[0], t_i.b[0]], writes=[hdst.b[0]])
            P.add("dve", TT(hf.s(0), hf.s(0), hbk.s(0), ALU.add), reads=[hbk.b[0]], writes=[hf.b[0]])
            for ci, (t0, n) in enumerate(TCH):
                pa = 6 + (ci % 2)
                gf = lg_f[ci % 2]
                gs = lg_s[ci % 2]
                for k in range(KC):
                    P.add("pe", MM(ps[pa][:, 0:n], wlg.s(k), hT.s(k, t0, t0 + n), k == 0, k == KC - 1), reads=[wlg.b[0], hTb[ci]], writes=[psb[pa]])
                P.add("act", ACTF(gs.s(0, 0, n), ps[pa][:, 0:n], AF.Square, scale=GELU_SQA), reads=[psb[pa]], writes=[gs.b[0]])
                P.add("dve", STT(gs.s(0, 0, n), gs.s(0, 0, n), 1.0, ps[pa][:, 0:n], ALU.add, ALU.mult), reads=[psb[pa]], writes=[gs.b[0]])
                P.add("act", ACTF(gs.s(0, 0, n), gs.s(0, 0, n), AF.Tanh, scale=GELU_C), reads=[], writes=[gs.b[0]])
                P.add("dve", STT(gs.s(0, 0, n), gs.s(0, 0, n), 1.0, ps[pa][:, 0:n], ALU.add, ALU.mult), reads=[psb[pa]], writes=[gs.b[0]])
                P.add("dve", STT(ylo.s(0, t0, t0 + n), gs.s(0, 0, n), 0.5, hf.s(0, t0, t0 + n), ALU.mult, ALU.mult), reads=[gs.b[0], hf.b[0]], writes=[ylo.b[0]])
            P.add("sp", DMA(ylin_d[cc * 128:(cc + 1) * 128, :], ylo.s(0)), reads=[ylo.b[0]], dma=True)
            if dbg and l == 0:
                P.add("sp", DMA(dbg_outs["d_ylin"][cc * 128:(cc + 1) * 128, :], ylo.s(0)), reads=[ylo.b[0]], dma=True)
        P.barrier()
        AR.reset(LAYER)

        wol = Tl(AR, KC, 1024, BF16, "wol")
        wgl = Tl(AR, KC, 1024, BF16, "wgl")
        wot = Tl(AR, KC, 1024, BF16, "wot")
        wgm = Tl(AR, KC, 1024, BF16, "wgm")
        wD = Buf("wD")
        P.add("sp", DMA(wgm.v3(), b_w_in[l][:, C_GM:C_GM + D].rearrange("(k p) n -> p k n", p=128)), reads=WC[("in", l)], writes=[wD], dma=True)
        P.add("sp", DMA(wol.v3(), b_w_o_lru[l].rearrange("(k p) n -> p k n", p=128)), reads=WC[("ol", l)], writes=[wD], dma=True)
        P.add("sp", DMA(wgl.v3(), b_w_in[l][:, C_GL:C_GL + D].rearrange("(k p) n -> p k n", p=128)), reads=WC[("in", l)], writes=[wD], dma=True)
        P.add("sp", DMA(wot.v3(), b_w_out[l].rearrange("(k p) n -> p k n", p=128)), reads=WC[("out", l)], writes=[wD], dma=True)
        ND = 256
        DCH = chunks(T, ND)
        NDC = len(DCH)
        yl_t = [Tl(AR, KC, ND, BF16, f"yl{i}") for i in range(2)]
        zm_t = [Tl(AR, KC, ND, BF16, f"zmi{i}") for i in range(2)]
        hr_t = [Tl(AR, KC, ND, F32, f"hr{i}") for i in range(3)]
        hbo = Tl(AR, KC, ND, BF16, "hbD")
        thl = [Tl(AR, 1, ND, F32, f"thl{i}") for i in range(2)]
        zb = [Tl(AR, 1, ND, F32, f"zb{i}") for i in range(2)]
        thm2 = [Tl(AR, 1, ND, F32, f"thm2{i}") for i in range(2)]
        zb2 = [Tl(AR, 1, ND, F32, f"zb2{i}") for i in range(2)]
        zT_t = [Tl(AR, KC, ND, BF16, f"zT{i}", perk=True) for i in range(2)]
        ybfD = Tl(AR, KC, ND, BF16, "ybfD")
        ysqD = Tl(AR, KC, ND, BF16, "ysqD")
        stD = Tl(AR, 4, ND, F32, "stD")
        ln_state = {}

        def ph1(ci):
            t0, n = DCH[ci]
            hci = t0 // 512
            yl = yl_t[ci % 2]
            zmi = zm_t[ci % 2]
            hr = hr_t[ci % 3]
            zT = zT_t[ci % 2]
            P.add("sp", DMA(yl.v3(n), fm(ylin_d)[:, :, t0:t0 + n]), writes=[yl.b[0]], dma=True)
            P.add("sp", DMA(zmi.v3(n), fm(zm_d)[:, :, t0:t0 + n]), writes=[zmi.b[0]], dma=True)
            P.add("sp", DMA(hr.v3(n), fm(hres_d)[:, :, t0:t0 + n]), writes=[hr.b[0]], dma=True)
            for m in range(KC):
                pa, pb_, pc = (m % 2), 2 + (m % 2), 4 + (m % 2)
                for k in range(KC):
                    P.add("pe", MM(ps[pa][:, 0:n], wol.s(k, m * 128, (m + 1) * 128), yl.s(k, 0, n), k == 0, k == KC - 1), reads=[wD, yl.b[0]], writes=[psb[pa]])
                for k in range(KC):
                    P.add("pe", MM(ps[pb_][:, 0:n], wgl.s(k, m * 128, (m + 1) * 128), hT.s(k, t0, t0 + n), k == 0, k == KC - 1), reads=[wD, hTb[hci]], writes=[psb[pb_]])
                for k in range(KC):
                    P.add("pe", MM(ps[pc][:, 0:n], wgm.s(k, m * 128, (m + 1) * 128), hT.s(k, t0, t0 + n), k == 0, k == KC - 1), reads=[wD, hTb[hci]], writes=[psb[pc]])
                th = thl[m % 2]
                th2 = thm2[m % 2]
                z = zb[m % 2]
                z2 = zb2[m % 2]
                P.add("act", ACTF(th.s(0, 0, n), ps[pb_][:, 0:n], AF.Tanh, scale=0.5), reads=[psb[pb_]], writes=[th.b[0]])
                P.add("act", ACTF(th2.s(0, 0, n), ps[pc][:, 0:n], AF.Tanh, scale=0.5), reads=[psb[pc]], writes=[th2.b[0]])
                P.add("dve", STT(z.s(0, 0, n), th.s(0, 0, n), 1.0, ps[pa][:, 0:n], ALU.add, ALU.mult), reads=[th.b[0], psb[pa]], writes=[z.b[0]])
                P.add("dve", STT(z2.s(0, 0, n), th2.s(0, 0, n), 1.0, zmi.s(m, 0, n), ALU.add, ALU.mult), reads=[th2.b[0], zmi.b[0]], writes=[z2.b[0]])
                P.add("dve", STT(zT.s(m, 0, n), z.s(0, 0, n), 0.5, z2.s(0, 0, n), ALU.mult, ALU.add), reads=[z.b[0], z2.b[0]], writes=[zT.b[m]])

        def ph2(ci):
            t0, n = DCH[ci]
            hr = hr_t[ci % 3]
            zT = zT_t[ci % 2]
            for m in range(KC):
                for k in range(KC):
                    P.add("pe", MM(ps[6][:, 0:n], wot.s(k, m * 128, (m + 1) * 128), zT.s(k, 0, n), k == 0, k == KC - 1), reads=[wD, zT.b[k]], writes=[psb[6]])
                P.add("dve", STT(hr.s(m, 0, n), hr.s(m, 0, n), DN_ALPHA, ps[6][:, 0:n], ALU.mult, ALU.add), reads=[psb[6]], writes=[hr.b[0]])
            pre, pe_, post = ln_parts(hr, n, f"l1g{l}", f"l1b{l}", ybfD, ysqD, stD, (ps[7][:, 0:n], psb[7]), (ps[7][:, 256:256 + n], psb[7]), hbo)
            pre()
            ln_state[ci] = (pe_, post)

        def ph3(ci):
            t0, n = DCH[ci]
            hr = hr_t[ci % 3]
            pe_, post = ln_state.pop(ci)
            pe_()
            post()
            P.add("sp", DMA(fm(hres_d)[:, :, t0:t0 + n], hr.v3(n)), reads=[hr.b[0]], dma=True)
            P.add("sp", DMA(fm(hbfB_d)[:, :, t0:t0 + n], hbo.v3(n)), reads=[hbo.b[0]], dma=True)
            if dbg and l == 0:
                P.add("sp", DMA(fm(dbg_outs["d_hA1"])[:, :, t0:t0 + n], hr.v3(n)), reads=[hr.b[0]], dma=True)

        ph1(0)
        if NDC > 1:
            ph1(1)
        ph2(0)
        for ci in range(2, NDC):
            ph1(ci)
            ph3(ci - 2)
            ph2(ci - 1)
        if NDC > 1:
            ph3(NDC - 2)
            ph2(NDC - 1)
        ph3(NDC - 1)
        P.barrier()
        AR.reset(PERSIST)

        wdn = Tl(AR, NFP, 1024, BF16, "wdn")
        wE = Buf("wE")
        for j0, jn in chunks(NFP, 8):
            P.add("sp", DMA(wdn.v3()[:, j0:j0 + jn, :], b_w_down[l][j0 * 128:(j0 + jn) * 128, :].rearrange("(k p) n -> p k n", p=128)),
                  reads=WC[("dn", l)], writes=[wE], dma=True)
        dg = Tl(AR, 4 * NFC, 128, BF16, "dg")
        dgb = Buf("dg")
        id3 = identF.ap.unsqueeze(1).broadcast_to([128, NFC, 128])
        for tap in range(4):
            if tap < 3:
                o_, _ = COL_LAY[f"fcw{l}"]
                o_ += tap * NFC
            else:
                o_, _ = COL_LAY[f"fcb{l}"]
            c3 = colsT.ap[:, o_:o_ + NFC].unsqueeze(2).broadcast_to([128, NFC, 128])
            P.add("dve", TT(dg.v3()[:, tap * NFC:(tap + 1) * NFC, :], id3, c3, ALU.mult), reads=[consts], writes=[dgb])
        onesN = Tl(AR, 1, 512, BF16, "onesN")
        P.add("pool", MSET(onesN.ap, 1.0), writes=[dgb])
        NWB = 3
        wupg = [Tl(AR, KC, 128, BF16, f"wupg{i}") for i in range(NWB)]
        wupv = [Tl(AR, KC, 128, BF16, f"wupv{i}") for i in range(NWB)]
        hin = [Tl(AR, KC, 512, BF16, f"hin{i}") for i in range(2)]
        gact = Tl(AR, NFP, 510, BF16, "gact", perk=True)
        ug = [Tl(AR, 1, 512, BF16, f"ug{i}") for i in range(2)]
        uv = [Tl(AR, 1, 512, BF16, f"uv{i}") for i in range(2)]
        gq = [Tl(AR, 1, 512, F32, f"gq{i}") for i in range(2)]
        hrE = Tl(AR, KC, 512, F32, "hrE")
        hbE = Tl(AR, KC, 512, BF16, "hbE")
        ybfE = Tl(AR, KC, 512, BF16, "ybfE")
        ysqE = Tl(AR, KC, 512, BF16, "ysqE")
        stE = Tl(AR, 4, 512, F32, "stE")
        wi = 0
        deferred = []

        def emit_conv(ci, j, t0, n, off, nn):
            for (src, mcol, pb) in ((ug[j % 2], j, 4), (uv[j % 2], NFP + j, 5)):
                a = 1 if t0 == 0 else 0
                bnd = n - 1 if t0 + n == T else n
                P.add("pe", MM(ps[pb][:, 0:n], dg.s(1 * NFC + mcol), src.s(0, off, off + n), True, False), reads=[dgb, src.b[0]], writes=[psb[pb]])
                P.add("pe", MM(ps[pb][:, a:n], dg.s(0 * NFC + mcol), src.s(0, off + a - 1, off + n - 1), False, False), reads=[dgb, src.b[0]], writes=[psb[pb]])
                P.add("pe", MM(ps[pb][:, 0:bnd], dg.s(2 * NFC + mcol), src.s(0, off + 1, off + 1 + bnd), False, False), reads=[dgb, src.b[0]], writes=[psb[pb]])
                P.add("pe", MM(ps[pb][:, 0:n], dg.s(3 * NFC + mcol), onesN.s(0, 0, n), False, True), reads=[dgb], writes=[psb[pb]])
            q_ = gq[j % 2]
            P.add("act", ACTF(q_.s(0, 0, n), ps[4][:, 0:n], AF.Square, scale=GELU_SQA), reads=[psb[4]], writes=[q_.b[0]])
            P.add("dve", STT(q_.s(0, 0, n), q_.s(0, 0, n), 1.0, ps[4][:, 0:n], ALU.add, ALU.mult), reads=[psb[4]], writes=[q_.b[0]])
            P.add("act", ACTF(q_.s(0, 0, n), q_.s(0, 0, n), AF.Tanh, scale=GELU_C), reads=[], writes=[q_.b[0]])
            P.add("dve", STT(q_.s(0, 0, n), q_.s(0, 0, n), 1.0, ps[4][:, 0:n], ALU.add, ALU.mult), reads=[psb[4]], writes=[q_.b[0]])
            P.add("dve", STT(gact.s(j, 0, n), q_.s(0, 0, n), 0.5, ps[5][:, 0:n], ALU.mult, ALU.mult), reads=[q_.b[0], psb[5]], writes=[gact.b[j]])

        for ci, (t0, n) in enumerate(FCH):
            lo = max(t0 - 1, 0)
            hi = min(t0 + n + 1, T)
            nn = hi - lo
            off = t0 - lo
            hi_t = hin[ci % 2]
            P.add("sp", DMA(hi_t.v3(nn), fm(hbfB_d)[:, :, lo:hi]), writes=[hi_t.b[0]], dma=True)
            prev = None
            for j in range(NFP):
                wg = wupg[wi % NWB]
                wv = wupv[wi % NWB]
                wi += 1
                P.add("sp", DMA(wg.ap, b_w_up[l][j]), reads=WC[("up", l, j)], writes=[wg.b[0]], dma=True)
                P.add("sp", DMA(wv.ap, b_w_up[l][NFP + j]), reads=WC[("up", l, NFP + j)], writes=[wv.b[0]], dma=True)
                if j == 8:
                    P.add("sp", DMA(hrE.v3(n), fm(hres_d)[:, :, t0:t0 + n]), writes=[hrE.b[0]], dma=True)
                pg, pv = (j % 2), 2 + (j % 2)
                for k in range(KC):
                    P.add("pe", MM(ps[pg][:, 0:nn], wg.s(k), hi_t.s(k, 0, nn), k == 0, k == KC - 1), reads=[wg.b[0], hi_t.b[0]], writes=[psb[pg]])
                for k in range(KC):
                    P.add("pe", MM(ps[pv][:, 0:nn], wv.s(k), hi_t.s(k, 0, nn), k == 0, k == KC - 1), reads=[wv.b[0], hi_t.b[0]], writes=[psb[pv]])
                P.add("act", ACTF(ug[j % 2].s(0, 0, nn), ps[pg][:, 0:nn], AF.Identity), reads=[psb[pg]], writes=[ug[j % 2].b[0]])
                P.add("act", ACTF(uv[j % 2].s(0, 0, nn), ps[pv][:, 0:nn], AF.Identity), reads=[psb[pv]], writes=[uv[j % 2].b[0]])
                if j == 3 and deferred:
                    for f in deferred:
                        f()
                    deferred = []
                if prev is not None:
                    emit_conv(ci, *prev)
                prev = (j, t0, n, off, nn)
            emit_conv(ci, *prev)
            for m in range(KC):
                pa = 6 + (m % 2)
                for j in range(NFP):
                    P.add("pe", MM(ps[pa][:, 0:n], wdn.s(j, m * 128, (m + 1) * 128), gact.s(j, 0, n), j == 0, j == NFP - 1), reads=[wE, gact.b[j]], writes=[psb[pa]])
                P.add("dve", STT(hrE.s(m, 0, n), hrE.s(m, 0, n), DN_ALPHA, ps[pa][:, 0:n], ALU.mult, ALU.add), reads=[psb[pa]], writes=[hrE.b[0]])
            pre, pe_, post = ln_parts(hrE, n, f"l2g{l}", f"l2b{l}", ybfE, ysqE, stE, 6, 7, None if last else hbE)
            pre()

            def outs(t0=t0, n=n):
                if not last:
                    P.add("sp", DMA(fm(hres_d)[:, :, t0:t0 + n], hrE.v3(n)), reads=[hrE.b[0]], dma=True)
                    P.add("sp", DMA(fm(hbfA_d)[:, :, t0:t0 + n], hbE.v3(n)), reads=[hbE.b[0]], dma=True)
                    if dbg:
                        P.add("sp", DMA(fm(dbg_outs["d_hE1"])[:, :, t0:t0 + n], hrE.v3(n)), reads=[hrE.b[0]], dma=True)
                else:
                    a = max(NMETA - t0, 0)
                    if a < n:
                        P.add("sp", DMA(fm(outT)[:, :, t0 + a - NMETA:t0 + n - NMETA], hrE.v3()[:, :, a:n]), reads=[hrE.b[0]], dma=True)

            deferred = [pe_, post, outs]
        for f in deferred:
            f()
        deferred = []
        P.barrier()
        AR.reset(PERSIST)

    with nc.Block() as block:
        P.finalize_and_emit(block)
    return nc


def _colpack(v):
    v = np.asarray(v, np.float32).reshape(-1)
    return np.ascontiguousarray(v.reshape(-1, 128).T)


def pack_cols(inp):
    cols = np.zeros((128, NCOL), np.float32)

    def put(name, arr):
        o, n = COL_LAY[name]
        a = _colpack(arr)
        assert a.shape[1] == n, (name, a.shape, n)
        cols[:, o:o + n] = a

    put("ln0_g", inp["ln0_g"])
    put("ln0_b", inp["ln0_b"])
    for l in range(L):
        put(f"qn{l}", inp["q_norm"][l])
        put(f"kvn{l}", inp["kv_norm"][l])
        put(f"lcw{l}", inp["lru_conv_w"][l])
        put(f"lcb{l}", inp["lru_conv_b"][l])
        put(f"brg{l}", inp["b_rg"][l])
        put(f"big{l}", inp["b_ig"][l])
        put(f"lam{l}", inp["lru_lambda"][l])
        put(f"l1g{l}", inp["ln1_g"][l])
        put(f"l1b{l}", inp["ln1_b"][l])
        put(f"fcw{l}", inp["ffn_conv_w"][l])
        put(f"fcb{l}", inp["ffn_conv_b"][l])
        put(f"l2g{l}", inp["ln2_g"][l])
        put(f"l2b{l}", inp["ln2_b"][l])
    return cols


def rope_tables(T):
    half = 32
    inv_freq = np.exp(-math.log(10000.0) * np.arange(half, dtype=np.float32) / half).astype(np.float32)
    ang = np.arange(T, dtype=np.float32)[:, None] * inv_freq[None, :]
    cos = np.cos(ang).astype(np.float32).T
    sin = np.sin(ang).astype(np.float32).T
    return np.ascontiguousarray(np.stack([np.concatenate([cos, cos], 0), np.concatenate([sin, sin], 0)], 0))


def make_in_maps(inp, nb, SEQ):
    f = lambda a: np.ascontiguousarray(np.asarray(a, np.float32))
    T = SEQ + NMETA
    shared = {
        "metaT": f(np.asarray(inp["meta_tokens"]).T),
        "cols": pack_cols({k: np.asarray(v) for k, v in inp.items()}),
        "cs2": rope_tables(T),
        "ident": np.eye(128, dtype=np.float32),
        "w_in": f(inp["w_in"]),
        "w_uq": f(np.asarray(inp["w_uq"]).reshape(L, 256, 1536)),
        "w_uk": f(np.asarray(inp["w_uk"]).reshape(L, 128, 1024)),
        "w_uv": f(np.asarray(inp["w_uv"]).reshape(L, 128, 1024)),
        "w_o_mla": f(inp["w_o_mla"]),
        "w_rg": f(np.asarray(inp["w_rg"]).reshape(L, 2048, 128)),
        "w_ig": f(np.asarray(inp["w_ig"]).reshape(L, 2048, 128)),
        "w_o_lru": f(inp["w_o_lru"]),
        "w_out": f(inp["w_out"]),
        "w_up": f(inp["w_up"]),
        "w_down": f(inp["w_down"]),
    }
    x = np.asarray(inp["x"], np.float32)
    maps = []
    for b in range(nb):
        m = dict(shared)
        m["xT"] = np.ascontiguousarray(x[b].T)
        maps.append(m)
    return maps


_NC_CACHE = {}


def kernel(**inputs):
    x = np.asarray(inputs["x"])
    B, SEQ, _ = x.shape
    if SEQ not in _NC_CACHE:
        _NC_CACHE[SEQ] = build(SEQ)
    nc = _NC_CACHE[SEQ]
    maps = make_in_maps(inputs, B, SEQ)
    res = run_bass_kernel_spmd(nc, maps, core_ids=list(range(B)))
    out = np.stack([np.asarray(r["outT"]).T for r in res.results], 0)
    return np.ascontiguousarray(out.astype(np.float32))
```

```python
import math
import numpy as np
import concourse.bass as bass
import concourse.mybir as mybir
from concourse.bass_utils import run_bass_kernel_spmd

F32 = mybir.dt.float32
BF16 = mybir.dt.bfloat16
AF = mybir.ActivationFunctionType
ALU = mybir.AluOpType

D = 1024
KC = 8
NMETA = 16
H = 8
L = 2
DFF = 2816
NFC = 44
NFP = 22
INCOLS = 4544
C_CQ, C_CKV, C_KR, C_LG, C_LX, C_GM, C_GL = 0, 256, 384, 448, 1472, 2496, 3520
DN_ALPHA = (2.0 * L) ** 0.25
LN_EPS = 1e-5
RMS_EPS = 1e-6
ATT_SCALE = 1.0 / math.sqrt(192.0)
GELU_C = math.sqrt(2.0 / math.pi)
GELU_SQA = math.sqrt(0.044715)

SAME_ENGINE_SYNC = True
EPOCH = 16000
N_DMA_SEM = {'sp': 20, 'pool': 8, 'act': 4, 'pe': 1, 'dve': 1}
DMA_SEM_MAX_USES = 900


class Buf:
    __slots__ = ("name", "w", "r")

    def __init__(self, name=""):
        self.name = name
        self.w = None
        self.r = {}


class Op:
    __slots__ = ("eng", "fn", "deps", "dma", "needed", "ev", "seq", "prev_on_sem")

    def __init__(self, eng, fn, dma):
        self.eng = eng
        self.fn = fn
        self.dma = dma
        self.deps = []
        self.needed = bool(dma)
        self.ev = None
        self.seq = -1
        self.prev_on_sem = None


class Prog:
    ENGS = ("pe", "act", "dve", "pool", "sp")

    def __init__(self, nc):
        self.nc = nc
        self.ops = {e: [] for e in self.ENGS}
        self.known = {e: {f: -1 for f in self.ENGS} for e in self.ENGS}
        self.known_dma = {e: set() for e in self.ENGS}
        self.pending_dma = []
        self.last_real = {e: None for e in self.ENGS}

    def add(self, eng, fn, reads=(), writes=(), dma=False, extra_deps=(), force_same=False, track=True):
        op = Op(eng, fn, dma)
        op.seq = len(self.ops[eng])
        deps = {}
        for b in reads:
            if b.w is not None:
                deps[id(b.w)] = b.w
        for b in writes:
            if b.w is not None:
                deps[id(b.w)] = b.w
            for r in b.r.values():
                deps[id(r)] = r
        for d in extra_deps:
            if d is not None:
                deps[id(d)] = d
        kn = self.known[eng]
        for d in deps.values():
            if d is op:
                continue
            if d.dma:
                if id(d) in self.known_dma[eng]:
                    continue
                self.known_dma[eng].add(id(d))
                op.deps.append(d)
            else:
                if d.eng == eng and not dma and not force_same:
                    if d.eng == "pe" or not SAME_ENGINE_SYNC:
                        continue
                if d.seq <= kn[d.eng]:
                    continue
                kn[d.eng] = d.seq
                d.needed = True
                op.deps.append(d)
        for b in reads:
            key = ("dma", id(op)) if dma else eng
            b.r[key] = op
        for b in writes:
            b.w = op
            b.r = {}
        self.ops[eng].append(op)
        if dma:
            if track:
                self.pending_dma.append(op)
        else:
            self.last_real[eng] = op
        return op

    def barrier(self):
        pend = list(self.pending_dma)
        self.pending_dma = []
        lasts = [self.last_real[e] for e in self.ENGS]
        spop = self.add("sp", lambda e: e.nop(), extra_deps=pend + lasts, force_same=True)
        for e in ("pe", "act", "dve", "pool"):
            self.add(e, lambda g: g.nop(), extra_deps=[spop])

    def finalize_and_emit(self, block):
        nc = self.nc
        eng_sems = {e: [] for e in self.ENGS}
        for e in self.ENGS:
            c = 0
            for op in self.ops[e]:
                if not op.needed or op.dma:
                    continue
                c += 1
                ep = (c - 1) // EPOCH
                while len(eng_sems[e]) <= ep:
                    eng_sems[e].append(nc.alloc_semaphore(f"s_{e}_{len(eng_sems[e])}"))
                op.ev = (eng_sems[e][ep], (c - 1) % EPOCH + 1, 1)
        for e in self.ENGS:
            dma_sems = []
            dma_state = []
            dma_rr = 0
            for op in self.ops[e]:
                if not op.dma:
                    continue
                if len(dma_sems) < N_DMA_SEM[e]:
                    dma_sems.append(nc.alloc_semaphore(f"s_dma_{e}_{len(dma_sems)}"))
                    dma_state.append([0, None])
                k = dma_rr % len(dma_sems)
                if dma_state[k][0] >= DMA_SEM_MAX_USES:
                    dma_sems[k] = nc.alloc_semaphore(f"s_dma_{e}_{k}_{dma_rr}")
                    dma_state[k] = [0, None]
                dma_rr += 1
                dma_state[k][0] += 1
                op.prev_on_sem = dma_state[k][1]
                dma_state[k][1] = op
                op.ev = (dma_sems[k], 16 * dma_state[k][0], 16)

        def emit(e, engobj):
            waited = {}

            def w(ev):
                sem, val, _ = ev
                key = id(sem)
                if waited.get(key, 0) >= val:
                    return
                waited[key] = val
                engobj.wait_ge(sem, val)

            for op in self.ops[e]:
                for d in op.deps:
                    w(d.ev)
                if op.prev_on_sem is not None:
                    w(op.prev_on_sem.ev)
                ins = op.fn(engobj)
                if op.ev is not None:
                    ins.then_inc(op.ev[0], op.ev[2])

        block.sync(lambda eng: emit("sp", eng))
        block.tensor(lambda eng: emit("pe", eng))
        block.scalar(lambda eng: emit("act", eng))
        block.vector(lambda eng: emit("dve", eng))
        block.gpsimd(lambda eng: emit("pool", eng))


def MM(out, lhsT, rhs, start, stop):
    return lambda e: e.matmul(out, lhsT, rhs, start=start, stop=stop)


def TR(out, in_, ident):
    return lambda e: e.transpose(out, in_, ident)


def ACTF(out, in_, func, scale=None, bias=None):
    kw = {}
    if scale is not None:
        kw["scale"] = scale
    if bias is not None:
        kw["bias"] = bias
    return lambda e: e.activation(out, in_, func, **kw)


def TT(out, a, b, op):
    return lambda e: e.tensor_tensor(out, a, b, op)


def TS(out, a, s1, s2, op0, op1):
    return lambda e: e.tensor_scalar(out, a, s1, s2, op0, op1)


def TS1(out, a, s1, op0):
    return lambda e: e.tensor_scalar(out, a, s1, None, op0)


def STT(out, a, s, b, op0, op1):
    return lambda e: e.scalar_tensor_tensor(out, a, s, b, op0, op1)


def CP(out, in_):
    return lambda e: e.tensor_copy(out, in_)


def RCP(out, in_):
    return lambda e: e.reciprocal(out, in_)


def MSET(ap, v):
    return lambda e: e.memset(ap, v)


def DMA(out, in_):
    return lambda e: e.dma_start(out=out, in_=in_)


def SCAN(out, a, u, init):
    return lambda e: e.tensor_tensor_scan(out, a, u, init, ALU.mult, ALU.add)


def chunks(total, size):
    out = []
    t = 0
    while t < total:
        n = min(size, total - t)
        out.append((t, n))
        t += n
    return out


def col_layout():
    lay = {}
    off = 0

    def put(name, n):
        nonlocal off
        lay[name] = (off, n)
        off += n

    put("ln0_g", 8)
    put("ln0_b", 8)
    for l in range(L):
        put(f"qn{l}", 2)
        put(f"kvn{l}", 1)
        put(f"lcw{l}", 32)
        put(f"lcb{l}", 8)
        put(f"brg{l}", 16)
        put(f"big{l}", 16)
        put(f"lam{l}", 16)
        put(f"l1g{l}", 8)
        put(f"l1b{l}", 8)
        put(f"fcw{l}", 3 * NFC)
        put(f"fcb{l}", NFC)
        put(f"l2g{l}", 8)
        put(f"l2b{l}", 8)
    return lay, off


COL_LAY, NCOL = col_layout()


class Arena:
    def __init__(self, nc, units):
        self.t = nc.alloc_sbuf_tensor("arena", [128, units], BF16)
        self.units = units
        self.off = 0

    def mark(self):
        return self.off

    def reset(self, m):
        self.off = m

    def alloc(self, cols, dt):
        u = cols * (2 if dt == F32 else 1)
        u = (u + 1) // 2 * 2
        a = self.off
        self.off += u
        assert self.off <= self.units, f"SBUF arena overflow {self.off} > {self.units}"
        ap = self.t[:, a:a + u]
        if dt == F32:
            ap = ap.bitcast(F32)
        return ap


class Tl:
    def __init__(self, arena, K, n, dt, name="", perk=False):
        self.ap = arena.alloc(K * n, dt)
        self.K, self.n = K, n
        self.b = [Buf(f"{name}{k}") for k in range(K)] if perk else [Buf(name)] * K

    def s(self, k=0, a=0, b=None, p0=0, p1=128):
        if b is None:
            b = self.n
        return self.ap[p0:p1, k * self.n + a:k * self.n + b]

    def flat(self, a=0, b=None, p0=0, p1=128):
        if b is None:
            b = self.K * self.n
        return self.ap[p0:p1, a:b]

    def v3(self, n=None):
        if n is None or n == self.n:
            return self.ap.rearrange("p (k n) -> p k n", k=self.K)
        return self.ap.rearrange("p (k n) -> p k n", k=self.K)[:, :, 0:n]


def build(SEQ, dbg=False):
    T = SEQ + NMETA
    TCH = chunks(T, 512)
    KT = chunks(T, 128)
    FCH = chunks(T, 510)
    nc = bass.Bass("TRN2", target_bir_lowering=False)
    P = Prog(nc)

    def din(name, shape, dt=F32):
        return nc.dram_tensor(name, shape, dt, kind="ExternalInput").ap()

    def dscr(name, shape, dt):
        return nc.dram_tensor(name, shape, dt).ap()

    xT = din("xT", [D, SEQ])
    metaT = din("metaT", [D, NMETA])
    cols_d = din("cols", [128, NCOL])
    cs2_d = din("cs2", [2, 64, T])
    ident_d = din("ident", [128, 128])
    w_in = din("w_in", [L, D, INCOLS])
    w_uq = din("w_uq", [L, 256, 1536])
    w_uk = din("w_uk", [L, 128, 1024])
    w_uv = din("w_uv", [L, 128, 1024])
    w_o_mla = din("w_o_mla", [L, D, D])
    w_rg = din("w_rg", [L, 2048, 128])
    w_ig = din("w_ig", [L, 2048, 128])
    w_o_lru = din("w_o_lru", [L, D, D])
    w_out = din("w_out", [L, D, D])
    w_up = din("w_up", [L, D, 2 * DFF])
    w_down = din("w_down", [L, DFF, D])
    outT = nc.dram_tensor("outT", [D, SEQ], F32, kind="ExternalOutput").ap()

    b_w_in = dscr("b_w_in", [L, D, INCOLS], BF16)
    b_w_uq = dscr("b_w_uq", [L, 256, 1536], BF16)
    b_w_uv = dscr("b_w_uv", [L, 128, 1024], BF16)
    b_w_o_mla = dscr("b_w_o_mla", [L, D, D], BF16)
    b_w_rg = dscr("b_w_rg", [L, 2048, 128], BF16)
    b_w_ig = dscr("b_w_ig", [L, 2048, 128], BF16)
    b_w_o_lru = dscr("b_w_o_lru", [L, D, D], BF16)
    b_w_out = dscr("b_w_out", [L, D, D], BF16)
    b_w_up = dscr("b_w_up", [L, NFC, 128, KC * 128], BF16)
    b_w_down = dscr("b_w_down", [L, DFF, D], BF16)
    hres_d = dscr("hres", [D, T], F32)
    hbfA_d = dscr("hbfA", [D, T], BF16)
    hbfB_d = dscr("hbfB", [D, T], BF16)
    zm_d = dscr("zm", [D, T], BF16)
    ylin_d = dscr("ylin", [D, T], BF16)

    def fm(ap):
        return ap.rearrange("(k p) t -> p k t", p=128)

    dbg_outs = {}
    if dbg:
        for nm in ["d_h0", "d_hA1", "d_hE1"]:
            dbg_outs[nm] = nc.dram_tensor(nm, [D, T], F32, kind="ExternalOutput").ap()
        for nm in ["d_zm", "d_ylin"]:
            dbg_outs[nm] = nc.dram_tensor(nm, [D, T], BF16, kind="ExternalOutput").ap()

    WC = {}

    def cast2d(key, dst, src, R, C):
        for r0, rn in chunks(R, 1024):
            for c0, cn in chunks(C, 2048):
                b = Buf(key)
                WC.setdefault(key, []).append(b)
                P.add("pool", DMA(dst[r0:r0 + rn, c0:c0 + cn], src[r0:r0 + rn, c0:c0 + cn]), writes=[b], dma=True, track=False)

    def cast_layer_small(l):
        cast2d(("in", l), b_w_in[l], w_in[l], D, INCOLS)
        cast2d(("uq", l), b_w_uq[l], w_uq[l], 256, 1536)
        cast2d(("uv", l), b_w_uv[l], w_uv[l], 128, 1024)
        cast2d(("om", l), b_w_o_mla[l], w_o_mla[l], D, D)
        cast2d(("rg", l), b_w_rg[l], w_rg[l], 2048, 128)
        cast2d(("ig", l), b_w_ig[l], w_ig[l], 2048, 128)
        cast2d(("ol", l), b_w_o_lru[l], w_o_lru[l], D, D)
        cast2d(("out", l), b_w_out[l], w_out[l], D, D)

    def cast_layer_ffn(l):
        cast2d(("dn", l), b_w_down[l], w_down[l], DFF, D)
        for m in range(NFC):
            for k0 in range(0, KC, 4):
                src = w_up[l][k0 * 128:(k0 + 4) * 128, m * 128:(m + 1) * 128].rearrange("(k p) c -> p k c", p=128)
                dst = b_w_up[l][m][:, k0 * 128:(k0 + 4) * 128].rearrange("p (k c) -> p k c", c=128)
                b = Buf("up")
                WC.setdefault(("up", l, m), []).append(b)
                P.add("pool", DMA(dst, src), writes=[b], dma=True, track=False)

    cast_layer_small(0)

    AR = Arena(nc, 106400)
    psW = [nc.alloc_psum_tensor(f"psw{i}", [128, 1024], F32) for i in range(4)]
    ps = [psW[i // 2][:, (i % 2) * 512:(i % 2 + 1) * 512] for i in range(8)]
    psb = [Buf(f"ps{i}") for i in range(8)]

    colsT = Tl(AR, 1, NCOL, F32, "cols")
    dcols = Tl(AR, 1, L * 64, F32, "dcols")
    identF = Tl(AR, 1, 128, F32, "ident")
    onesD = Tl(AR, 1, 128, BF16, "onesD")
    ones256 = Tl(AR, 1, 128, BF16, "ones256")
    ones128 = Tl(AR, 1, 128, BF16, "ones128")
    ones1 = Tl(AR, 1, 128, BF16, "ones1")
    consts = Buf("consts")
    onecol = Tl(AR, 1, 2, F32, "onecol")

    def col(name, i=0, p0=0, p1=128):
        o, n = COL_LAY[name]
        return colsT.ap[p0:p1, o + i:o + i + 1]

    def dcol(l, i, p0=0, p1=128):
        return dcols.ap[p0:p1, l * 64 + i:l * 64 + i + 1]

    P.add("sp", DMA(colsT.ap, cols_d), writes=[consts], dma=True)
    P.add("sp", DMA(identF.ap, ident_d), writes=[consts], dma=True)
    P.add("pool", MSET(onesD.ap, 1.0 / 1024), writes=[consts])
    P.add("pool", MSET(ones256.ap, 1.0 / 256), writes=[consts])
    P.add("pool", MSET(ones128.ap, 1.0 / 128), writes=[consts])
    P.add("pool", MSET(ones1.ap, 1.0), writes=[consts])
    P.add("pool", MSET(onecol.ap, 1.0), writes=[consts])
    for l in range(L):
        o, _ = COL_LAY[f"lam{l}"]
        lam = colsT.ap[:, o:o + 16]
        d0 = dcols.ap[:, l * 64:l * 64 + 16]
        d1 = dcols.ap[:, l * 64 + 16:l * 64 + 32]
        P.add("act", ACTF(d0, lam, AF.Exp, scale=-1.0), reads=[consts], writes=[consts])
        P.add("dve", TS1(d0, d0, 1.0, ALU.add), reads=[consts], writes=[consts])
        P.add("act", ACTF(d0, d0, AF.Ln), reads=[consts], writes=[consts])
        P.add("dve", TS1(d1, d0, -4.0, ALU.mult), reads=[consts], writes=[consts])
        P.add("dve", TS1(d0, d0, -8.0, ALU.mult), reads=[consts], writes=[consts])
        o, _ = COL_LAY[f"brg{l}"]
        P.add("dve", TS1(dcols.ap[:, l * 64 + 32:l * 64 + 48], colsT.ap[:, o:o + 16], 0.5, ALU.mult), reads=[consts], writes=[consts])
        o, _ = COL_LAY[f"big{l}"]
        P.add("dve", TS1(dcols.ap[:, l * 64 + 48:l * 64 + 64], colsT.ap[:, o:o + 16], 0.5, ALU.mult), reads=[consts], writes=[consts])

    PERSIST = AR.mark()

    def ln_parts(y, n, gname, bname, ybf, ysq, st, psA, psB, hbf):
        yb = y.b[0]
        if isinstance(psA, int):
            apA, bA, apB, bB = ps[psA][:, 0:n], psb[psA], ps[psB][:, 0:n], psb[psB]
        else:
            (apA, bA), (apB, bB) = psA, psB

        def pre():
            P.add("act", ACTF(ybf.v3(n), y.v3(n), AF.Identity), reads=[yb], writes=[ybf.b[0]])
            P.add("act", ACTF(ysq.v3(n), y.v3(n), AF.Square), reads=[yb], writes=[ysq.b[0]])

        def pe():
            for k in range(KC):
                P.add("pe", MM(apA, onesD.ap, ybf.s(k, 0, n), k == 0, k == KC - 1), reads=[ybf.b[0], consts], writes=[bA])
            for k in range(KC):
                P.add("pe", MM(apB, onesD.ap, ysq.s(k, 0, n), k == 0, k == KC - 1), reads=[ysq.b[0], consts], writes=[bB])

        def post():
            mean, msq, var, rstd = (st.s(i, 0, n) for i in range(4))
            sb = st.b[0]
            P.add("dve", CP(mean, apA), reads=[bA], writes=[sb])
            P.add("dve", TT(msq, mean, mean, ALU.mult), reads=[sb], writes=[sb])
            P.add("dve", STT(var, apB, LN_EPS, msq, ALU.add, ALU.subtract), reads=[bB, sb], writes=[sb])
            P.add("act", ACTF(var, var, AF.Sqrt), reads=[sb], writes=[sb])
            P.add("dve", RCP(rstd, var), reads=[sb], writes=[sb])
            mean3 = mean.unsqueeze(1).broadcast_to([128, KC, n])
            rstd3 = rstd.unsqueeze(1).broadcast_to([128, KC, n])
            P.add("dve", TT(y.v3(n), y.v3(n), mean3, ALU.subtract), reads=[sb], writes=[yb])
            P.add("dve", TT(y.v3(n), y.v3(n), rstd3, ALU.mult), reads=[sb], writes=[yb])
            for k in range(KC):
                if hbf is not None:
                    P.add("act", ACTF(hbf.s(k, 0, n), y.s(k, 0, n), AF.Identity, scale=col(gname, k), bias=col(bname, k)), reads=[consts, yb], writes=[hbf.b[0]])
                P.add("act", ACTF(y.s(k, 0, n), y.s(k, 0, n), AF.Identity, scale=col(gname, k), bias=col(bname, k)), reads=[consts], writes=[yb])

        return pre, pe, post

    def ln_chunk(*a):
        pre, pe, post = ln_parts(*a)
        pre()
        pe()
        post()

    ys = [Tl(AR, KC, 512, F32, f"y{i}") for i in range(2)]
    hb = [Tl(AR, KC, 512, BF16, f"hb{i}") for i in range(2)]
    ybf0 = [Tl(AR, KC, 512, BF16, f"ybf{i}") for i in range(2)]
    ysq0 = [Tl(AR, KC, 512, BF16, f"ysq{i}") for i in range(2)]
    st0 = [Tl(AR, 4, 512, F32, f"st{i}") for i in range(2)]
    s0_parts = {}

    def s0_load_pre(ci):
        t0, n = TCH[ci]
        y = ys[ci % 2]
        if t0 == 0:
            P.add("sp", DMA(y.v3()[:, :, 0:NMETA], fm(metaT)), writes=[y.b[0]], dma=True)
            P.add("sp", DMA(y.v3()[:, :, NMETA:n], fm(xT)[:, :, 0:n - NMETA]), writes=[y.b[0]], dma=True)
        else:
            P.add("sp", DMA(y.v3(n), fm(xT)[:, :, t0 - NMETA:t0 - NMETA + n]), writes=[y.b[0]], dma=True)
        pre, pe_, post = ln_parts(y, n, "ln0_g", "ln0_b", ybf0[ci % 2], ysq0[ci % 2], st0[ci % 2], 2 * (ci % 2), 2 * (ci % 2) + 1, hb[ci % 2])
        pre()
        s0_parts[ci] = (pe_, post)

    def s0_finish(ci):
        t0, n = TCH[ci]
        y = ys[ci % 2]
        hbt = hb[ci % 2]
        pe_, post = s0_parts.pop(ci)
        pe_()
        post()
        P.add("sp", DMA(fm(hres_d)[:, :, t0:t0 + n], y.v3(n)), reads=[y.b[0]], dma=True)
        P.add("sp", DMA(fm(hbfA_d)[:, :, t0:t0 + n], hbt.v3(n)), reads=[hbt.b[0]], dma=True)
        if dbg:
            P.add("sp", DMA(fm(dbg_outs["d_h0"])[:, :, t0:t0 + n], y.v3(n)), reads=[y.b[0]], dma=True)

    s0_load_pre(0)
    for ci in range(1, len(TCH)):
        s0_load_pre(ci)
        s0_finish(ci - 1)
    s0_finish(len(TCH) - 1)
    cast_layer_ffn(0)
    P.barrier()
    AR.reset(PERSIST)

    for l in range(L):
        last = l == L - 1
        hT = Tl(AR, KC, T, BF16, "hT", perk=False)
        hTb = [Buf(f"hT{c}") for c in range(len(TCH))]
        for ci, (t0, n) in enumerate(TCH):
            P.add("sp", DMA(hT.v3()[:, :, t0:t0 + n], fm(hbfA_d)[:, :, t0:t0 + n]), writes=[hTb[ci]], dma=True)
        LAYER = AR.mark()

        cqn = Tl(AR, 2, T, BF16, "cqn")
        ckvn = Tl(AR, 1, T, BF16, "ckvn")
        krope = Tl(AR, 1, T, BF16, "krope")
        Vt = Tl(AR, len(KT), 128, BF16, "V")
        latb = [Buf(f"lat{c}") for c in range(len(TCH))]
        Vb = Buf("V")
        kpad = Buf("kpad")
        P.add("pool", MSET(krope.s(0, 0, T, 64, 128), 0.0), writes=[kpad])
        STA = AR.mark()
        wA = Tl(AR, KC, 448, BF16, "wA")
        P.add("sp", DMA(wA.v3(), b_w_in[l][:, 0:448].rearrange("(k p) n -> p k n", p=128)), reads=WC[("in", l)], writes=[wA.b[0]], dma=True)
        cqf = Tl(AR, 2, 512, F32, "cqf")
        cqs = Tl(AR, 2, 512, BF16, "cqs")
        ckf = Tl(AR, 1, 512, F32, "ckf")
        cks = Tl(AR, 1, 512, BF16, "cks")
        ckn = Tl(AR, 1, 512, F32, "ckn")
        krf = Tl(AR, 1, 512, F32, "krf")
        rot = Tl(AR, 1, 512, F32, "rot")
        cst = Tl(AR, 2, 512, F32, "cst")
        rq = Tl(AR, 1, 512, F32, "rq")
        rk = Tl(AR, 1, 512, F32, "rk")
        for ci, (t0, n) in enumerate(TCH):
            P.add("sp", DMA(cst.v3(n)[0:64], cs2_d[:, :, t0:t0 + n].rearrange("a p t -> p a t")), writes=[cst.b[0]], dma=True)
            for m in range(2):
                for k in range(KC):
                    P.add("pe", MM(ps[m][:, 0:n], wA.s(k, m * 128, (m + 1) * 128), hT.s(k, t0, t0 + n), k == 0, k == KC - 1),
                          reads=[wA.b[0], hTb[ci]], writes=[psb[m]])
            for k in range(KC):
                P.add("pe", MM(ps[2][:, 0:n], wA.s(k, 256, 384), hT.s(k, t0, t0 + n), k == 0, k == KC - 1), reads=[wA.b[0], hTb[ci]], writes=[psb[2]])
            for k in range(KC):
                P.add("pe", MM(ps[3][0:64, 0:n], wA.s(k, 384, 448), hT.s(k, t0, t0 + n), k == 0, k == KC - 1), reads=[wA.b[0], hTb[ci]], writes=[psb[3]])
            for m in range(2):
                P.add("act", ACTF(cqf.s(m, 0, n), ps[m][:, 0:n], AF.Identity), reads=[psb[m]], writes=[cqf.b[0]])
                P.add("act", ACTF(cqs.s(m, 0, n), ps[m][:, 0:n], AF.Square), reads=[psb[m]], writes=[cqs.b[0]])
            P.add("act", ACTF(ckf.s(0, 0, n), ps[2][:, 0:n], AF.Identity), reads=[psb[2]], writes=[ckf.b[0]])
            P.add("act", ACTF(cks.s(0, 0, n), ps[2][:, 0:n], AF.Square), reads=[psb[2]], writes=[cks.b[0]])
            P.add("dve", CP(krf.s(0, 0, n, 0, 64), ps[3][0:64, 0:n]), reads=[psb[3]], writes=[krf.b[0]])
            for m in range(2):
                P.add("pe", MM(ps[4][:, 0:n], ones256.ap, cqs.s(m, 0, n), m == 0, m == 1), reads=[cqs.b[0], consts], writes=[psb[4]])
            P.add("pe", MM(ps[5][:, 0:n], ones128.ap, cks.s(0, 0, n), True, True), reads=[cks.b[0], consts], writes=[psb[5]])
            P.add("dve", TS1(rq.s(0, 0, n), ps[4][:, 0:n], RMS_EPS, ALU.add), reads=[psb[4]], writes=[rq.b[0]])
            P.add("dve", TS1(rk.s(0, 0, n), ps[5][:, 0:n], RMS_EPS, ALU.add), reads=[psb[5]], writes=[rk.b[0]])
            P.add("act", ACTF(rq.s(0, 0, n), rq.s(0, 0, n), AF.Sqrt), reads=[], writes=[rq.b[0]])
            P.add("act", ACTF(rk.s(0, 0, n), rk.s(0, 0, n), AF.Sqrt), reads=[], writes=[rk.b[0]])
            P.add("dve", RCP(rq.s(0, 0, n), rq.s(0, 0, n)), reads=[], writes=[rq.b[0]])
            P.add("dve", RCP(rk.s(0, 0, n), rk.s(0, 0, n)), reads=[], writes=[rk.b[0]])
            for m in range(2):
                P.add("dve", STT(cqn.s(m, t0, t0 + n), cqf.s(m, 0, n), col(f"qn{l}", m), rq.s(0, 0, n), ALU.mult, ALU.mult),
                      reads=[cqf.b[0], rq.b[0], consts], writes=[latb[ci]])
            P.add("dve", STT(ckn.s(0, 0, n), ckf.s(0, 0, n), col(f"kvn{l}", 0), rk.s(0, 0, n), ALU.mult, ALU.mult),
                  reads=[ckf.b[0], rk.b[0], consts], writes=[ckn.b[0]])
            P.add("act", ACTF(ckvn.s(0, t0, t0 + n), ckn.s(0, 0, n), AF.Identity), reads=[ckn.b[0]], writes=[latb[ci]])
            P.add("act", ACTF(rot.s(0, 0, n, 0, 32), krf.s(0, 0, n, 32, 64), AF.Identity, scale=-1.0), reads=[krf.b[0]], writes=[rot.b[0]])
            P.add("act", ACTF(rot.s(0, 0, n, 32, 64), krf.s(0, 0, n, 0, 32), AF.Identity), reads=[krf.b[0]], writes=[rot.b[0]])
            P.add("dve", TT(krf.s(0, 0, n, 0, 64), krf.s(0, 0, n, 0, 64), cst.s(0, 0, n, 0, 64), ALU.mult), reads=[cst.b[0], rot.b[0]], writes=[krf.b[0]])
            P.add("dve", TT(rot.s(0, 0, n, 0, 64), rot.s(0, 0, n, 0, 64), cst.s(1, 0, n, 0, 64), ALU.mult), reads=[cst.b[0]], writes=[rot.b[0]])
            P.add("dve", TT(krope.s(0, t0, t0 + n, 0, 64), krf.s(0, 0, n, 0, 64), rot.s(0, 0, n, 0, 64), ALU.add), reads=[krf.b[0], rot.b[0]], writes=[latb[ci]])
            for j, (kt0, kn) in enumerate(KT):
                if kt0 < t0 or kt0 >= t0 + n:
                    continue
                a = kt0 - t0
                P.add("pe", TR(ps[6][0:kn, 0:128], ckn.s(0, a, a + kn), identF.ap), reads=[ckn.b[0], consts], writes=[psb[6]])
                P.add("act", ACTF(Vt.s(j, 0, 128, 0, kn), ps[6][0:kn, 0:128], AF.Identity), reads=[psb[6]], writes=[Vb])
        if l == 0:
            cast_layer_small(1)
            cast_layer_ffn(1)
        P.barrier()
        AR.reset(STA)

        wuq = Tl(AR, 2, 1536, BF16, "wuq")
        wukT = Tl(AR, H, 128, BF16, "wukT")
        wuv = Tl(AR, 1, 1024, BF16, "wuv")
        wom = Tl(AR, KC, 1024, BF16, "wom")
        wB = Buf("wB")
        P.add("sp", DMA(wuq.v3(), b_w_uq[l].rearrange("(k p) n -> p k n", p=128)), reads=WC[("uq", l)], writes=[wB], dma=True)
        P.add("sp", DMA(wuv.ap, b_w_uv[l]), reads=WC[("uv", l)], writes=[wB], dma=True)
        P.add("sp", DMA(wom.v3(), b_w_o_mla[l].rearrange("(k p) n -> p k n", p=128)), reads=WC[("om", l)], writes=[wB], dma=True)
        qn_t = [Tl(AR, 1, 512, BF16, f"qn{i}") for i in range(2)]
        qabs = Tl(AR, H, 512, BF16, "qabs", perk=True)
        qrf = [Tl(AR, 1, 512, F32, f"qrf{i}") for i in range(2)]
        qrot = [Tl(AR, 1, 512, F32, f"qrot{i}") for i in range(2)]
        qrope = Tl(AR, H, 512, BF16, "qrope", perk=True)
        cstq = Tl(AR, 2, 512, F32, "cstq")
        qpad = Buf("qpad")
        P.add("pool", MSET(qrope.flat(0, H * 512, 64, 128), 0.0), writes=[qpad])
        NPT = 3
        PT = [Tl(AR, 2, 512, BF16, f"PT{i}") for i in range(NPT)]
        rl = [Tl(AR, 1, 512, F32, f"rl{i}") for i in range(2)]
        olat = [Tl(AR, 1, 512, BF16, f"olat{i}") for i in range(2)]
        oT = qabs
        zm1 = Tl(AR, KC, 512, BF16, "zmT")
        wukF_ap = zm1.ap.bitcast(F32)[:, 0:1024]
        P.add("sp", DMA(wukF_ap, w_uk[l]), writes=[zm1.b[0]], dma=True)
        wukTb = Buf("wukT")
        for h in range(H):
            P.add("pe", TR(ps[5][:, 0:128], wukF_ap[:, h * 128:(h + 1) * 128], identF.ap), reads=[zm1.b[0], consts], writes=[psb[5]])
            P.add("act", ACTF(wukT.s(h), ps[5][:, 0:128], AF.Identity), reads=[psb[5]], writes=[wukTb])
        KG = []
        full = [(j, kt0, kn) for j, (kt0, kn) in enumerate(KT) if kn == 128]
        for i in range(0, len(full), 2):
            KG.append(full[i:i + 2])
        for j, (kt0, kn) in enumerate(KT):
            if kn != 128:
                KG.append([(j, kt0, kn)])
        NG = len(KG)
        sgb = [Buf("sg0"), Buf("sg1")]
        B_O, B_L, B_MA, B_MB = 4, 5, 6, 7
        gi = 0
        for ci, (t0, n) in enumerate(TCH):
            P.add("sp", DMA(cstq.v3(n)[0:64], cs2_d[:, :, t0:t0 + n].rearrange("a p t -> p a t")), writes=[cstq.b[0]], dma=True)
            for h in range(H):
                pa, pb_ = B_MA, B_MB
                qn = qn_t[h % 2]
                qf = qrf[h % 2]
                qo = qrot[h % 2]
                for k in range(2):
                    P.add("pe", MM(ps[pa][:, 0:n], wuq.s(k, h * 192, h * 192 + 128), cqn.s(k, t0, t0 + n), k == 0, k == 1),
                          reads=[wB, latb[ci]], writes=[psb[pa]])
                P.add("act", ACTF(qn.s(0, 0, n), ps[pa][:, 0:n], AF.Identity), reads=[psb[pa]], writes=[qn.b[0]])
                for k in range(2):
                    P.add("pe", MM(ps[pb_][0:64, 0:n], wuq.s(k, h * 192 + 128, h * 192 + 192), cqn.s(k, t0, t0 + n), k == 0, k == 1),
                          reads=[wB, latb[ci]], writes=[psb[pb_]])
                P.add("dve", CP(qf.s(0, 0, n, 0, 64), ps[pb_][0:64, 0:n]), reads=[psb[pb_]], writes=[qf.b[0]])
                P.add("pe", MM(ps[pa][:, 0:n], wukT.s(h), qn.s(0, 0, n), True, True), reads=[wukTb, qn.b[0]], writes=[psb[pa]])
                P.add("act", ACTF(qabs.s(h, 0, n), ps[pa][:, 0:n], AF.Identity), reads=[psb[pa]], writes=[qabs.b[h]])
                P.add("act", ACTF(qo.s(0, 0, n, 0, 32), qf.s(0, 0, n, 32, 64), AF.Identity, scale=-1.0), reads=[qf.b[0]], writes=[qo.b[0]])
                P.add("act", ACTF(qo.s(0, 0, n, 32, 64), qf.s(0, 0, n, 0, 32), AF.Identity), reads=[qf.b[0]], writes=[qo.b[0]])
                P.add("dve", TT(qf.s(0, 0, n, 0, 64), qf.s(0, 0, n, 0, 64), cstq.s(0, 0, n, 0, 64), ALU.mult), reads=[cstq.b[0], qo.b[0]], writes=[qf.b[0]])
                P.add("dve", TT(qo.s(0, 0, n, 0, 64), qo.s(0, 0, n, 0, 64), cstq.s(1, 0, n, 0, 64), ALU.mult), reads=[cstq.b[0]], writes=[qo.b[0]])
                P.add("dve", TT(qrope.s(h, 0, n, 0, 64), qf.s(0, 0, n, 0, 64), qo.s(0, 0, n, 0, 64), ALU.add), reads=[qf.b[0], qo.b[0]], writes=[qrope.b[h]])
            pending_vup = None

            def emit_S(h, g, gidx):
                sg = psW[gidx % 2]
                for ti, (j, kt0, kn) in enumerate(KG[g]):
                    kci = kt0 // 512
                    o_ = sg[0:kn, ti * 512:ti * 512 + n]
                    P.add("pe", MM(o_, ckvn.s(0, kt0, kt0 + kn), qabs.s(h, 0, n), True, False), reads=[latb[kci], qabs.b[h]], writes=[sgb[gidx % 2]])
                    P.add("pe", MM(o_, krope.s(0, kt0, kt0 + kn), qrope.s(h, 0, n), False, True), reads=[latb[kci], qrope.b[h], kpad, qpad], writes=[sgb[gidx % 2]])

            def emit_exp(g, gidx):
                sg = psW[gidx % 2]
                pt = PT[gidx % NPT]
                nt = len(KG[g])
                kn = KG[g][0][2]
                src = sg.rearrange("p (t c) -> p t c", t=2)[0:kn, 0:nt, 0:n]
                dst = pt.v3()[0:kn, 0:nt, 0:n]
                P.add("act", ACTF(dst, src, AF.Exp, scale=ATT_SCALE), reads=[sgb[gidx % 2]], writes=[pt.b[0]])

            def emit_PV(g, gidx):
                pt = PT[gidx % NPT]
                for ti, (j, kt0, kn) in enumerate(KG[g]):
                    first = (g == 0 and ti == 0)
                    lastt = (g == NG - 1 and ti == len(KG[g]) - 1)
                    P.add("pe", MM(ps[B_O][:, 0:n], Vt.s(j, 0, 128, 0, kn), pt.s(ti, 0, n, 0, kn), first, lastt), reads=[Vb, pt.b[0]], writes=[psb[B_O]])
                    P.add("pe", MM(ps[B_L][:, 0:n], ones1.s(0, 0, 128, 0, kn), pt.s(ti, 0, n, 0, kn), first, lastt), reads=[consts, pt.b[0]], writes=[psb[B_L]])

            def emit_vup(h):
                ol = olat[h % 2]
                P.add("pe", MM(ps[B_MA][:, 0:n], wuv.s(0, h * 128, (h + 1) * 128), ol.s(0, 0, n), True, True), reads=[wB, ol.b[0]], writes=[psb[B_MA]])
                P.add("act", ACTF(oT.s(h, 0, n), ps[B_MA][:, 0:n], AF.Identity), reads=[psb[B_MA]], writes=[oT.b[h]])

            for h in range(H):
                gbase = gi
                emit_S(h, 0, gbase)
                emit_exp(0, gbase)
                for g in range(1, NG):
                    emit_S(h, g, gbase + g)
                    emit_exp(g, gbase + g)
                    if g == 2 and pending_vup is not None:
                        emit_vup(pending_vup)
                        pending_vup = None
                    emit_PV(g - 1, gbase + g - 1)
                emit_PV(NG - 1, gbase + NG - 1)
                gi += NG
                r_ = rl[h % 2]
                ol = olat[h % 2]
                P.add("dve", RCP(r_.s(0, 0, n), ps[B_L][:, 0:n]), reads=[psb[B_L]], writes=[r_.b[0]])
                P.add("dve", TT(ol.s(0, 0, n), ps[B_O][:, 0:n], r_.s(0, 0, n), ALU.mult), reads=[psb[B_O], r_.b[0]], writes=[ol.b[0]])
                if pending_vup is not None:
                    emit_vup(pending_vup)
                pending_vup = h
            emit_vup(pending_vup)
            zo = zm1
            for m in range(KC):
                pa = B_MA + (m % 2)
                for hh in range(H):
                    P.add("pe", MM(ps[pa][:, 0:n], wom.s(hh, m * 128, (m + 1) * 128), oT.s(hh, 0, n), hh == 0, hh == H - 1),
                          reads=[wB] + [oT.b[hh]], writes=[psb[pa]])
                P.add("act", ACTF(zo.s(m, 0, n), ps[pa][:, 0:n], AF.Identity, scale=0.5), reads=[psb[pa]], writes=[zo.b[0]])
            P.add("sp", DMA(fm(zm_d)[:, :, t0:t0 + n], zo.v3(n)), reads=[zo.b[0]], dma=True)
            if dbg and l == 0:
                P.add("sp", DMA(fm(dbg_outs["d_zm"])[:, :, t0:t0 + n], zo.v3(n)), reads=[zo.b[0]], dma=True)
        P.barrier()
        AR.reset(LAYER)

        TP = T + 4
        PCS = chunks(T, 1024)
        NPC = len(PCS)
        xpad = Tl(AR, 1, TP, F32, "xpad")
        xc = Tl(AR, 1, T, F32, "xc")
        xcb = Tl(AR, 1, T, BF16, "xcb")
        t_a = Tl(AR, 1, T, F32, "t_a")
        t_i = Tl(AR, 1, T, F32, "t_i")
        t_g = Tl(AR, 1, T, F32, "t_g")
        hf = Tl(AR, 1, T, F32, "hf")
        hbk = t_g
        xc_b = [Buf(f"xc{p}") for p in range(NPC)]
        xcb_b = [Buf(f"xcb{p}") for p in range(NPC)]
        ta_b = [Buf(f"ta{p}") for p in range(NPC)]
        ti_b = [Buf(f"ti{p}") for p in range(NPC)]
        tg_b = [Buf(f"tg{p}") for p in range(NPC)]
        hf_b = [Buf(f"hf{p}") for p in range(NPC)]
        wlx = Tl(AR, KC, 128, BF16, "wlx")
        wlg = Tl(AR, KC, 128, BF16, "wlg")
        wgt = Tl(AR, 4, 128, BF16, "wgt")
        lg_s = [Tl(AR, 1, 512, F32, f"lgs{i}") for i in range(2)]
        ylo = Tl(AR, 1, T, BF16, "ylo")
        bk = 0
        for cc in range(KC):
            P.add("sp", DMA(wlx.v3(), b_w_in[l][:, C_LX + cc * 128:C_LX + (cc + 1) * 128].rearrange("(k p) n -> p k n", p=128)), reads=WC[("in", l)], writes=[wlx.b[0]], dma=True)
            P.add("sp", DMA(wlg.v3(), b_w_in[l][:, C_LG + cc * 128:C_LG + (cc + 1) * 128].rearrange("(k p) n -> p k n", p=128)), reads=WC[("in", l)], writes=[wlg.b[0]], dma=True)
            for d in range(2):
                P.add("sp", DMA(wgt.s(d), b_w_rg[l][(d * 8 + cc) * 128:(d * 8 + cc + 1) * 128, :]), reads=WC[("rg", l)], writes=[wgt.b[0]], dma=True)
                P.add("sp", DMA(wgt.s(2 + d), b_w_ig[l][(d * 8 + cc) * 128:(d * 8 + cc + 1) * 128, :]), reads=WC[("ig", l)], writes=[wgt.b[0]], dma=True)
            P.add("pool", MSET(xpad.s(0, 0, 2), 0.0), writes=[xpad.b[0]])
            P.add("pool", MSET(xpad.s(0, T + 2, T + 4), 0.0), writes=[xpad.b[0]])
            for ci, (t0, n) in enumerate(TCH):
                pa = ci % 2
                for k in range(KC):
                    P.add("pe", MM(ps[pa][:, 0:n], wlx.s(k), hT.s(k, t0, t0 + n), k == 0, k == KC - 1), reads=[wlx.b[0], hTb[ci]], writes=[psb[pa]])
                P.add("act", ACTF(xpad.s(0, 2 + t0, 2 + t0 + n), ps[pa][:, 0:n], AF.Identity), reads=[psb[pa]], writes=[xpad.b[0]])
            for p, (a0, pn) in enumerate(PCS):
                a1 = a0 + pn
                P.add("act", ACTF(xc.s(0, a0, a1), xpad.s(0, a0, a1), AF.Identity, scale=col(f"lcw{l}", 0 * 8 + cc), bias=col(f"lcb{l}", cc)),
                      reads=[xpad.b[0], consts], writes=[xc_b[p]])
                for j in range(1, 4):
                    P.add("dve", STT(xc.s(0, a0, a1), xpad.s(0, j + a0, j + a1), col(f"lcw{l}", j * 8 + cc), xc.s(0, a0, a1), ALU.mult, ALU.add),
                          reads=[xpad.b[0], consts], writes=[xc_b[p]])
                P.add("act", ACTF(xcb.s(0, a0, a1), xc.s(0, a0, a1), AF.Identity), reads=[xc_b[p]], writes=[xcb_b[p]])
            for d in range(2):
                hdst = hf if d == 0 else hbk
                order = list(range(NPC)) if d == 0 else list(range(NPC - 1, -1, -1))
                for p in order:
                    a0, pn = PCS[p]
                    a1 = a0 + pn
                    for (s0_, sn) in chunks(pn, 512):
                        q0 = a0 + s0_
                        pa, pb_ = 2 + (bk % 2), 4 + (bk % 2)
                        bk += 1
                        P.add("pe", MM(ps[pa][:, 0:sn], wgt.s(d), xcb.s(0, q0, q0 + sn), True, True), reads=[wgt.b[0], xcb_b[p]], writes=[psb[pa]])
                        P.add("pe", MM(ps[pb_][:, 0:sn], wgt.s(2 + d), xcb.s(0, q0, q0 + sn), True, True), reads=[wgt.b[0], xcb_b[p]], writes=[psb[pb_]])
                        P.add("act", ACTF(t_a.s(0, q0, q0 + sn), ps[pa][:, 0:sn], AF.Tanh, scale=0.5, bias=dcol(l, 32 + d * 8 + cc)),
                              reads=[psb[pa], consts], writes=[ta_b[p]])
                        P.add("act", ACTF(t_i.s(0, q0, q0 + sn), ps[pb_][:, 0:sn], AF.Tanh, scale=0.5, bias=dcol(l, 48 + d * 8 + cc)),
                              reads=[psb[pb_], consts], writes=[ti_b[p]])
                    P.add("act", ACTF(t_g.s(0, a0, a1), t_a.s(0, a0, a1), AF.Exp, scale=dcol(l, d * 8 + cc), bias=dcol(l, d * 8 + cc)), reads=[ta_b[p], consts], writes=[tg_b[p]])
                    P.add("act", ACTF(t_a.s(0, a0, a1), t_a.s(0, a0, a1), AF.Exp, scale=dcol(l, 16 + d * 8 + cc), bias=dcol(l, 16 + d * 8 + cc)), reads=[consts], writes=[ta_b[p]])
                    P.add("act", ACTF(t_g.s(0, a0, a1), t_g.s(0, a0, a1), AF.Sqrt, scale=-1.0, bias=onecol.ap[:, 0:1]), reads=[consts], writes=[tg_b[p]])
                    P.add("dve", STT(t_i.s(0, a0, a1), t_i.s(0, a0, a1), 1.0, xc.s(0, a0, a1), ALU.add, ALU.mult), reads=[xc_b[p]], writes=[ti_b[p]])
                    P.add("dve", STT(t_i.s(0, a0, a1), t_g.s(0, a0, a1), 0.5, t_i.s(0, a0, a1), ALU.mult, ALU.mult), reads=[tg_b[p]], writes=[ti_b[p]])
                    if d == 0:
                        init = 0.0 if p == 0 else hf.s(0, a0 - 1, a0)
                        rd = [ta_b[p], ti_b[p]] + ([hf_b[p - 1]] if p > 0 else [])
                        P.add("dve", SCAN(hf.s(0, a0, a1), t_a.s(0, a0, a1), t_i.s(0, a0, a1), init), reads=rd, writes=[hf_b[p]])
                    else:
                        init = 0.0 if p == NPC - 1 else hbk.s(0, a1, a1 + 1)
                        rd = [ta_b[p], ti_b[p]] + ([tg_b[p + 1]] if p < NPC - 1 else [])
                        P.add("dve", SCAN(hbk.ap[:, a0:a1][:, ::-1], t_a.ap[:, a0:a1][:, ::-1], t_i.ap[:, a0:a1][:, ::-1], init), reads=rd, writes=[tg_b[p]])
                        P.add("dve", TT(hf.s(0, a0, a1), hf.s(0, a0, a1), hbk.s(0, a0, a1), ALU.add), reads=[tg_b[p]], writes=[hf_b[p]])
                        for (s0_, sn) in chunks(pn, 512):
                            q0 = a0 + s0_
                            pa = 6 + (bk % 2)
                            gs = lg_s[bk % 2]
                            bk += 1
                            hci = q0 // 512
                            hcs = sorted({q0 // 512, (q0 + sn - 1) // 512})
                            for k in range(KC):
                                P.add("pe", MM(ps[pa][:, 0:sn], wlg.s(k), hT.s(k, q0, q0 + sn), k == 0, k == KC - 1), reads=[wlg.b[0]] + [hTb[c_] for c_ in hcs], writes=[psb[pa]])
                            P.add("act", ACTF(gs.s(0, 0, sn), ps[pa][:, 0:sn], AF.Square, scale=GELU_SQA), reads=[psb[pa]], writes=[gs.b[0]])
                            P.add("dve", STT(gs.s(0, 0, sn), gs.s(0, 0, sn), 1.0, ps[pa][:, 0:sn], ALU.add, ALU.mult), reads=[psb[pa]], writes=[gs.b[0]])
                            P.add("act", ACTF(gs.s(0, 0, sn), gs.s(0, 0, sn), AF.Tanh, scale=GELU_C), reads=[], writes=[gs.b[0]])
                            P.add("dve", STT(gs.s(0, 0, sn), gs.s(0, 0, sn), 1.0, ps[pa][:, 0:sn], ALU.add, ALU.mult), reads=[psb[pa]], writes=[gs.b[0]])
                            P.add("dve", STT(ylo.s(0, q0, q0 + sn), gs.s(0, 0, sn), 0.5, hf.s(0, q0, q0 + sn), ALU.mult, ALU.mult), reads=[gs.b[0], hf_b[p]], writes=[ylo.b[0]])
            P.add("sp", DMA(ylin_d[cc * 128:(cc + 1) * 128, :], ylo.s(0)), reads=[ylo.b[0]], dma=True)
            if dbg and l == 0:
                P.add("sp", DMA(dbg_outs["d_ylin"][cc * 128:(cc + 1) * 128, :], ylo.s(0)), reads=[ylo.b[0]], dma=True)
        P.barrier()
        AR.reset(LAYER)

        wol = Tl(AR, KC, 1024, BF16, "wol")
        wgl = Tl(AR, KC, 1024, BF16, "wgl")
        wot = Tl(AR, KC, 1024, BF16, "wot")
        wgm = Tl(AR, KC, 1024, BF16, "wgm")
        wD = Buf("wD")
        P.add("sp", DMA(wgm.v3(), b_w_in[l][:, C_GM:C_GM + D].rearrange("(k p) n -> p k n", p=128)), reads=WC[("in", l)], writes=[wD], dma=True)
        P.add("sp", DMA(wol.v3(), b_w_o_lru[l].rearrange("(k p) n -> p k n", p=128)), reads=WC[("ol", l)], writes=[wD], dma=True)
        P.add("sp", DMA(wgl.v3(), b_w_in[l][:, C_GL:C_GL + D].rearrange("(k p) n -> p k n", p=128)), reads=WC[("in", l)], writes=[wD], dma=True)
        P.add("sp", DMA(wot.v3(), b_w_out[l].rearrange("(k p) n -> p k n", p=128)), reads=WC[("out", l)], writes=[wD], dma=True)
        ND = 256
        DCH = chunks(T, ND)
        NDC = len(DCH)
        yl_t = [Tl(AR, KC, ND, BF16, f"yl{i}") for i in range(2)]
        zm_t = [Tl(AR, KC, ND, BF16, f"zmi{i}") for i in range(2)]
        hr_t = [Tl(AR, KC, ND, F32, f"hr{i}") for i in range(3)]
        hbo = Tl(AR, KC, ND, BF16, "hbD")
        thl = [Tl(AR, 1, ND, F32, f"thl{i}") for i in range(2)]
        zb = [Tl(AR, 1, ND, F32, f"zb{i}") for i in range(2)]
        thm2 = [Tl(AR, 1, ND, F32, f"thm2{i}") for i in range(2)]
        zb2 = [Tl(AR, 1, ND, F32, f"zb2{i}") for i in range(2)]
        zT_t = [Tl(AR, KC, ND, BF16, f"zT{i}", perk=True) for i in range(2)]
        ybfD = Tl(AR, KC, ND, BF16, "ybfD")
        ysqD = Tl(AR, KC, ND, BF16, "ysqD")
        stD = Tl(AR, 4, ND, F32, "stD")
        ln_state = {}

        def ph1(ci):
            t0, n = DCH[ci]
            hci = t0 // 512
            yl = yl_t[ci % 2]
            zmi = zm_t[ci % 2]
            hr = hr_t[ci % 3]
            zT = zT_t[ci % 2]
            P.add("sp", DMA(yl.v3(n), fm(ylin_d)[:, :, t0:t0 + n]), writes=[yl.b[0]], dma=True)
            P.add("sp", DMA(zmi.v3(n), fm(zm_d)[:, :, t0:t0 + n]), writes=[zmi.b[0]], dma=True)
            P.add("sp", DMA(hr.v3(n), fm(hres_d)[:, :, t0:t0 + n]), writes=[hr.b[0]], dma=True)
            for m in range(KC):
                pa, pb_, pc = (m % 2), 2 + (m % 2), 4 + (m % 2)
                for k in range(KC):
                    P.add("pe", MM(ps[pa][:, 0:n], wol.s(k, m * 128, (m + 1) * 128), yl.s(k, 0, n), k == 0, k == KC - 1), reads=[wD, yl.b[0]], writes=[psb[pa]])
                for k in range(KC):
                    P.add("pe", MM(ps[pb_][:, 0:n], wgl.s(k, m * 128, (m + 1) * 128), hT.s(k, t0, t0 + n), k == 0, k == KC - 1), reads=[wD, hTb[hci]], writes=[psb[pb_]])
                for k in range(KC):
                    P.add("pe", MM(ps[pc][:, 0:n], wgm.s(k, m * 128, (m + 1) * 128), hT.s(k, t0, t0 + n), k == 0, k == KC - 1), reads=[wD, hTb[hci]], writes=[psb[pc]])
                th = thl[m % 2]
                th2 = thm2[m % 2]
                z = zb[m % 2]
                z2 = zb2[m % 2]
                P.add("act", ACTF(th.s(0, 0, n), ps[pb_][:, 0:n], AF.Tanh, scale=0.5), reads=[psb[pb_]], writes=[th.b[0]])
                P.add("act", ACTF(th2.s(0, 0, n), ps[pc][:, 0:n], AF.Tanh, scale=0.5), reads=[psb[pc]], writes=[th2.b[0]])
                P.add("dve", STT(z.s(0, 0, n), th.s(0, 0, n), 1.0, ps[pa][:, 0:n], ALU.add, ALU.mult), reads=[th.b[0], psb[pa]], writes=[z.b[0]])
                P.add("dve", STT(z2.s(0, 0, n), th2.s(0, 0, n), 1.0, zmi.s(m, 0, n), ALU.add, ALU.mult), reads=[th2.b[0], zmi.b[0]], writes=[z2.b[0]])
                P.add("dve", STT(zT.s(m, 0, n), z.s(0, 0, n), 0.5, z2.s(0, 0, n), ALU.mult, ALU.add), reads=[z.b[0], z2.b[0]], writes=[zT.b[m]])

        def ph2(ci):
            t0, n = DCH[ci]
            hr = hr_t[ci % 3]
            zT = zT_t[ci % 2]
            for m in range(KC):
                for k in range(KC):
                    P.add("pe", MM(ps[6][:, 0:n], wot.s(k, m * 128, (m + 1) * 128), zT.s(k, 0, n), k == 0, k == KC - 1), reads=[wD, zT.b[k]], writes=[psb[6]])
                P.add("dve", STT(hr.s(m, 0, n), hr.s(m, 0, n), DN_ALPHA, ps[6][:, 0:n], ALU.mult, ALU.add), reads=[psb[6]], writes=[hr.b[0]])
            pre, pe_, post = ln_parts(hr, n, f"l1g{l}", f"l1b{l}", ybfD, ysqD, stD, (ps[7][:, 0:n], psb[7]), (ps[7][:, 256:256 + n], psb[7]), hbo)
            pre()
            ln_state[ci] = (pe_, post)

        def ph3(ci):
            t0, n = DCH[ci]
            hr = hr_t[ci % 3]
            pe_, post = ln_state.pop(ci)
            pe_()
            post()
            P.add("sp", DMA(fm(hres_d)[:, :, t0:t0 + n], hr.v3(n)), reads=[hr.b[0]], dma=True)
            P.add("sp", DMA(fm(hbfB_d)[:, :, t0:t0 + n], hbo.v3(n)), reads=[hbo.b[0]], dma=True)
            if dbg and l == 0:
                P.add("sp", DMA(fm(dbg_outs["d_hA1"])[:, :, t0:t0 + n], hr.v3(n)), reads=[hr.b[0]], dma=True)

        ph1(0)
        if NDC > 1:
            ph1(1)
        ph2(0)
        for ci in range(2, NDC):
            ph1(ci)
            ph3(ci - 2)
            ph2(ci - 1)
        if NDC > 1:
            ph3(NDC - 2)
            ph2(NDC - 1)
        ph3(NDC - 1)
        P.barrier()
        AR.reset(PERSIST)

        wdn = Tl(AR, NFP, 1024, BF16, "wdn")
        wE = Buf("wE")
        for j0, jn in chunks(NFP, 8):
            P.add("sp", DMA(wdn.v3()[:, j0:j0 + jn, :], b_w_down[l][j0 * 128:(j0 + jn) * 128, :].rearrange("(k p) n -> p k n", p=128)),
                  reads=WC[("dn", l)], writes=[wE], dma=True)
        dg = Tl(AR, 4 * NFC, 128, BF16, "dg")
        dgb = Buf("dg")
        id3 = identF.ap.unsqueeze(1).broadcast_to([128, NFC, 128])
        for tap in range(4):
            if tap < 3:
                o_, _ = COL_LAY[f"fcw{l}"]
                o_ += tap * NFC
            else:
                o_, _ = COL_LAY[f"fcb{l}"]
            c3 = colsT.ap[:, o_:o_ + NFC].unsqueeze(2).broadcast_to([128, NFC, 128])
            P.add("dve", TT(dg.v3()[:, tap * NFC:(tap + 1) * NFC, :], id3, c3, ALU.mult), reads=[consts], writes=[dgb])
        onesN = Tl(AR, 1, 512, BF16, "onesN")
        P.add("pool", MSET(onesN.ap, 1.0), writes=[dgb])
        NWB = 3
        wupg = [Tl(AR, KC, 128, BF16, f"wupg{i}") for i in range(NWB)]
        wupv = [Tl(AR, KC, 128, BF16, f"wupv{i}") for i in range(NWB)]
        hin = [Tl(AR, KC, 512, BF16, f"hin{i}") for i in range(2)]
        gact = Tl(AR, NFP, 510, BF16, "gact", perk=True)
        ug = [Tl(AR, 1, 512, BF16, f"ug{i}") for i in range(2)]
        uv = [Tl(AR, 1, 512, BF16, f"uv{i}") for i in range(2)]
        gq = [Tl(AR, 1, 512, F32, f"gq{i}") for i in range(2)]
        hrE = Tl(AR, KC, 512, F32, "hrE")
        hbE = Tl(AR, KC, 512, BF16, "hbE")
        ybfE = Tl(AR, KC, 512, BF16, "ybfE")
        ysqE = Tl(AR, KC, 512, BF16, "ysqE")
        stE = Tl(AR, 4, 512, F32, "stE")
        wi = 0
        deferred = []

        def emit_conv(ci, j, t0, n, off, nn):
            for (src, mcol, pb) in ((ug[j % 2], j, 4), (uv[j % 2], NFP + j, 5)):
                a = 1 if t0 == 0 else 0
                bnd = n - 1 if t0 + n == T else n
                P.add("pe", MM(ps[pb][:, 0:n], dg.s(1 * NFC + mcol), src.s(0, off, off + n), True, False), reads=[dgb, src.b[0]], writes=[psb[pb]])
                P.add("pe", MM(ps[pb][:, a:n], dg.s(0 * NFC + mcol), src.s(0, off + a - 1, off + n - 1), False, False), reads=[dgb, src.b[0]], writes=[psb[pb]])
                P.add("pe", MM(ps[pb][:, 0:bnd], dg.s(2 * NFC + mcol), src.s(0, off + 1, off + 1 + bnd), False, False), reads=[dgb, src.b[0]], writes=[psb[pb]])
                P.add("pe", MM(ps[pb][:, 0:n], dg.s(3 * NFC + mcol), onesN.s(0, 0, n), False, True), reads=[dgb], writes=[psb[pb]])
            q_ = gq[j % 2]
            P.add("act", ACTF(q_.s(0, 0, n), ps[4][:, 0:n], AF.Square, scale=GELU_SQA), reads=[psb[4]], writes=[q_.b[0]])
            P.add("dve", STT(q_.s(0, 0, n), q_.s(0, 0, n), 1.0, ps[4][:, 0:n], ALU.add, ALU.mult), reads=[psb[4]], writes=[q_.b[0]])
            P.add("act", ACTF(q_.s(0, 0, n), q_.s(0, 0, n), AF.Tanh, scale=GELU_C), reads=[], writes=[q_.b[0]])
            P.add("dve", STT(q_.s(0, 0, n), q_.s(0, 0, n), 1.0, ps[4][:, 0:n], ALU.add, ALU.mult), reads=[psb[4]], writes=[q_.b[0]])
            P.add("dve", STT(gact.s(j, 0, n), q_.s(0, 0, n), 0.5, ps[5][:, 0:n], ALU.mult, ALU.mult), reads=[q_.b[0], psb[5]], writes=[gact.b[j]])

        for ci, (t0, n) in enumerate(FCH):
            lo = max(t0 - 1, 0)
            hi = min(t0 + n + 1, T)
            nn = hi - lo
            off = t0 - lo
            hi_t = hin[ci % 2]
            P.add("sp", DMA(hi_t.v3(nn), fm(hbfB_d)[:, :, lo:hi]), writes=[hi_t.b[0]], dma=True)
            prev = None
            for j in range(NFP):
                wg = wupg[wi % NWB]
                wv = wupv[wi % NWB]
                wi += 1
                P.add("sp", DMA(wg.ap, b_w_up[l][j]), reads=WC[("up", l, j)], writes=[wg.b[0]], dma=True)
                P.add("sp", DMA(wv.ap, b_w_up[l][NFP + j]), reads=WC[("up", l, NFP + j)], writes=[wv.b[0]], dma=True)
                if j == 8:
                    P.add("sp", DMA(hrE.v3(n), fm(hres_d)[:, :, t0:t0 + n]), writes=[hrE.b[0]], dma=True)
                pg, pv = (j % 2), 2 + (j % 2)
                for k in range(KC):
                    P.add("pe", MM(ps[pg][:, 0:nn], wg.s(k), hi_t.s(k, 0, nn), k == 0, k == KC - 1), reads=[wg.b[0], hi_t.b[0]], writes=[psb[pg]])
                for k in range(KC):
                    P.add("pe", MM(ps[pv][:, 0:nn], wv.s(k), hi_t.s(k, 0, nn), k == 0, k == KC - 1), reads=[wv.b[0], hi_t.b[0]], writes=[psb[pv]])
                P.add("act", ACTF(ug[j % 2].s(0, 0, nn), ps[pg][:, 0:nn], AF.Identity), reads=[psb[pg]], writes=[ug[j % 2].b[0]])
                P.add("act", ACTF(uv[j % 2].s(0, 0, nn), ps[pv][:, 0:nn], AF.Identity), reads=[psb[pv]], writes=[uv[j % 2].b[0]])
                if j == 3 and deferred:
                    for f in deferred:
                        f()
                    deferred = []
                if prev is not None:
                    emit_conv(ci, *prev)
                prev = (j, t0, n, off, nn)
            emit_conv(ci, *prev)
            for m in range(KC):
                pa = 6 + (m % 2)
                for j in range(NFP):
                    P.add("pe", MM(ps[pa][:, 0:n], wdn.s(j, m * 128, (m + 1) * 128), gact.s(j, 0, n), j == 0, j == NFP - 1), reads=[wE, gact.b[j]], writes=[psb[pa]])
                P.add("dve", STT(hrE.s(m, 0, n), hrE.s(m, 0, n), DN_ALPHA, ps[pa][:, 0:n], ALU.mult, ALU.add), reads=[psb[pa]], writes=[hrE.b[0]])
            pre, pe_, post = ln_parts(hrE, n, f"l2g{l}", f"l2b{l}", ybfE, ysqE, stE, 6, 7, None if last else hbE)
            pre()

            def outs(t0=t0, n=n):
                if not last:
                    P.add("sp", DMA(fm(hres_d)[:, :, t0:t0 + n], hrE.v3(n)), reads=[hrE.b[0]], dma=True)
                    P.add("sp", DMA(fm(hbfA_d)[:, :, t0:t0 + n], hbE.v3(n)), reads=[hbE.b[0]], dma=True)
                    if dbg:
                        P.add("sp", DMA(fm(dbg_outs["d_hE1"])[:, :, t0:t0 + n], hrE.v3(n)), reads=[hrE.b[0]], dma=True)
                else:
                    a = max(NMETA - t0, 0)
                    if a < n:
                        P.add("sp", DMA(fm(outT)[:, :, t0 + a - NMETA:t0 + n - NMETA], hrE.v3()[:, :, a:n]), reads=[hrE.b[0]], dma=True)

            deferred = [pe_, post, outs]
        for f in deferred:
            f()
        deferred = []
        P.barrier()
        AR.reset(PERSIST)

    with nc.Block() as block:
        P.finalize_and_emit(block)
    return nc


def _colpack(v):
    v = np.asarray(v, np.float32).reshape(-1)
    return np.ascontiguousarray(v.reshape(-1, 128).T)


def pack_cols(inp):
    cols = np.zeros((128, NCOL), np.float32)

    def put(name, arr):
        o, n = COL_LAY[name]
        a = _colpack(arr)
        assert a.shape[1] == n, (name, a.shape, n)
        cols[:, o:o + n] = a

    put("ln0_g", inp["ln0_g"])
    put("ln0_b", inp["ln0_b"])
    for l in range(L):
        put(f"qn{l}", inp["q_norm"][l])
        put(f"kvn{l}", inp["kv_norm"][l])
        put(f"lcw{l}", inp["lru_conv_w"][l])
        put(f"lcb{l}", inp["lru_conv_b"][l])
        put(f"brg{l}", inp["b_rg"][l])
        put(f"big{l}", inp["b_ig"][l])
        put(f"lam{l}", inp["lru_lambda"][l])
        put(f"l1g{l}", inp["ln1_g"][l])
        put(f"l1b{l}", inp["ln1_b"][l])
        put(f"fcw{l}", inp["ffn_conv_w"][l])
        put(f"fcb{l}", inp["ffn_conv_b"][l])
        put(f"l2g{l}", inp["ln2_g"][l])
        put(f"l2b{l}", inp["ln2_b"][l])
    return cols


def rope_tables(T):
    half = 32
    inv_freq = np.exp(-math.log(10000.0) * np.arange(half, dtype=np.float32) / half).astype(np.float32)
    ang = np.arange(T, dtype=np.float32)[:, None] * inv_freq[None, :]
    cos = np.cos(ang).astype(np.float32).T
    sin = np.sin(ang).astype(np.float32).T
    return np.ascontiguousarray(np.stack([np.concatenate([cos, cos], 0), np.concatenate([sin, sin], 0)], 0))


def make_in_maps(inp, nb, SEQ):
    f = lambda a: np.ascontiguousarray(np.asarray(a, np.float32))
    T = SEQ + NMETA
    shared = {
        "metaT": f(np.asarray(inp["meta_tokens"]).T),
        "cols": pack_cols({k: np.asarray(v) for k, v in inp.items()}),
        "cs2": rope_tables(T),
        "ident": np.eye(128, dtype=np.float32),
        "w_in": f(inp["w_in"]),
        "w_uq": f(np.asarray(inp["w_uq"]).reshape(L, 256, 1536)),
        "w_uk": f(np.asarray(inp["w_uk"]).reshape(L, 128, 1024)),
        "w_uv": f(np.asarray(inp["w_uv"]).reshape(L, 128, 1024)),
        "w_o_mla": f(inp["w_o_mla"]),
        "w_rg": f(np.asarray(inp["w_rg"]).reshape(L, 2048, 128)),
        "w_ig": f(np.asarray(inp["w_ig"]).reshape(L, 2048, 128)),
        "w_o_lru": f(inp["w_o_lru"]),
        "w_out": f(inp["w_out"]),
        "w_up": f(inp["w_up"]),
        "w_down": f(inp["w_down"]),
    }
    x = np.asarray(inp["x"], np.float32)
    maps = []
    for b in range(nb):
        m = dict(shared)
        m["xT"] = np.ascontiguousarray(x[b].T)
        maps.append(m)
    return maps


_NC_CACHE = {}


def kernel(**inputs):
    x = np.asarray(inputs["x"])
    B, SEQ, _ = x.shape
    if SEQ not in _NC_CACHE:
        _NC_CACHE[SEQ] = build(SEQ)
    nc = _NC_CACHE[SEQ]
    maps = make_in_maps(inputs, B, SEQ)
    res = run_bass_kernel_spmd(nc, maps, core_ids=list(range(B)))
    out = np.stack([np.asarray(r["outT"]).T for r in res.results], 0)
    return np.ascontiguousarray(out.astype(np.float32))
```

```python
import math
import numpy as np
import concourse.bass as bass
import concourse.mybir as mybir
from concourse.bass_utils import run_bass_kernel_spmd

F32 = mybir.dt.float32
BF16 = mybir.dt.bfloat16
AF = mybir.ActivationFunctionType
ALU = mybir.AluOpType

D = 1024
KC = 8
NMETA = 16
H = 8
L = 2
DFF = 2816
NFC = 44
NFP = 22
INCOLS = 4544
C_CQ, C_CKV, C_KR, C_LG, C_LX, C_GM, C_GL = 0, 256, 384, 448, 1472, 2496, 3520
DN_ALPHA = (2.0 * L) ** 0.25
LN_EPS = 1e-5
RMS_EPS = 1e-6
ATT_SCALE = 1.0 / math.sqrt(192.0)
GELU_C = math.sqrt(2.0 / math.pi)
GELU_SQA = math.sqrt(0.044715)

SAME_ENGINE_SYNC = True
EPOCH = 16000
N_DMA_SEM = {'sp': 20, 'pool': 8, 'act': 4, 'pe': 1, 'dve': 1}
DMA_SEM_MAX_USES = 900


class Buf:
    __slots__ = ("name", "w", "r")

    def __init__(self, name=""):
        self.name = name
        self.w = None
        self.r = {}


class Op:
    __slots__ = ("eng", "fn", "deps", "dma", "needed", "ev", "seq", "prev_on_sem")

    def __init__(self, eng, fn, dma):
        self.eng = eng
        self.fn = fn
        self.dma = dma
        self.deps = []
        self.needed = bool(dma)
        self.ev = None
        self.seq = -1
        self.prev_on_sem = None


class Prog:
    ENGS = ("pe", "act", "dve", "pool", "sp")

    def __init__(self, nc):
        self.nc = nc
        self.ops = {e: [] for e in self.ENGS}
        self.known = {e: {f: -1 for f in self.ENGS} for e in self.ENGS}
        self.known_dma = {e: set() for e in self.ENGS}
        self.pending_dma = []
        self.last_real = {e: None for e in self.ENGS}

    def add(self, eng, fn, reads=(), writes=(), dma=False, extra_deps=(), force_same=False, track=True):
        op = Op(eng, fn, dma)
        op.seq = len(self.ops[eng])
        deps = {}
        for b in reads:
            if b.w is not None:
                deps[id(b.w)] = b.w
        for b in writes:
            if b.w is not None:
                deps[id(b.w)] = b.w
            for r in b.r.values():
                deps[id(r)] = r
        for d in extra_deps:
            if d is not None:
                deps[id(d)] = d
        kn = self.known[eng]
        for d in deps.values():
            if d is op:
                continue
            if d.dma:
                if id(d) in self.known_dma[eng]:
                    continue
                self.known_dma[eng].add(id(d))
                op.deps.append(d)
            else:
                if d.eng == eng and not dma and not force_same:
                    if d.eng == "pe" or not SAME_ENGINE_SYNC:
                        continue
                if d.seq <= kn[d.eng]:
                    continue
                kn[d.eng] = d.seq
                d.needed = True
                op.deps.append(d)
        for b in reads:
            key = ("dma", id(op)) if dma else eng
            b.r[key] = op
        for b in writes:
            b.w = op
            b.r = {}
        self.ops[eng].append(op)
        if dma:
            if track:
                self.pending_dma.append(op)
        else:
            self.last_real[eng] = op
        return op

    def barrier(self):
        pend = list(self.pending_dma)
        self.pending_dma = []
        lasts = [self.last_real[e] for e in self.ENGS]
        spop = self.add("sp", lambda e: e.nop(), extra_deps=pend + lasts, force_same=True)
        for e in ("pe", "act", "dve", "pool"):
            self.add(e, lambda g: g.nop(), extra_deps=[spop])

    def finalize_and_emit(self, block):
        nc = self.nc
        eng_sems = {e: [] for e in self.ENGS}
        for e in self.ENGS:
            c = 0
            for op in self.ops[e]:
                if not op.needed or op.dma:
                    continue
                c += 1
                ep = (c - 1) // EPOCH
                while len(eng_sems[e]) <= ep:
                    eng_sems[e].append(nc.alloc_semaphore(f"s_{e}_{len(eng_sems[e])}"))
                op.ev = (eng_sems[e][ep], (c - 1) % EPOCH + 1, 1)
        for e in self.ENGS:
            dma_sems = []
            dma_state = []
            dma_rr = 0
            for op in self.ops[e]:
                if not op.dma:
                    continue
                if len(dma_sems) < N_DMA_SEM[e]:
                    dma_sems.append(nc.alloc_semaphore(f"s_dma_{e}_{len(dma_sems)}"))
                    dma_state.append([0, None])
                k = dma_rr % len(dma_sems)
                if dma_state[k][0] >= DMA_SEM_MAX_USES:
                    dma_sems[k] = nc.alloc_semaphore(f"s_dma_{e}_{k}_{dma_rr}")
                    dma_state[k] = [0, None]
                dma_rr += 1
                dma_state[k][0] += 1
                op.prev_on_sem = dma_state[k][1]
                dma_state[k][1] = op
                op.ev = (dma_sems[k], 16 * dma_state[k][0], 16)

        def emit(e, engobj):
            waited = {}

            def w(ev):
                sem, val, _ = ev
                key = id(sem)
                if waited.get(key, 0) >= val:
                    return
                waited[key] = val
                engobj.wait_ge(sem, val)

            for op in self.ops[e]:
                for d in op.deps:
                    w(d.ev)
                if op.prev_on_sem is not None:
                    w(op.prev_on_sem.ev)
                ins = op.fn(engobj)
                if op.ev is not None:
                    ins.then_inc(op.ev[0], op.ev[2])

        block.sync(lambda eng: emit("sp", eng))
        block.tensor(lambda eng: emit("pe", eng))
        block.scalar(lambda eng: emit("act", eng))
        block.vector(lambda eng: emit("dve", eng))
        block.gpsimd(lambda eng: emit("pool", eng))


def MM(out, lhsT, rhs, start, stop):
    return lambda e: e.matmul(out, lhsT, rhs, start=start, stop=stop)


def TR(out, in_, ident):
    return lambda e: e.transpose(out, in_, ident)


def ACTF(out, in_, func, scale=None, bias=None):
    kw = {}
    if scale is not None:
        kw["scale"] = scale
    if bias is not None:
        kw["bias"] = bias
    return lambda e: e.activation(out, in_, func, **kw)


def TT(out, a, b, op):
    return lambda e: e.tensor_tensor(out, a, b, op)


def TS(out, a, s1, s2, op0, op1):
    return lambda e: e.tensor_scalar(out, a, s1, s2, op0, op1)


def TS1(out, a, s1, op0):
    return lambda e: e.tensor_scalar(out, a, s1, None, op0)


def STT(out, a, s, b, op0, op1):
    return lambda e: e.scalar_tensor_tensor(out, a, s, b, op0, op1)


def CP(out, in_):
    return lambda e: e.tensor_copy(out, in_)


def RCP(out, in_):
    return lambda e: e.reciprocal(out, in_)


def MSET(ap, v):
    return lambda e: e.memset(ap, v)


def DMA(out, in_):
    return lambda e: e.dma_start(out=out, in_=in_)


def SCAN(out, a, u, init):
    return lambda e: e.tensor_tensor_scan(out, a, u, init, ALU.mult, ALU.add)


def chunks(total, size):
    out = []
    t = 0
    while t < total:
        n = min(size, total - t)
        out.append((t, n))
        t += n
    return out


def col_layout():
    lay = {}
    off = 0

    def put(name, n):
        nonlocal off
        lay[name] = (off, n)
        off += n

    put("ln0_g", 8)
    put("ln0_b", 8)
    for l in range(L):
        put(f"qn{l}", 2)
        put(f"kvn{l}", 1)
        put(f"lcw{l}", 32)
        put(f"lcb{l}", 8)
        put(f"brg{l}", 16)
        put(f"big{l}", 16)
        put(f"lam{l}", 16)
        put(f"l1g{l}", 8)
        put(f"l1b{l}", 8)
        put(f"fcw{l}", 3 * NFC)
        put(f"fcb{l}", NFC)
        put(f"l2g{l}", 8)
        put(f"l2b{l}", 8)
    return lay, off


COL_LAY, NCOL = col_layout()


class Arena:
    def __init__(self, nc, units):
        self.t = nc.alloc_sbuf_tensor("arena", [128, units], BF16)
        self.units = units
        self.off = 0

    def mark(self):
        return self.off

    def reset(self, m):
        self.off = m

    def alloc(self, cols, dt):
        u = cols * (2 if dt == F32 else 1)
        u = (u + 1) // 2 * 2
        a = self.off
        self.off += u
        assert self.off <= self.units, f"SBUF arena overflow {self.off} > {self.units}"
        ap = self.t[:, a:a + u]
        if dt == F32:
            ap = ap.bitcast(F32)
        return ap


class Tl:
    def __init__(self, arena, K, n, dt, name="", perk=False):
        self.ap = arena.alloc(K * n, dt)
        self.K, self.n = K, n
        self.b = [Buf(f"{name}{k}") for k in range(K)] if perk else [Buf(name)] * K

    def s(self, k=0, a=0, b=None, p0=0, p1=128):
        if b is None:
            b = self.n
        return self.ap[p0:p1, k * self.n + a:k * self.n + b]

    def flat(self, a=0, b=None, p0=0, p1=128):
        if b is None:
            b = self.K * self.n
        return self.ap[p0:p1, a:b]

    def v3(self, n=None):
        if n is None or n == self.n:
            return self.ap.rearrange("p (k n) -> p k n", k=self.K)
        return self.ap.rearrange("p (k n) -> p k n", k=self.K)[:, :, 0:n]


def build(SEQ, dbg=False):
    T = SEQ + NMETA
    TCH = chunks(T, 512)
    KT = chunks(T, 128)
    FCH = chunks(T, 510)
    nc = bass.Bass("TRN2", target_bir_lowering=False)
    P = Prog(nc)

    def din(name, shape, dt=F32):
        return nc.dram_tensor(name, shape, dt, kind="ExternalInput").ap()

    def dscr(name, shape, dt):
        return nc.dram_tensor(name, shape, dt).ap()

    xT = din("xT", [D, SEQ])
    metaT = din("metaT", [D, NMETA])
    cols_d = din("cols", [128, NCOL])
    cs2_d = din("cs2", [2, 64, T])
    ident_d = din("ident", [128, 128])
    w_in = din("w_in", [L, D, INCOLS])
    w_uq = din("w_uq", [L, 256, 1536])
    w_uk = din("w_uk", [L, 128, 1024])
    w_uv = din("w_uv", [L, 128, 1024])
    w_o_mla = din("w_o_mla", [L, D, D])
    w_rg = din("w_rg", [L, 2048, 128])
    w_ig = din("w_ig", [L, 2048, 128])
    w_o_lru = din("w_o_lru", [L, D, D])
    w_out = din("w_out", [L, D, D])
    w_up = din("w_up", [L, D, 2 * DFF])
    w_down = din("w_down", [L, DFF, D])
    outT = nc.dram_tensor("outT", [D, SEQ], F32, kind="ExternalOutput").ap()

    b_w_in = dscr("b_w_in", [L, D, INCOLS], BF16)
    b_w_uq = dscr("b_w_uq", [L, 256, 1536], BF16)
    b_w_uv = dscr("b_w_uv", [L, 128, 1024], BF16)
    b_w_o_mla = dscr("b_w_o_mla", [L, D, D], BF16)
    b_w_rg = dscr("b_w_rg", [L, 2048, 128], BF16)
    b_w_ig = dscr("b_w_ig", [L, 2048, 128], BF16)
    b_w_o_lru = dscr("b_w_o_lru", [L, D, D], BF16)
    b_w_out = dscr("b_w_out", [L, D, D], BF16)
    b_w_up = dscr("b_w_up", [L, NFC, 128, KC * 128], BF16)
    b_w_down = dscr("b_w_down", [L, DFF, D], BF16)
    hres_d = dscr("hres", [D, T], F32)
    hbfA_d = dscr("hbfA", [D, T], BF16)
    hbfB_d = dscr("hbfB", [D, T], BF16)
    zm_d = dscr("zm", [D, T], BF16)
    ylin_d = dscr("ylin", [D, T], BF16)

    def fm(ap):
        return ap.rearrange("(k p) t -> p k t", p=128)

    dbg_outs = {}
    if dbg:
        for nm in ["d_h0", "d_hA1", "d_hE1"]:
            dbg_outs[nm] = nc.dram_tensor(nm, [D, T], F32, kind="ExternalOutput").ap()
        for nm in ["d_zm", "d_ylin"]:
            dbg_outs[nm] = nc.dram_tensor(nm, [D, T], BF16, kind="ExternalOutput").ap()

    WC = {}

    def cast2d(key, dst, src, R, C):
        for r0, rn in chunks(R, 1024):
            for c0, cn in chunks(C, 2048):
                b = Buf(key)
                WC.setdefault(key, []).append(b)
                P.add("pool", DMA(dst[r0:r0 + rn, c0:c0 + cn], src[r0:r0 + rn, c0:c0 + cn]), writes=[b], dma=True, track=False)

    def cast_cols(key, l, c0, c1):
        b = Buf("in")
        WC.setdefault(key, []).append(b)
        P.add("pool", DMA(b_w_in[l][:, c0:c1], w_in[l][:, c0:c1]), writes=[b], dma=True, track=False)

    def cast_layer_urgent(l):
        cast_cols(("inA", l), l, 0, 448)
        cast2d(("uq", l), b_w_uq[l], w_uq[l], 256, 1536)
        cast2d(("uv", l), b_w_uv[l], w_uv[l], 128, 1024)
        cast2d(("om", l), b_w_o_mla[l], w_o_mla[l], D, D)

    def cast_layer_small(l):
        cast_cols(("inLX", l), l, C_LX, C_LX + 1024)
        cast_cols(("inLG", l), l, C_LG, C_LG + 1024)
        cast_cols(("inGM", l), l, C_GM, C_GM + 1024)
        cast_cols(("inGL", l), l, C_GL, C_GL + 1024)
        cast2d(("rg", l), b_w_rg[l], w_rg[l], 2048, 128)
        cast2d(("ig", l), b_w_ig[l], w_ig[l], 2048, 128)
        cast2d(("ol", l), b_w_o_lru[l], w_o_lru[l], D, D)
        cast2d(("out", l), b_w_out[l], w_out[l], D, D)

    def cast_layer_ffn(l):
        cast2d(("dn", l), b_w_down[l], w_down[l], DFF, D)
        for m in range(NFC):
            for k0 in range(0, KC, 4):
                src = w_up[l][k0 * 128:(k0 + 4) * 128, m * 128:(m + 1) * 128].rearrange("(k p) c -> p k c", p=128)
                dst = b_w_up[l][m][:, k0 * 128:(k0 + 4) * 128].rearrange("p (k c) -> p k c", c=128)
                b = Buf("up")
                WC.setdefault(("up", l, m), []).append(b)
                P.add("pool", DMA(dst, src), writes=[b], dma=True, track=False)

    cast_layer_urgent(0)

    AR = Arena(nc, 106400)
    psW = [nc.alloc_psum_tensor(f"psw{i}", [128, 1024], F32) for i in range(4)]
    ps = [psW[i // 2][:, (i % 2) * 512:(i % 2 + 1) * 512] for i in range(8)]
    psb = [Buf(f"ps{i}") for i in range(8)]

    colsT = Tl(AR, 1, NCOL, F32, "cols")
    dcols = Tl(AR, 1, L * 64, F32, "dcols")
    identF = Tl(AR, 1, 128, F32, "ident")
    onesD = Tl(AR, 1, 128, BF16, "onesD")
    ones256 = Tl(AR, 1, 128, BF16, "ones256")
    ones128 = Tl(AR, 1, 128, BF16, "ones128")
    ones1 = Tl(AR, 1, 128, BF16, "ones1")
    consts = Buf("consts")
    onecol = Tl(AR, 1, 2, F32, "onecol")

    def col(name, i=0, p0=0, p1=128):
        o, n = COL_LAY[name]
        return colsT.ap[p0:p1, o + i:o + i + 1]

    def dcol(l, i, p0=0, p1=128):
        return dcols.ap[p0:p1, l * 64 + i:l * 64 + i + 1]

    P.add("sp", DMA(colsT.ap, cols_d), writes=[consts], dma=True)
    P.add("sp", DMA(identF.ap, ident_d), writes=[consts], dma=True)
    P.add("pool", MSET(onesD.ap, 1.0 / 1024), writes=[consts])
    P.add("pool", MSET(ones256.ap, 1.0 / 256), writes=[consts])
    P.add("pool", MSET(ones128.ap, 1.0 / 128), writes=[consts])
    P.add("pool", MSET(ones1.ap, 1.0), writes=[consts])
    P.add("pool", MSET(onecol.ap, 1.0), writes=[consts])
    for l in range(L):
        o, _ = COL_LAY[f"lam{l}"]
        lam = colsT.ap[:, o:o + 16]
        d0 = dcols.ap[:, l * 64:l * 64 + 16]
        d1 = dcols.ap[:, l * 64 + 16:l * 64 + 32]
        P.add("act", ACTF(d0, lam, AF.Exp, scale=-1.0), reads=[consts], writes=[consts])
        P.add("dve", TS1(d0, d0, 1.0, ALU.add), reads=[consts], writes=[consts])
        P.add("act", ACTF(d0, d0, AF.Ln), reads=[consts], writes=[consts])
        P.add("dve", TS1(d1, d0, -4.0, ALU.mult), reads=[consts], writes=[consts])
        P.add("dve", TS1(d0, d0, -8.0, ALU.mult), reads=[consts], writes=[consts])
        o, _ = COL_LAY[f"brg{l}"]
        P.add("dve", TS1(dcols.ap[:, l * 64 + 32:l * 64 + 48], colsT.ap[:, o:o + 16], 0.5, ALU.mult), reads=[consts], writes=[consts])
        o, _ = COL_LAY[f"big{l}"]
        P.add("dve", TS1(dcols.ap[:, l * 64 + 48:l * 64 + 64], colsT.ap[:, o:o + 16], 0.5, ALU.mult), reads=[consts], writes=[consts])

    PERSIST = AR.mark()

    def ln_parts(y, n, gname, bname, ybf, ysq, st, psA, psB, hbf):
        yb = y.b[0]
        if isinstance(psA, int):
            apA, bA, apB, bB = ps[psA][:, 0:n], psb[psA], ps[psB][:, 0:n], psb[psB]
        else:
            (apA, bA), (apB, bB) = psA, psB

        def pre():
            P.add("act", ACTF(ybf.v3(n), y.v3(n), AF.Identity), reads=[yb], writes=[ybf.b[0]])
            P.add("act", ACTF(ysq.v3(n), y.v3(n), AF.Square), reads=[yb], writes=[ysq.b[0]])

        def pe():
            for k in range(KC):
                P.add("pe", MM(apA, onesD.ap, ybf.s(k, 0, n), k == 0, k == KC - 1), reads=[ybf.b[0], consts], writes=[bA])
            for k in range(KC):
                P.add("pe", MM(apB, onesD.ap, ysq.s(k, 0, n), k == 0, k == KC - 1), reads=[ysq.b[0], consts], writes=[bB])

        def post():
            mean, msq, var, rstd = (st.s(i, 0, n) for i in range(4))
            sb = st.b[0]
            P.add("dve", CP(mean, apA), reads=[bA], writes=[sb])
            P.add("dve", TT(msq, mean, mean, ALU.mult), reads=[sb], writes=[sb])
            P.add("dve", STT(var, apB, LN_EPS, msq, ALU.add, ALU.subtract), reads=[bB, sb], writes=[sb])
            P.add("act", ACTF(var, var, AF.Sqrt), reads=[sb], writes=[sb])
            P.add("dve", RCP(rstd, var), reads=[sb], writes=[sb])
            mean3 = mean.unsqueeze(1).broadcast_to([128, KC, n])
            rstd3 = rstd.unsqueeze(1).broadcast_to([128, KC, n])
            P.add("dve", TT(y.v3(n), y.v3(n), mean3, ALU.subtract), reads=[sb], writes=[yb])
            P.add("dve", TT(y.v3(n), y.v3(n), rstd3, ALU.mult), reads=[sb], writes=[yb])
            for k in range(KC):
                if hbf is not None:
                    P.add("act", ACTF(hbf.s(k, 0, n), y.s(k, 0, n), AF.Identity, scale=col(gname, k), bias=col(bname, k)), reads=[consts, yb], writes=[hbf.b[0]])
                P.add("act", ACTF(y.s(k, 0, n), y.s(k, 0, n), AF.Identity, scale=col(gname, k), bias=col(bname, k)), reads=[consts], writes=[yb])

        return pre, pe, post

    def ln_chunk(*a):
        pre, pe, post = ln_parts(*a)
        pre()
        pe()
        post()

    ys = [Tl(AR, KC, 512, F32, f"y{i}") for i in range(2)]
    hb = [Tl(AR, KC, 512, BF16, f"hb{i}") for i in range(2)]
    ybf0 = [Tl(AR, KC, 512, BF16, f"ybf{i}") for i in range(2)]
    ysq0 = [Tl(AR, KC, 512, BF16, f"ysq{i}") for i in range(2)]
    st0 = [Tl(AR, 4, 512, F32, f"st{i}") for i in range(2)]
    s0_parts = {}

    def s0_load_pre(ci):
        t0, n = TCH[ci]
        y = ys[ci % 2]
        if t0 == 0:
            P.add("sp", DMA(y.v3()[:, :, 0:NMETA], fm(metaT)), writes=[y.b[0]], dma=True)
            P.add("sp", DMA(y.v3()[:, :, NMETA:n], fm(xT)[:, :, 0:n - NMETA]), writes=[y.b[0]], dma=True)
        else:
            P.add("sp", DMA(y.v3(n), fm(xT)[:, :, t0 - NMETA:t0 - NMETA + n]), writes=[y.b[0]], dma=True)
        pre, pe_, post = ln_parts(y, n, "ln0_g", "ln0_b", ybf0[ci % 2], ysq0[ci % 2], st0[ci % 2], 2 * (ci % 2), 2 * (ci % 2) + 1, hb[ci % 2])
        pre()
        s0_parts[ci] = (pe_, post)

    def s0_finish(ci):
        t0, n = TCH[ci]
        y = ys[ci % 2]
        hbt = hb[ci % 2]
        pe_, post = s0_parts.pop(ci)
        pe_()
        post()
        P.add("sp", DMA(fm(hres_d)[:, :, t0:t0 + n], y.v3(n)), reads=[y.b[0]], dma=True)
        P.add("sp", DMA(fm(hbfA_d)[:, :, t0:t0 + n], hbt.v3(n)), reads=[hbt.b[0]], dma=True)
        if dbg:
            P.add("sp", DMA(fm(dbg_outs["d_h0"])[:, :, t0:t0 + n], y.v3(n)), reads=[y.b[0]], dma=True)

    s0_load_pre(0)
    for ci in range(1, len(TCH)):
        s0_load_pre(ci)
        s0_finish(ci - 1)
    s0_finish(len(TCH) - 1)
    P.barrier()
    AR.reset(PERSIST)

    for l in range(L):
        last = l == L - 1
        hT = Tl(AR, KC, T, BF16, "hT", perk=False)
        hTb = [Buf(f"hT{c}") for c in range(len(TCH))]
        for ci, (t0, n) in enumerate(TCH):
            P.add("sp", DMA(hT.v3()[:, :, t0:t0 + n], fm(hbfA_d)[:, :, t0:t0 + n]), writes=[hTb[ci]], dma=True)
        LAYER = AR.mark()

        cqn = Tl(AR, 2, T, BF16, "cqn")
        ckvn = Tl(AR, 1, T, BF16, "ckvn")
        krope = Tl(AR, 1, T, BF16, "krope")
        Vt = Tl(AR, len(KT), 128, BF16, "V")
        latb = [Buf(f"lat{c}") for c in range(len(TCH))]
        Vb = Buf("V")
        kpad = Buf("kpad")
        P.add("pool", MSET(krope.s(0, 0, T, 64, 128), 0.0), writes=[kpad])
        STA = AR.mark()
        wA = Tl(AR, KC, 448, BF16, "wA")
        P.add("sp", DMA(wA.v3(), b_w_in[l][:, 0:448].rearrange("(k p) n -> p k n", p=128)), reads=WC[("inA", l)], writes=[wA.b[0]], dma=True)
        cqf = Tl(AR, 2, 512, F32, "cqf")
        cqs = Tl(AR, 2, 512, BF16, "cqs")
        ckf = Tl(AR, 1, 512, F32, "ckf")
        cks = Tl(AR, 1, 512, BF16, "cks")
        ckn = Tl(AR, 1, 512, F32, "ckn")
        krf = Tl(AR, 1, 512, F32, "krf")
        rot = Tl(AR, 1, 512, F32, "rot")
        cst = Tl(AR, 2, 512, F32, "cst")
        rq = Tl(AR, 1, 512, F32, "rq")
        rk = Tl(AR, 1, 512, F32, "rk")
        for ci, (t0, n) in enumerate(TCH):
            P.add("sp", DMA(cst.v3(n)[0:64], cs2_d[:, :, t0:t0 + n].rearrange("a p t -> p a t")), writes=[cst.b[0]], dma=True)
            for m in range(2):
                for k in range(KC):
                    P.add("pe", MM(ps[m][:, 0:n], wA.s(k, m * 128, (m + 1) * 128), hT.s(k, t0, t0 + n), k == 0, k == KC - 1),
                          reads=[wA.b[0], hTb[ci]], writes=[psb[m]])
            for k in range(KC):
                P.add("pe", MM(ps[2][:, 0:n], wA.s(k, 256, 384), hT.s(k, t0, t0 + n), k == 0, k == KC - 1), reads=[wA.b[0], hTb[ci]], writes=[psb[2]])
            for k in range(KC):
                P.add("pe", MM(ps[3][0:64, 0:n], wA.s(k, 384, 448), hT.s(k, t0, t0 + n), k == 0, k == KC - 1), reads=[wA.b[0], hTb[ci]], writes=[psb[3]])
            for m in range(2):
                P.add("act", ACTF(cqf.s(m, 0, n), ps[m][:, 0:n], AF.Identity), reads=[psb[m]], writes=[cqf.b[0]])
                P.add("act", ACTF(cqs.s(m, 0, n), ps[m][:, 0:n], AF.Square), reads=[psb[m]], writes=[cqs.b[0]])
            P.add("act", ACTF(ckf.s(0, 0, n), ps[2][:, 0:n], AF.Identity), reads=[psb[2]], writes=[ckf.b[0]])
            P.add("act", ACTF(cks.s(0, 0, n), ps[2][:, 0:n], AF.Square), reads=[psb[2]], writes=[cks.b[0]])
            P.add("dve", CP(krf.s(0, 0, n, 0, 64), ps[3][0:64, 0:n]), reads=[psb[3]], writes=[krf.b[0]])
            for m in range(2):
                P.add("pe", MM(ps[4][:, 0:n], ones256.ap, cqs.s(m, 0, n), m == 0, m == 1), reads=[cqs.b[0], consts], writes=[psb[4]])
            P.add("pe", MM(ps[5][:, 0:n], ones128.ap, cks.s(0, 0, n), True, True), reads=[cks.b[0], consts], writes=[psb[5]])
            P.add("dve", TS1(rq.s(0, 0, n), ps[4][:, 0:n], RMS_EPS, ALU.add), reads=[psb[4]], writes=[rq.b[0]])
            P.add("dve", TS1(rk.s(0, 0, n), ps[5][:, 0:n], RMS_EPS, ALU.add), reads=[psb[5]], writes=[rk.b[0]])
            P.add("act", ACTF(rq.s(0, 0, n), rq.s(0, 0, n), AF.Sqrt), reads=[], writes=[rq.b[0]])
            P.add("act", ACTF(rk.s(0, 0, n), rk.s(0, 0, n), AF.Sqrt), reads=[], writes=[rk.b[0]])
            P.add("dve", RCP(rq.s(0, 0, n), rq.s(0, 0, n)), reads=[], writes=[rq.b[0]])
            P.add("dve", RCP(rk.s(0, 0, n), rk.s(0, 0, n)), reads=[], writes=[rk.b[0]])
            for m in range(2):
                P.add("dve", STT(cqn.s(m, t0, t0 + n), cqf.s(m, 0, n), col(f"qn{l}", m), rq.s(0, 0, n), ALU.mult, ALU.mult),
                      reads=[cqf.b[0], rq.b[0], consts], writes=[latb[ci]])
            P.add("dve", STT(ckn.s(0, 0, n), ckf.s(0, 0, n), col(f"kvn{l}", 0), rk.s(0, 0, n), ALU.mult, ALU.mult),
                  reads=[ckf.b[0], rk.b[0], consts], writes=[ckn.b[0]])
            P.add("act", ACTF(ckvn.s(0, t0, t0 + n), ckn.s(0, 0, n), AF.Identity), reads=[ckn.b[0]], writes=[latb[ci]])
            P.add("act", ACTF(rot.s(0, 0, n, 0, 32), krf.s(0, 0, n, 32, 64), AF.Identity, scale=-1.0), reads=[krf.b[0]], writes=[rot.b[0]])
            P.add("act", ACTF(rot.s(0, 0, n, 32, 64), krf.s(0, 0, n, 0, 32), AF.Identity), reads=[krf.b[0]], writes=[rot.b[0]])
            P.add("dve", TT(krf.s(0, 0, n, 0, 64), krf.s(0, 0, n, 0, 64), cst.s(0, 0, n, 0, 64), ALU.mult), reads=[cst.b[0], rot.b[0]], writes=[krf.b[0]])
            P.add("dve", TT(rot.s(0, 0, n, 0, 64), rot.s(0, 0, n, 0, 64), cst.s(1, 0, n, 0, 64), ALU.mult), reads=[cst.b[0]], writes=[rot.b[0]])
            P.add("dve", TT(krope.s(0, t0, t0 + n, 0, 64), krf.s(0, 0, n, 0, 64), rot.s(0, 0, n, 0, 64), ALU.add), reads=[krf.b[0], rot.b[0]], writes=[latb[ci]])
            for j, (kt0, kn) in enumerate(KT):
                if kt0 < t0 or kt0 >= t0 + n:
                    continue
                a = kt0 - t0
                P.add("pe", TR(ps[6][0:kn, 0:128], ckn.s(0, a, a + kn), identF.ap), reads=[ckn.b[0], consts], writes=[psb[6]])
                P.add("act", ACTF(Vt.s(j, 0, 128, 0, kn), ps[6][0:kn, 0:128], AF.Identity), reads=[psb[6]], writes=[Vb])
        P.barrier()
        AR.reset(STA)
        if l == 0:
            cast_layer_small(0)
            cast_layer_ffn(0)
            cast_layer_urgent(1)
            cast_layer_small(1)
            cast_layer_ffn(1)

        wuq = Tl(AR, 2, 1536, BF16, "wuq")
        wukT = Tl(AR, H, 128, BF16, "wukT")
        wuv = Tl(AR, 1, 1024, BF16, "wuv")
        wom = Tl(AR, KC, 1024, BF16, "wom")
        wB = Buf("wB")
        P.add("sp", DMA(wuq.v3(), b_w_uq[l].rearrange("(k p) n -> p k n", p=128)), reads=WC[("uq", l)], writes=[wB], dma=True)
        P.add("sp", DMA(wuv.ap, b_w_uv[l]), reads=WC[("uv", l)], writes=[wB], dma=True)
        P.add("sp", DMA(wom.v3(), b_w_o_mla[l].rearrange("(k p) n -> p k n", p=128)), reads=WC[("om", l)], writes=[wB], dma=True)
        qn_t = [Tl(AR, 1, 512, BF16, f"qn{i}") for i in range(2)]
        qabs = Tl(AR, H, 512, BF16, "qabs", perk=True)
        qrf = [Tl(AR, 1, 512, F32, f"qrf{i}") for i in range(2)]
        qrot = [Tl(AR, 1, 512, F32, f"qrot{i}") for i in range(2)]
        qrope = Tl(AR, H, 512, BF16, "qrope", perk=True)
        cstq = Tl(AR, 2, 512, F32, "cstq")
        qpad = Buf("qpad")
        P.add("pool", MSET(qrope.flat(0, H * 512, 64, 128), 0.0), writes=[qpad])
        NPT = 3
        PT = [Tl(AR, 2, 512, BF16, f"PT{i}") for i in range(NPT)]
        rl = [Tl(AR, 1, 512, F32, f"rl{i}") for i in range(2)]
        psum_t = [Tl(AR, 1, 512, BF16, f"psum{i}") for i in range(2)]
        olat = [Tl(AR, 1, 512, BF16, f"olat{i}") for i in range(2)]
        oT = qabs
        zm1 = Tl(AR, KC, 512, BF16, "zmT")
        wukF_ap = zm1.ap.bitcast(F32)[:, 0:1024]
        P.add("sp", DMA(wukF_ap, w_uk[l]), writes=[zm1.b[0]], dma=True)
        wukTb = Buf("wukT")
        for h in range(H):
            P.add("pe", TR(ps[5][:, 0:128], wukF_ap[:, h * 128:(h + 1) * 128], identF.ap), reads=[zm1.b[0], consts], writes=[psb[5]])
            P.add("act", ACTF(wukT.s(h), ps[5][:, 0:128], AF.Identity), reads=[psb[5]], writes=[wukTb])
        KG = []
        full = [(j, kt0, kn) for j, (kt0, kn) in enumerate(KT) if kn == 128]
        for i in range(0, len(full), 2):
            KG.append(full[i:i + 2])
        for j, (kt0, kn) in enumerate(KT):
            if kn != 128:
                KG.append([(j, kt0, kn)])
        NG = len(KG)
        sgb = [Buf("sg0"), Buf("sg1")]
        B_O, B_L, B_MA, B_MB = 4, 5, 6, 7
        gi = 0
        for ci, (t0, n) in enumerate(TCH):
            P.add("sp", DMA(cstq.v3(n)[0:64], cs2_d[:, :, t0:t0 + n].rearrange("a p t -> p a t")), writes=[cstq.b[0]], dma=True)
            for h in range(H):
                pa, pb_ = B_MA, B_MB
                qn = qn_t[h % 2]
                qf = qrf[h % 2]
                qo = qrot[h % 2]
                for k in range(2):
                    P.add("pe", MM(ps[pa][:, 0:n], wuq.s(k, h * 192, h * 192 + 128), cqn.s(k, t0, t0 + n), k == 0, k == 1),
                          reads=[wB, latb[ci]], writes=[psb[pa]])
                P.add("act", ACTF(qn.s(0, 0, n), ps[pa][:, 0:n], AF.Identity), reads=[psb[pa]], writes=[qn.b[0]])
                for k in range(2):
                    P.add("pe", MM(ps[pb_][0:64, 0:n], wuq.s(k, h * 192 + 128, h * 192 + 192), cqn.s(k, t0, t0 + n), k == 0, k == 1),
                          reads=[wB, latb[ci]], writes=[psb[pb_]])
                P.add("dve", CP(qf.s(0, 0, n, 0, 64), ps[pb_][0:64, 0:n]), reads=[psb[pb_]], writes=[qf.b[0]])
                P.add("pe", MM(ps[pa][:, 0:n], wukT.s(h), qn.s(0, 0, n), True, True), reads=[wukTb, qn.b[0]], writes=[psb[pa]])
                P.add("act", ACTF(qabs.s(h, 0, n), ps[pa][:, 0:n], AF.Identity), reads=[psb[pa]], writes=[qabs.b[h]])
                P.add("act", ACTF(qo.s(0, 0, n, 0, 32), qf.s(0, 0, n, 32, 64), AF.Identity, scale=-1.0), reads=[qf.b[0]], writes=[qo.b[0]])
                P.add("act", ACTF(qo.s(0, 0, n, 32, 64), qf.s(0, 0, n, 0, 32), AF.Identity), reads=[qf.b[0]], writes=[qo.b[0]])
                P.add("dve", TT(qf.s(0, 0, n, 0, 64), qf.s(0, 0, n, 0, 64), cstq.s(0, 0, n, 0, 64), ALU.mult), reads=[cstq.b[0], qo.b[0]], writes=[qf.b[0]])
                P.add("dve", TT(qo.s(0, 0, n, 0, 64), qo.s(0, 0, n, 0, 64), cstq.s(1, 0, n, 0, 64), ALU.mult), reads=[cstq.b[0]], writes=[qo.b[0]])
                P.add("dve", TT(qrope.s(h, 0, n, 0, 64), qf.s(0, 0, n, 0, 64), qo.s(0, 0, n, 0, 64), ALU.add), reads=[qf.b[0], qo.b[0]], writes=[qrope.b[h]])
            pending_vup = None

            def emit_S(h, g, gidx):
                sg = psW[gidx % 2]
                for ti, (j, kt0, kn) in enumerate(KG[g]):
                    kci = kt0 // 512
                    o_ = sg[0:kn, ti * 512:ti * 512 + n]
                    P.add("pe", MM(o_, ckvn.s(0, kt0, kt0 + kn), qabs.s(h, 0, n), True, False), reads=[latb[kci], qabs.b[h]], writes=[sgb[gidx % 2]])
                    P.add("pe", MM(o_, krope.s(0, kt0, kt0 + kn), qrope.s(h, 0, n), False, True), reads=[latb[kci], qrope.b[h], kpad, qpad], writes=[sgb[gidx % 2]])

            def emit_exp(g, gidx):
                sg = psW[gidx % 2]
                pt = PT[gidx % NPT]
                nt = len(KG[g])
                kn = KG[g][0][2]
                src = sg.rearrange("p (t c) -> p t c", t=2)[0:kn, 0:nt, 0:n]
                dst = pt.v3()[0:kn, 0:nt, 0:n]
                P.add("act", ACTF(dst, src, AF.Exp, scale=ATT_SCALE), reads=[sgb[gidx % 2]], writes=[pt.b[0]])
                if nt == 2:
                    pst = psum_t[gidx % 2]
                    P.add("dve", TT(pst.s(0, 0, n, 0, kn), pt.s(0, 0, n, 0, kn), pt.s(1, 0, n, 0, kn), ALU.add), reads=[pt.b[0]], writes=[pst.b[0]])

            def emit_PV(g, gidx):
                pt = PT[gidx % NPT]
                for ti, (j, kt0, kn) in enumerate(KG[g]):
                    first = (g == 0 and ti == 0)
                    lastt = (g == NG - 1 and ti == len(KG[g]) - 1)
                    P.add("pe", MM(ps[B_O][:, 0:n], Vt.s(j, 0, 128, 0, kn), pt.s(ti, 0, n, 0, kn), first, lastt), reads=[Vb, pt.b[0]], writes=[psb[B_O]])
                kn0 = KG[g][0][2]
                if len(KG[g]) == 2:
                    src_, sb_ = psum_t[gidx % 2].s(0, 0, n, 0, kn0), psum_t[gidx % 2].b[0]
                else:
                    src_, sb_ = pt.s(0, 0, n, 0, kn0), pt.b[0]
                P.add("pe", MM(ps[B_L][:, 0:n], ones1.s(0, 0, 128, 0, kn0), src_, g == 0, g == NG - 1), reads=[consts, sb_], writes=[psb[B_L]])

            def emit_vup(h):
                ol = olat[h % 2]
                P.add("pe", MM(ps[B_MA][:, 0:n], wuv.s(0, h * 128, (h + 1) * 128), ol.s(0, 0, n), True, True), reads=[wB, ol.b[0]], writes=[psb[B_MA]])
                P.add("act", ACTF(oT.s(h, 0, n), ps[B_MA][:, 0:n], AF.Identity), reads=[psb[B_MA]], writes=[oT.b[h]])

            for h in range(H):
                gbase = gi
                emit_S(h, 0, gbase)
                emit_exp(0, gbase)
                for g in range(1, NG):
                    emit_S(h, g, gbase + g)
                    emit_exp(g, gbase + g)
                    if g == 2 and pending_vup is not None:
                        emit_vup(pending_vup)
                        pending_vup = None
                    emit_PV(g - 1, gbase + g - 1)
                emit_PV(NG - 1, gbase + NG - 1)
                gi += NG
                r_ = rl[h % 2]
                ol = olat[h % 2]
                P.add("dve", RCP(r_.s(0, 0, n), ps[B_L][:, 0:n]), reads=[psb[B_L]], writes=[r_.b[0]])
                P.add("dve", TT(ol.s(0, 0, n), ps[B_O][:, 0:n], r_.s(0, 0, n), ALU.mult), reads=[psb[B_O], r_.b[0]], writes=[ol.b[0]])
                if pending_vup is not None:
                    emit_vup(pending_vup)
                pending_vup = h
            emit_vup(pending_vup)
            zo = zm1
            for m in range(KC):
                pa = B_MA + (m % 2)
                for hh in range(H):
                    P.add("pe", MM(ps[pa][:, 0:n], wom.s(hh, m * 128, (m + 1) * 128), oT.s(hh, 0, n), hh == 0, hh == H - 1),
                          reads=[wB] + [oT.b[hh]], writes=[psb[pa]])
                P.add("act", ACTF(zo.s(m, 0, n), ps[pa][:, 0:n], AF.Identity, scale=0.5), reads=[psb[pa]], writes=[zo.b[0]])
            P.add("sp", DMA(fm(zm_d)[:, :, t0:t0 + n], zo.v3(n)), reads=[zo.b[0]], dma=True)
            if dbg and l == 0:
                P.add("sp", DMA(fm(dbg_outs["d_zm"])[:, :, t0:t0 + n], zo.v3(n)), reads=[zo.b[0]], dma=True)
        P.barrier()
        AR.reset(LAYER)

        TP = T + 4
        PCS = chunks(T, 1024)
        NPC = len(PCS)
        xpad = Tl(AR, 1, TP, F32, "xpad")
        xc = Tl(AR, 1, T, F32, "xc")
        xcb = Tl(AR, 1, T, BF16, "xcb")
        t_a = Tl(AR, 1, T, F32, "t_a")
        t_i = Tl(AR, 1, T, F32, "t_i")
        t_g = Tl(AR, 1, T, F32, "t_g")
        hf = Tl(AR, 1, T, F32, "hf")
        hbk = t_g
        xc_b = [Buf(f"xc{p}") for p in range(NPC)]
        xcb_b = [Buf(f"xcb{p}") for p in range(NPC)]
        ta_b = [Buf(f"ta{p}") for p in range(NPC)]
        ti_b = [Buf(f"ti{p}") for p in range(NPC)]
        tg_b = [Buf(f"tg{p}") for p in range(NPC)]
        hf_b = [Buf(f"hf{p}") for p in range(NPC)]
        wlx = Tl(AR, KC, 128, BF16, "wlx")
        wlg = Tl(AR, KC, 128, BF16, "wlg")
        wgt = Tl(AR, 4, 128, BF16, "wgt")
        lg_s = [Tl(AR, 1, 512, F32, f"lgs{i}") for i in range(2)]
        ylo = Tl(AR, 1, T, BF16, "ylo")
        bk = 0
        for cc in range(KC):
            P.add("sp", DMA(wlx.v3(), b_w_in[l][:, C_LX + cc * 128:C_LX + (cc + 1) * 128].rearrange("(k p) n -> p k n", p=128)), reads=WC[("inLX", l)], writes=[wlx.b[0]], dma=True)
            P.add("sp", DMA(wlg.v3(), b_w_in[l][:, C_LG + cc * 128:C_LG + (cc + 1) * 128].rearrange("(k p) n -> p k n", p=128)), reads=WC[("inLG", l)], writes=[wlg.b[0]], dma=True)
            for d in range(2):
                P.add("sp", DMA(wgt.s(d), b_w_rg[l][(d * 8 + cc) * 128:(d * 8 + cc + 1) * 128, :]), reads=WC[("rg", l)], writes=[wgt.b[0]], dma=True)
                P.add("sp", DMA(wgt.s(2 + d), b_w_ig[l][(d * 8 + cc) * 128:(d * 8 + cc + 1) * 128, :]), reads=WC[("ig", l)], writes=[wgt.b[0]], dma=True)
            P.add("pool", MSET(xpad.s(0, 0, 2), 0.0), writes=[xpad.b[0]])
            P.add("pool", MSET(xpad.s(0, T + 2, T + 4), 0.0), writes=[xpad.b[0]])
            for ci, (t0, n) in enumerate(TCH):
                pa = ci % 2
                for k in range(KC):
                    P.add("pe", MM(ps[pa][:, 0:n], wlx.s(k), hT.s(k, t0, t0 + n), k == 0, k == KC - 1), reads=[wlx.b[0], hTb[ci]], writes=[psb[pa]])
                P.add("act", ACTF(xpad.s(0, 2 + t0, 2 + t0 + n), ps[pa][:, 0:n], AF.Identity), reads=[psb[pa]], writes=[xpad.b[0]])
            for p, (a0, pn) in enumerate(PCS):
                a1 = a0 + pn
                P.add("act", ACTF(xc.s(0, a0, a1), xpad.s(0, a0, a1), AF.Identity, scale=col(f"lcw{l}", 0 * 8 + cc), bias=col(f"lcb{l}", cc)),
                      reads=[xpad.b[0], consts], writes=[xc_b[p]])
                for j in range(1, 4):
                    P.add("dve", STT(xc.s(0, a0, a1), xpad.s(0, j + a0, j + a1), col(f"lcw{l}", j * 8 + cc), xc.s(0, a0, a1), ALU.mult, ALU.add),
                          reads=[xpad.b[0], consts], writes=[xc_b[p]])
                P.add("act", ACTF(xcb.s(0, a0, a1), xc.s(0, a0, a1), AF.Identity), reads=[xc_b[p]], writes=[xcb_b[p]])
            for d in range(2):
                hdst = hf if d == 0 else hbk
                order = list(range(NPC)) if d == 0 else list(range(NPC - 1, -1, -1))
                for p in order:
                    a0, pn = PCS[p]
                    a1 = a0 + pn
                    for (s0_, sn) in chunks(pn, 512):
                        q0 = a0 + s0_
                        pa, pb_ = 2 + (bk % 2), 4 + (bk % 2)
                        bk += 1
                        P.add("pe", MM(ps[pa][:, 0:sn], wgt.s(d), xcb.s(0, q0, q0 + sn), True, True), reads=[wgt.b[0], xcb_b[p]], writes=[psb[pa]])
                        P.add("pe", MM(ps[pb_][:, 0:sn], wgt.s(2 + d), xcb.s(0, q0, q0 + sn), True, True), reads=[wgt.b[0], xcb_b[p]], writes=[psb[pb_]])
                        P.add("act", ACTF(t_a.s(0, q0, q0 + sn), ps[pa][:, 0:sn], AF.Tanh, scale=0.5, bias=dcol(l, 32 + d * 8 + cc)),
                              reads=[psb[pa], consts], writes=[ta_b[p]])
                        P.add("act", ACTF(t_i.s(0, q0, q0 + sn), ps[pb_][:, 0:sn], AF.Tanh, scale=0.5, bias=dcol(l, 48 + d * 8 + cc)),
                              reads=[psb[pb_], consts], writes=[ti_b[p]])
                    P.add("act", ACTF(t_g.s(0, a0, a1), t_a.s(0, a0, a1), AF.Exp, scale=dcol(l, d * 8 + cc), bias=dcol(l, d * 8 + cc)), reads=[ta_b[p], consts], writes=[tg_b[p]])
                    P.add("act", ACTF(t_a.s(0, a0, a1), t_a.s(0, a0, a1), AF.Exp, scale=dcol(l, 16 + d * 8 + cc), bias=dcol(l, 16 + d * 8 + cc)), reads=[consts], writes=[ta_b[p]])
                    P.add("act", ACTF(t_g.s(0, a0, a1), t_g.s(0, a0, a1), AF.Sqrt, scale=-1.0, bias=onecol.ap[:, 0:1]), reads=[consts], writes=[tg_b[p]])
                    P.add("dve", STT(t_i.s(0, a0, a1), t_i.s(0, a0, a1), 1.0, xc.s(0, a0, a1), ALU.add, ALU.mult), reads=[xc_b[p]], writes=[ti_b[p]])
                    P.add("dve", STT(t_i.s(0, a0, a1), t_g.s(0, a0, a1), 0.5, t_i.s(0, a0, a1), ALU.mult, ALU.mult), reads=[tg_b[p]], writes=[ti_b[p]])
                    if d == 0:
                        init = 0.0 if p == 0 else hf.s(0, a0 - 1, a0)
                        rd = [ta_b[p], ti_b[p]] + ([hf_b[p - 1]] if p > 0 else [])
                        P.add("dve", SCAN(hf.s(0, a0, a1), t_a.s(0, a0, a1), t_i.s(0, a0, a1), init), reads=rd, writes=[hf_b[p]])
                    else:
                        init = 0.0 if p == NPC - 1 else hbk.s(0, a1, a1 + 1)
                        rd = [ta_b[p], ti_b[p]] + ([tg_b[p + 1]] if p < NPC - 1 else [])
                        P.add("dve", SCAN(hbk.ap[:, a0:a1][:, ::-1], t_a.ap[:, a0:a1][:, ::-1], t_i.ap[:, a0:a1][:, ::-1], init), reads=rd, writes=[tg_b[p]])
                        P.add("dve", TT(hf.s(0, a0, a1), hf.s(0, a0, a1), hbk.s(0, a0, a1), ALU.add), reads=[tg_b[p]], writes=[hf_b[p]])
                        for (s0_, sn) in chunks(pn, 512):
                            q0 = a0 + s0_
                            pa = 6 + (bk % 2)
                            gs = lg_s[bk % 2]
                            bk += 1
                            hci = q0 // 512
                            hcs = sorted({q0 // 512, (q0 + sn - 1) // 512})
                            for k in range(KC):
                                P.add("pe", MM(ps[pa][:, 0:sn], wlg.s(k), hT.s(k, q0, q0 + sn), k == 0, k == KC - 1), reads=[wlg.b[0]] + [hTb[c_] for c_ in hcs], writes=[psb[pa]])
                            P.add("act", ACTF(gs.s(0, 0, sn), ps[pa][:, 0:sn], AF.Square, scale=GELU_SQA), reads=[psb[pa]], writes=[gs.b[0]])
                            P.add("dve", STT(gs.s(0, 0, sn), gs.s(0, 0, sn), 1.0, ps[pa][:, 0:sn], ALU.add, ALU.mult), reads=[psb[pa]], writes=[gs.b[0]])
                            P.add("act", ACTF(gs.s(0, 0, sn), gs.s(0, 0, sn), AF.Tanh, scale=GELU_C), reads=[], writes=[gs.b[0]])
                            P.add("dve", STT(gs.s(0, 0, sn), gs.s(0, 0, sn), 1.0, ps[pa][:, 0:sn], ALU.add, ALU.mult), reads=[psb[pa]], writes=[gs.b[0]])
                            P.add("dve", STT(ylo.s(0, q0, q0 + sn), gs.s(0, 0, sn), 0.5, hf.s(0, q0, q0 + sn), ALU.mult, ALU.mult), reads=[gs.b[0], hf_b[p]], writes=[ylo.b[0]])
            P.add("sp", DMA(ylin_d[cc * 128:(cc + 1) * 128, :], ylo.s(0)), reads=[ylo.b[0]], dma=True)
            if dbg and l == 0:
                P.add("sp", DMA(dbg_outs["d_ylin"][cc * 128:(cc + 1) * 128, :], ylo.s(0)), reads=[ylo.b[0]], dma=True)
        P.barrier()
        AR.reset(LAYER)

        wol = Tl(AR, KC, 1024, BF16, "wol")
        wgl = Tl(AR, KC, 1024, BF16, "wgl")
        wot = Tl(AR, KC, 1024, BF16, "wot")
        wgm = Tl(AR, KC, 1024, BF16, "wgm")
        wD = Buf("wD")
        P.add("sp", DMA(wgm.v3(), b_w_in[l][:, C_GM:C_GM + D].rearrange("(k p) n -> p k n", p=128)), reads=WC[("inGM", l)], writes=[wD], dma=True)
        P.add("sp", DMA(wol.v3(), b_w_o_lru[l].rearrange("(k p) n -> p k n", p=128)), reads=WC[("ol", l)], writes=[wD], dma=True)
        P.add("sp", DMA(wgl.v3(), b_w_in[l][:, C_GL:C_GL + D].rearrange("(k p) n -> p k n", p=128)), reads=WC[("inGL", l)], writes=[wD], dma=True)
        P.add("sp", DMA(wot.v3(), b_w_out[l].rearrange("(k p) n -> p k n", p=128)), reads=WC[("out", l)], writes=[wD], dma=True)
        ND = 256
        DCH = chunks(T, ND)
        NDC = len(DCH)
        yl_t = [Tl(AR, KC, ND, BF16, f"yl{i}") for i in range(2)]
        zm_t = [Tl(AR, KC, ND, BF16, f"zmi{i}") for i in range(2)]
        hr_t = [Tl(AR, KC, ND, F32, f"hr{i}") for i in range(3)]
        hbo = Tl(AR, KC, ND, BF16, "hbD")
        thl = [Tl(AR, 1, ND, F32, f"thl{i}") for i in range(2)]
        zb = [Tl(AR, 1, ND, F32, f"zb{i}") for i in range(2)]
        thm2 = [Tl(AR, 1, ND, F32, f"thm2{i}") for i in range(2)]
        zb2 = [Tl(AR, 1, ND, F32, f"zb2{i}") for i in range(2)]
        zT_t = [Tl(AR, KC, ND, BF16, f"zT{i}", perk=True) for i in range(2)]
        ybfD = Tl(AR, KC, ND, BF16, "ybfD")
        ysqD = Tl(AR, KC, ND, BF16, "ysqD")
        stD = Tl(AR, 4, ND, F32, "stD")
        ln_state = {}

        def ph1(ci):
            t0, n = DCH[ci]
            hci = t0 // 512
            yl = yl_t[ci % 2]
            zmi = zm_t[ci % 2]
            hr = hr_t[ci % 3]
            zT = zT_t[ci % 2]
            P.add("sp", DMA(yl.v3(n), fm(ylin_d)[:, :, t0:t0 + n]), writes=[yl.b[0]], dma=True)
            P.add("sp", DMA(zmi.v3(n), fm(zm_d)[:, :, t0:t0 + n]), writes=[zmi.b[0]], dma=True)
            P.add("sp", DMA(hr.v3(n), fm(hres_d)[:, :, t0:t0 + n]), writes=[hr.b[0]], dma=True)
            for m in range(KC):
                pa, pb_, pc = (m % 2), 2 + (m % 2), 4 + (m % 2)
                for k in range(KC):
                    P.add("pe", MM(ps[pa][:, 0:n], wol.s(k, m * 128, (m + 1) * 128), yl.s(k, 0, n), k == 0, k == KC - 1), reads=[wD, yl.b[0]], writes=[psb[pa]])
                for k in range(KC):
                    P.add("pe", MM(ps[pb_][:, 0:n], wgl.s(k, m * 128, (m + 1) * 128), hT.s(k, t0, t0 + n), k == 0, k == KC - 1), reads=[wD, hTb[hci]], writes=[psb[pb_]])
                for k in range(KC):
                    P.add("pe", MM(ps[pc][:, 0:n], wgm.s(k, m * 128, (m + 1) * 128), hT.s(k, t0, t0 + n), k == 0, k == KC - 1), reads=[wD, hTb[hci]], writes=[psb[pc]])
                th = thl[m % 2]
                th2 = thm2[m % 2]
                z = zb[m % 2]
                z2 = zb2[m % 2]
                P.add("act", ACTF(th.s(0, 0, n), ps[pb_][:, 0:n], AF.Tanh, scale=0.5), reads=[psb[pb_]], writes=[th.b[0]])
                P.add("act", ACTF(th2.s(0, 0, n), ps[pc][:, 0:n], AF.Tanh, scale=0.5), reads=[psb[pc]], writes=[th2.b[0]])
                P.add("dve", STT(z.s(0, 0, n), th.s(0, 0, n), 1.0, ps[pa][:, 0:n], ALU.add, ALU.mult), reads=[th.b[0], psb[pa]], writes=[z.b[0]])
                P.add("dve", STT(z2.s(0, 0, n), th2.s(0, 0, n), 1.0, zmi.s(m, 0, n), ALU.add, ALU.mult), reads=[th2.b[0], zmi.b[0]], writes=[z2.b[0]])
                P.add("dve", STT(zT.s(m, 0, n), z.s(0, 0, n), 0.5, z2.s(0, 0, n), ALU.mult, ALU.add), reads=[z.b[0], z2.b[0]], writes=[zT.b[m]])

        def ph2(ci):
            t0, n = DCH[ci]
            hr = hr_t[ci % 3]
            zT = zT_t[ci % 2]
            for m in range(KC):
                for k in range(KC):
                    P.add("pe", MM(ps[6][:, 0:n], wot.s(k, m * 128, (m + 1) * 128), zT.s(k, 0, n), k == 0, k == KC - 1), reads=[wD, zT.b[k]], writes=[psb[6]])
                P.add("dve", STT(hr.s(m, 0, n), hr.s(m, 0, n), DN_ALPHA, ps[6][:, 0:n], ALU.mult, ALU.add), reads=[psb[6]], writes=[hr.b[0]])
            pre, pe_, post = ln_parts(hr, n, f"l1g{l}", f"l1b{l}", ybfD, ysqD, stD, (ps[7][:, 0:n], psb[7]), (ps[7][:, 256:256 + n], psb[7]), hbo)
            pre()
            ln_state[ci] = (pe_, post)

        def ph3(ci):
            t0, n = DCH[ci]
            hr = hr_t[ci % 3]
            pe_, post = ln_state.pop(ci)
            pe_()
            post()
            P.add("sp", DMA(fm(hres_d)[:, :, t0:t0 + n], hr.v3(n)), reads=[hr.b[0]], dma=True)
            P.add("sp", DMA(fm(hbfB_d)[:, :, t0:t0 + n], hbo.v3(n)), reads=[hbo.b[0]], dma=True)
            if dbg and l == 0:
                P.add("sp", DMA(fm(dbg_outs["d_hA1"])[:, :, t0:t0 + n], hr.v3(n)), reads=[hr.b[0]], dma=True)

        ph1(0)
        if NDC > 1:
            ph1(1)
        ph2(0)
        for ci in range(2, NDC):
            ph1(ci)
            ph3(ci - 2)
            ph2(ci - 1)
        if NDC > 1:
            ph3(NDC - 2)
            ph2(NDC - 1)
        ph3(NDC - 1)
        P.barrier()
        AR.reset(PERSIST)

        wdn = Tl(AR, NFP, 1024, BF16, "wdn")
        wE = Buf("wE")
        for j0, jn in chunks(NFP, 8):
            P.add("sp", DMA(wdn.v3()[:, j0:j0 + jn, :], b_w_down[l][j0 * 128:(j0 + jn) * 128, :].rearrange("(k p) n -> p k n", p=128)),
                  reads=WC[("dn", l)], writes=[wE], dma=True)
        dg = Tl(AR, 4 * NFC, 128, BF16, "dg")
        dgb = Buf("dg")
        id3 = identF.ap.unsqueeze(1).broadcast_to([128, NFC, 128])
        for tap in range(4):
            if tap < 3:
                o_, _ = COL_LAY[f"fcw{l}"]
                o_ += tap * NFC
            else:
                o_, _ = COL_LAY[f"fcb{l}"]
            c3 = colsT.ap[:, o_:o_ + NFC].unsqueeze(2).broadcast_to([128, NFC, 128])
            P.add("dve", TT(dg.v3()[:, tap * NFC:(tap + 1) * NFC, :], id3, c3, ALU.mult), reads=[consts], writes=[dgb])
        onesN = Tl(AR, 1, 512, BF16, "onesN")
        P.add("pool", MSET(onesN.ap, 1.0), writes=[dgb])
        NWB = 3
        wupg = [Tl(AR, KC, 128, BF16, f"wupg{i}") for i in range(NWB)]
        wupv = [Tl(AR, KC, 128, BF16, f"wupv{i}") for i in range(NWB)]
        hin = [Tl(AR, KC, 512, BF16, f"hin{i}") for i in range(2)]
        gact = Tl(AR, NFP, 510, BF16, "gact", perk=True)
        ug = [Tl(AR, 1, 512, BF16, f"ug{i}") for i in range(2)]
        uv = [Tl(AR, 1, 512, BF16, f"uv{i}") for i in range(2)]
        gq = [Tl(AR, 1, 512, F32, f"gq{i}") for i in range(2)]
        hrE = Tl(AR, KC, 512, F32, "hrE")
        hbE = Tl(AR, KC, 512, BF16, "hbE")
        ybfE = Tl(AR, KC, 512, BF16, "ybfE")
        ysqE = Tl(AR, KC, 512, BF16, "ysqE")
        stE = Tl(AR, 4, 512, F32, "stE")
        wi = 0
        deferred = []

        def emit_conv(ci, j, t0, n, off, nn):
            for (src, mcol, pb) in ((ug[j % 2], j, 4), (uv[j % 2], NFP + j, 5)):
                a = 1 if t0 == 0 else 0
                bnd = n - 1 if t0 + n == T else n
                P.add("pe", MM(ps[pb][:, 0:n], dg.s(1 * NFC + mcol), src.s(0, off, off + n), True, False), reads=[dgb, src.b[0]], writes=[psb[pb]])
                P.add("pe", MM(ps[pb][:, a:n], dg.s(0 * NFC + mcol), src.s(0, off + a - 1, off + n - 1), False, False), reads=[dgb, src.b[0]], writes=[psb[pb]])
                P.add("pe", MM(ps[pb][:, 0:bnd], dg.s(2 * NFC + mcol), src.s(0, off + 1, off + 1 + bnd), False, False), reads=[dgb, src.b[0]], writes=[psb[pb]])
                P.add("pe", MM(ps[pb][:, 0:n], dg.s(3 * NFC + mcol), onesN.s(0, 0, n), False, True), reads=[dgb], writes=[psb[pb]])
            q_ = gq[j % 2]
            P.add("act", ACTF(q_.s(0, 0, n), ps[4][:, 0:n], AF.Square, scale=GELU_SQA), reads=[psb[4]], writes=[q_.b[0]])
            P.add("dve", STT(q_.s(0, 0, n), q_.s(0, 0, n), 1.0, ps[4][:, 0:n], ALU.add, ALU.mult), reads=[psb[4]], writes=[q_.b[0]])
            P.add("act", ACTF(q_.s(0, 0, n), q_.s(0, 0, n), AF.Tanh, scale=GELU_C), reads=[], writes=[q_.b[0]])
            P.add("dve", STT(q_.s(0, 0, n), q_.s(0, 0, n), 1.0, ps[4][:, 0:n], ALU.add, ALU.mult), reads=[psb[4]], writes=[q_.b[0]])
            P.add("dve", STT(gact.s(j, 0, n), q_.s(0, 0, n), 0.5, ps[5][:, 0:n], ALU.mult, ALU.mult), reads=[q_.b[0], psb[5]], writes=[gact.b[j]])

        for ci, (t0, n) in enumerate(FCH):
            lo = max(t0 - 1, 0)
            hi = min(t0 + n + 1, T)
            nn = hi - lo
            off = t0 - lo
            hi_t = hin[ci % 2]
            P.add("sp", DMA(hi_t.v3(nn), fm(hbfB_d)[:, :, lo:hi]), writes=[hi_t.b[0]], dma=True)
            prev = None
            for j in range(NFP):
                wg = wupg[wi % NWB]
                wv = wupv[wi % NWB]
                wi += 1
                P.add("sp", DMA(wg.ap, b_w_up[l][j]), reads=WC[("up", l, j)], writes=[wg.b[0]], dma=True)
                P.add("sp", DMA(wv.ap, b_w_up[l][NFP + j]), reads=WC[("up", l, NFP + j)], writes=[wv.b[0]], dma=True)
                if j == 8:
                    P.add("sp", DMA(hrE.v3(n), fm(hres_d)[:, :, t0:t0 + n]), writes=[hrE.b[0]], dma=True)
                pg, pv = (j % 2), 2 + (j % 2)
                for k in range(KC):
                    P.add("pe", MM(ps[pg][:, 0:nn], wg.s(k), hi_t.s(k, 0, nn), k == 0, k == KC - 1), reads=[wg.b[0], hi_t.b[0]], writes=[psb[pg]])
                for k in range(KC):
                    P.add("pe", MM(ps[pv][:, 0:nn], wv.s(k), hi_t.s(k, 0, nn), k == 0, k == KC - 1), reads=[wv.b[0], hi_t.b[0]], writes=[psb[pv]])
                P.add("act", ACTF(ug[j % 2].s(0, 0, nn), ps[pg][:, 0:nn], AF.Identity), reads=[psb[pg]], writes=[ug[j % 2].b[0]])
                P.add("act", ACTF(uv[j % 2].s(0, 0, nn), ps[pv][:, 0:nn], AF.Identity), reads=[psb[pv]], writes=[uv[j % 2].b[0]])
                if j == 3 and deferred:
                    for f in deferred:
                        f()
                    deferred = []
                if prev is not None:
                    emit_conv(ci, *prev)
                prev = (j, t0, n, off, nn)
            emit_conv(ci, *prev)
            for m in range(KC):
                pa = 6 + (m % 2)
                for j in range(NFP):
                    P.add("pe", MM(ps[pa][:, 0:n], wdn.s(j, m * 128, (m + 1) * 128), gact.s(j, 0, n), j == 0, j == NFP - 1), reads=[wE, gact.b[j]], writes=[psb[pa]])
                P.add("dve", STT(hrE.s(m, 0, n), hrE.s(m, 0, n), DN_ALPHA, ps[pa][:, 0:n], ALU.mult, ALU.add), reads=[psb[pa]], writes=[hrE.b[0]])
            pre, pe_, post = ln_parts(hrE, n, f"l2g{l}", f"l2b{l}", ybfE, ysqE, stE, 6, 7, None if last else hbE)
            pre()

            def outs(t0=t0, n=n):
                if not last:
                    P.add("sp", DMA(fm(hres_d)[:, :, t0:t0 + n], hrE.v3(n)), reads=[hrE.b[0]], dma=True)
                    P.add("sp", DMA(fm(hbfA_d)[:, :, t0:t0 + n], hbE.v3(n)), reads=[hbE.b[0]], dma=True)
                    if dbg:
                        P.add("sp", DMA(fm(dbg_outs["d_hE1"])[:, :, t0:t0 + n], hrE.v3(n)), reads=[hrE.b[0]], dma=True)
                else:
                    a = max(NMETA - t0, 0)
                    if a < n:
                        P.add("sp", DMA(fm(outT)[:, :, t0 + a - NMETA:t0 + n - NMETA], hrE.v3()[:, :, a:n]), reads=[hrE.b[0]], dma=True)

            deferred = [pe_, post, outs]
        for f in deferred:
            f()
        deferred = []
        P.barrier()
        AR.reset(PERSIST)

    with nc.Block() as block:
        P.finalize_and_emit(block)
    return nc


def _colpack(v):
    v = np.asarray(v, np.float32).reshape(-1)
    return np.ascontiguousarray(v.reshape(-1, 128).T)


def pack_cols(inp):
    cols = np.zeros((128, NCOL), np.float32)

    def put(name, arr):
        o, n = COL_LAY[name]
        a = _colpack(arr)
        assert a.shape[1] == n, (name, a.shape, n)
        cols[:, o:o + n] = a

    put("ln0_g", inp["ln0_g"])
    put("ln0_b", inp["ln0_b"])
    for l in range(L):
        put(f"qn{l}", inp["q_norm"][l])
        put(f"kvn{l}", inp["kv_norm"][l])
        put(f"lcw{l}", inp["lru_conv_w"][l])
        put(f"lcb{l}", inp["lru_conv_b"][l])
        put(f"brg{l}", inp["b_rg"][l])
        put(f"big{l}", inp["b_ig"][l])
        put(f"lam{l}", inp["lru_lambda"][l])
        put(f"l1g{l}", inp["ln1_g"][l])
        put(f"l1b{l}", inp["ln1_b"][l])
        put(f"fcw{l}", inp["ffn_conv_w"][l])
        put(f"fcb{l}", inp["ffn_conv_b"][l])
        put(f"l2g{l}", inp["ln2_g"][l])
        put(f"l2b{l}", inp["ln2_b"][l])
    return cols


def rope_tables(T):
    half = 32
    inv_freq = np.exp(-math.log(10000.0) * np.arange(half, dtype=np.float32) / half).astype(np.float32)
    ang = np.arange(T, dtype=np.float32)[:, None] * inv_freq[None, :]
    cos = np.cos(ang).astype(np.float32).T
    sin = np.sin(ang).astype(np.float32).T
    return np.ascontiguousarray(np.stack([np.concatenate([cos, cos], 0), np.concatenate([sin, sin], 0)], 0))


def make_in_maps(inp, nb, SEQ):
    f = lambda a: np.ascontiguousarray(np.asarray(a, np.float32))
    T = SEQ + NMETA
    shared = {
        "metaT": f(np.asarray(inp["meta_tokens"]).T),
        "cols": pack_cols({k: np.asarray(v) for k, v in inp.items()}),
        "cs2": rope_tables(T),
        "ident": np.eye(128, dtype=np.float32),
        "w_in": f(inp["w_in"]),
        "w_uq": f(np.asarray(inp["w_uq"]).reshape(L, 256, 1536)),
        "w_uk": f(np.asarray(inp["w_uk"]).reshape(L, 128, 1024)),
        "w_uv": f(np.asarray(inp["w_uv"]).reshape(L, 128, 1024)),
        "w_o_mla": f(inp["w_o_mla"]),
        "w_rg": f(np.asarray(inp["w_rg"]).reshape(L, 2048, 128)),
        "w_ig": f(np.asarray(inp["w_ig"]).reshape(L, 2048, 128)),
        "w_o_lru": f(inp["w_o_lru"]),
        "w_out": f(inp["w_out"]),
        "w_up": f(inp["w_up"]),
        "w_down": f(inp["w_down"]),
    }
    x = np.asarray(inp["x"], np.float32)
    maps = []
    for b in range(nb):
        m = dict(shared)
        m["xT"] = np.ascontiguousarray(x[b].T)
        maps.append(m)
    return maps


_NC_CACHE = {}


def kernel(**inputs):
    x = np.asarray(inputs["x"])
    B, SEQ, _ = x.shape
    if SEQ not in _NC_CACHE:
        _NC_CACHE[SEQ] = build(SEQ)
    nc = _NC_CACHE[SEQ]
    maps = make_in_maps(inputs, B, SEQ)
    res = run_bass_kernel_spmd(nc, maps, core_ids=list(range(B)))
    out = np.stack([np.asarray(r["outT"]).T for r in res.results], 0)
    return np.ascontiguousarray(out.astype(np.float32))
```

```python
import math
import numpy as np
import concourse.bass as bass
import concourse.mybir as mybir
from concourse.bass_utils import run_bass_kernel_spmd

F32 = mybir.dt.float32
BF16 = mybir.dt.bfloat16
AF = mybir.ActivationFunctionType
ALU = mybir.AluOpType

D = 1024
KC = 8
NMETA = 16
H = 8
L = 2
DFF = 2816
NFC = 44
NFP = 22
INCOLS = 4544
C_CQ, C_CKV, C_KR, C_LG, C_LX, C_GM, C_GL = 0, 256, 384, 448, 1472, 2496, 3520
DN_ALPHA = (2.0 * L) ** 0.25
LN_EPS = 1e-5
RMS_EPS = 1e-6
ATT_SCALE = 1.0 / math.sqrt(192.0)
GELU_C = math.sqrt(2.0 / math.pi)
GELU_SQA = math.sqrt(0.044715)

SAME_ENGINE_SYNC = True
EPOCH = 16000
N_DMA_SEM = {'sp': 20, 'pool': 8, 'act': 4, 'pe': 1, 'dve': 1}
DMA_SEM_MAX_USES = 900


class Buf:
    __slots__ = ("name", "w", "r")

    def __init__(self, name=""):
        self.name = name
        self.w = None
        self.r = {}


class Op:
    __slots__ = ("eng", "fn", "deps", "dma", "needed", "ev", "seq", "prev_on_sem")

    def __init__(self, eng, fn, dma):
        self.eng = eng
        self.fn = fn
        self.dma = dma
        self.deps = []
        self.needed = bool(dma)
        self.ev = None
        self.seq = -1
        self.prev_on_sem = None


class Prog:
    ENGS = ("pe", "act", "dve", "pool", "sp")

    def __init__(self, nc):
        self.nc = nc
        self.ops = {e: [] for e in self.ENGS}
        self.known = {e: {f: -1 for f in self.ENGS} for e in self.ENGS}
        self.known_dma = {e: set() for e in self.ENGS}
        self.pending_dma = []
        self.last_real = {e: None for e in self.ENGS}

    def add(self, eng, fn, reads=(), writes=(), dma=False, extra_deps=(), force_same=False, track=True):
        op = Op(eng, fn, dma)
        op.seq = len(self.ops[eng])
        deps = {}
        for b in reads:
            if b.w is not None:
                deps[id(b.w)] = b.w
        for b in writes:
            if b.w is not None:
                deps[id(b.w)] = b.w
            for r in b.r.values():
                deps[id(r)] = r
        for d in extra_deps:
            if d is not None:
                deps[id(d)] = d
        kn = self.known[eng]
        for d in deps.values():
            if d is op:
                continue
            if d.dma:
                if id(d) in self.known_dma[eng]:
                    continue
                self.known_dma[eng].add(id(d))
                op.deps.append(d)
            else:
                if d.eng == eng and not dma and not force_same:
                    if d.eng == "pe" or not SAME_ENGINE_SYNC:
                        continue
                if d.seq <= kn[d.eng]:
                    continue
                kn[d.eng] = d.seq
                d.needed = True
                op.deps.append(d)
        for b in reads:
            key = ("dma", id(op)) if dma else eng
            b.r[key] = op
        for b in writes:
            b.w = op
            b.r = {}
        self.ops[eng].append(op)
        if dma:
            if track:
                self.pending_dma.append(op)
        else:
            self.last_real[eng] = op
        return op

    def barrier(self):
        pend = list(self.pending_dma)
        self.pending_dma = []
        lasts = [self.last_real[e] for e in self.ENGS]
        spop = self.add("sp", lambda e: e.nop(), extra_deps=pend + lasts, force_same=True)
        for e in ("pe", "act", "dve", "pool"):
            self.add(e, lambda g: g.nop(), extra_deps=[spop])

    def finalize_and_emit(self, block):
        nc = self.nc
        eng_sems = {e: [] for e in self.ENGS}
        for e in self.ENGS:
            c = 0
            for op in self.ops[e]:
                if not op.needed or op.dma:
                    continue
                c += 1
                ep = (c - 1) // EPOCH
                while len(eng_sems[e]) <= ep:
                    eng_sems[e].append(nc.alloc_semaphore(f"s_{e}_{len(eng_sems[e])}"))
                op.ev = (eng_sems[e][ep], (c - 1) % EPOCH + 1, 1)
        for e in self.ENGS:
            dma_sems = []
            dma_state = []
            dma_rr = 0
            for op in self.ops[e]:
                if not op.dma:
                    continue
                if len(dma_sems) < N_DMA_SEM[e]:
                    dma_sems.append(nc.alloc_semaphore(f"s_dma_{e}_{len(dma_sems)}"))
                    dma_state.append([0, None])
                k = dma_rr % len(dma_sems)
                if dma_state[k][0] >= DMA_SEM_MAX_USES:
                    dma_sems[k] = nc.alloc_semaphore(f"s_dma_{e}_{k}_{dma_rr}")
                    dma_state[k] = [0, None]
                dma_rr += 1
                dma_state[k][0] += 1
                op.prev_on_sem = dma_state[k][1]
                dma_state[k][1] = op
                op.ev = (dma_sems[k], 16 * dma_state[k][0], 16)

        def emit(e, engobj):
            waited = {}

            def w(ev):
                sem, val, _ = ev
                key = id(sem)
                if waited.get(key, 0) >= val:
                    return
                waited[key] = val
                engobj.wait_ge(sem, val)

            for op in self.ops[e]:
                for d in op.deps:
                    w(d.ev)
                if op.prev_on_sem is not None:
                    w(op.prev_on_sem.ev)
                ins = op.fn(engobj)
                if op.ev is not None:
                    ins.then_inc(op.ev[0], op.ev[2])

        block.sync(lambda eng: emit("sp", eng))
        block.tensor(lambda eng: emit("pe", eng))
        block.scalar(lambda eng: emit("act", eng))
        block.vector(lambda eng: emit("dve", eng))
        block.gpsimd(lambda eng: emit("pool", eng))


def MM(out, lhsT, rhs, start, stop):
    return lambda e: e.matmul(out, lhsT, rhs, start=start, stop=stop)


def TR(out, in_, ident):
    return lambda e: e.transpose(out, in_, ident)


def ACTF(out, in_, func, scale=None, bias=None):
    kw = {}
    if scale is not None:
        kw["scale"] = scale
    if bias is not None:
        kw["bias"] = bias
    return lambda e: e.activation(out, in_, func, **kw)


def TT(out, a, b, op):
    return lambda e: e.tensor_tensor(out, a, b, op)


def TS(out, a, s1, s2, op0, op1):
    return lambda e: e.tensor_scalar(out, a, s1, s2, op0, op1)


def TS1(out, a, s1, op0):
    return lambda e: e.tensor_scalar(out, a, s1, None, op0)


def STT(out, a, s, b, op0, op1):
    return lambda e: e.scalar_tensor_tensor(out, a, s, b, op0, op1)


def CP(out, in_):
    return lambda e: e.tensor_copy(out, in_)


def RCP(out, in_):
    return lambda e: e.reciprocal(out, in_)


def MSET(ap, v):
    return lambda e: e.memset(ap, v)


def DMA(out, in_):
    return lambda e: e.dma_start(out=out, in_=in_)


def SCAN(out, a, u, init):
    return lambda e: e.tensor_tensor_scan(out, a, u, init, ALU.mult, ALU.add)


def chunks(total, size):
    out = []
    t = 0
    while t < total:
        n = min(size, total - t)
        out.append((t, n))
        t += n
    return out


def col_layout():
    lay = {}
    off = 0

    def put(name, n):
        nonlocal off
        lay[name] = (off, n)
        off += n

    put("ln0_g", 8)
    put("ln0_b", 8)
    for l in range(L):
        put(f"qn{l}", 2)
        put(f"kvn{l}", 1)
        put(f"lcw{l}", 32)
        put(f"lcb{l}", 8)
        put(f"brg{l}", 16)
        put(f"big{l}", 16)
        put(f"lam{l}", 16)
        put(f"l1g{l}", 8)
        put(f"l1b{l}", 8)
        put(f"fcw{l}", 3 * NFC)
        put(f"fcb{l}", NFC)
        put(f"l2g{l}", 8)
        put(f"l2b{l}", 8)
    return lay, off


COL_LAY, NCOL = col_layout()


class Arena:
    def __init__(self, nc, units):
        self.t = nc.alloc_sbuf_tensor("arena", [128, units], BF16)
        self.units = units
        self.off = 0

    def mark(self):
        return self.off

    def reset(self, m):
        self.off = m

    def alloc(self, cols, dt):
        u = cols * (2 if dt == F32 else 1)
        u = (u + 1) // 2 * 2
        a = self.off
        self.off += u
        assert self.off <= self.units, f"SBUF arena overflow {self.off} > {self.units}"
        ap = self.t[:, a:a + u]
        if dt == F32:
            ap = ap.bitcast(F32)
        return ap


class Tl:
    def __init__(self, arena, K, n, dt, name="", perk=False):
        self.ap = arena.alloc(K * n, dt)
        self.K, self.n = K, n
        self.b = [Buf(f"{name}{k}") for k in range(K)] if perk else [Buf(name)] * K

    def s(self, k=0, a=0, b=None, p0=0, p1=128):
        if b is None:
            b = self.n
        return self.ap[p0:p1, k * self.n + a:k * self.n + b]

    def flat(self, a=0, b=None, p0=0, p1=128):
        if b is None:
            b = self.K * self.n
        return self.ap[p0:p1, a:b]

    def v3(self, n=None):
        if n is None or n == self.n:
            return self.ap.rearrange("p (k n) -> p k n", k=self.K)
        return self.ap.rearrange("p (k n) -> p k n", k=self.K)[:, :, 0:n]


def build(SEQ, dbg=False):
    T = SEQ + NMETA
    TCH = chunks(T, 512)
    KT = chunks(T, 128)
    FCH = chunks(T, 510)
    nc = bass.Bass("TRN2", target_bir_lowering=False)
    P = Prog(nc)

    def din(name, shape, dt=F32):
        return nc.dram_tensor(name, shape, dt, kind="ExternalInput").ap()

    def dscr(name, shape, dt):
        return nc.dram_tensor(name, shape, dt).ap()

    xT = din("xT", [D, SEQ])
    metaT = din("metaT", [D, NMETA])
    cols_d = din("cols", [128, NCOL])
    cs2_d = din("cs2", [2, 64, T])
    ident_d = din("ident", [128, 128])
    w_in = din("w_in", [L, D, INCOLS])
    w_uq = din("w_uq", [L, 256, 1536])
    w_uk = din("w_uk", [L, 128, 1024])
    w_uv = din("w_uv", [L, 128, 1024])
    w_o_mla = din("w_o_mla", [L, D, D])
    w_rg = din("w_rg", [L, 2048, 128])
    w_ig = din("w_ig", [L, 2048, 128])
    w_o_lru = din("w_o_lru", [L, D, D])
    w_out = din("w_out", [L, D, D])
    w_up = din("w_up", [L, D, 2 * DFF])
    w_down = din("w_down", [L, DFF, D])
    outT = nc.dram_tensor("outT", [D, SEQ], F32, kind="ExternalOutput").ap()

    b_w_in = dscr("b_w_in", [L, D, INCOLS], BF16)
    b_w_uq = dscr("b_w_uq", [L, 256, 1536], BF16)
    b_w_uv = dscr("b_w_uv", [L, 128, 1024], BF16)
    b_w_o_mla = dscr("b_w_o_mla", [L, D, D], BF16)
    b_w_rg = dscr("b_w_rg", [L, 2048, 128], BF16)
    b_w_ig = dscr("b_w_ig", [L, 2048, 128], BF16)
    b_w_o_lru = dscr("b_w_o_lru", [L, D, D], BF16)
    b_w_out = dscr("b_w_out", [L, D, D], BF16)
    b_w_up = dscr("b_w_up", [L, NFC, 128, KC * 128], BF16)
    b_w_down = dscr("b_w_down", [L, DFF, D], BF16)
    hres_d = dscr("hres", [D, T], F32)
    hbfA_d = dscr("hbfA", [D, T], BF16)
    hbfB_d = dscr("hbfB", [D, T], BF16)
    zm_d = dscr("zm", [D, T], BF16)
    ylin_d = dscr("ylin", [D, T], BF16)

    def fm(ap):
        return ap.rearrange("(k p) t -> p k t", p=128)

    dbg_outs = {}
    if dbg:
        for nm in ["d_h0", "d_hA1", "d_hE1"]:
            dbg_outs[nm] = nc.dram_tensor(nm, [D, T], F32, kind="ExternalOutput").ap()
        for nm in ["d_zm", "d_ylin"]:
            dbg_outs[nm] = nc.dram_tensor(nm, [D, T], BF16, kind="ExternalOutput").ap()

    WC = {}
    cast_after = []

    def cast2d(key, dst, src, R, C):
        for r0, rn in chunks(R, 1024):
            for c0, cn in chunks(C, 2048):
                b = Buf(key)
                WC.setdefault(key, []).append(b)
                P.add("pool", DMA(dst[r0:r0 + rn, c0:c0 + cn], src[r0:r0 + rn, c0:c0 + cn]), reads=list(cast_after), writes=[b], dma=True, track=False)

    def cast_cols(key, l, c0, c1):
        b = Buf("in")
        WC.setdefault(key, []).append(b)
        P.add("pool", DMA(b_w_in[l][:, c0:c1], w_in[l][:, c0:c1]), reads=list(cast_after), writes=[b], dma=True, track=False)

    def cast_layer_urgent(l):
        cast_cols(("inA", l), l, 0, 448)
        cast2d(("uq", l), b_w_uq[l], w_uq[l], 256, 1536)
        cast2d(("uv", l), b_w_uv[l], w_uv[l], 128, 1024)
        cast2d(("om", l), b_w_o_mla[l], w_o_mla[l], D, D)

    def cast_layer_small(l):
        cast_cols(("inLX", l), l, C_LX, C_LX + 1024)
        cast_cols(("inLG", l), l, C_LG, C_LG + 1024)
        cast_cols(("inGM", l), l, C_GM, C_GM + 1024)
        cast_cols(("inGL", l), l, C_GL, C_GL + 1024)
        cast2d(("rg", l), b_w_rg[l], w_rg[l], 2048, 128)
        cast2d(("ig", l), b_w_ig[l], w_ig[l], 2048, 128)
        cast2d(("ol", l), b_w_o_lru[l], w_o_lru[l], D, D)
        cast2d(("out", l), b_w_out[l], w_out[l], D, D)

    def cast_layer_ffn(l):
        cast2d(("dn", l), b_w_down[l], w_down[l], DFF, D)
        for m in range(NFC):
            for k0 in range(0, KC, 4):
                src = w_up[l][k0 * 128:(k0 + 4) * 128, m * 128:(m + 1) * 128].rearrange("(k p) c -> p k c", p=128)
                dst = b_w_up[l][m][:, k0 * 128:(k0 + 4) * 128].rearrange("p (k c) -> p k c", c=128)
                b = Buf("up")
                WC.setdefault(("up", l, m), []).append(b)
                P.add("pool", DMA(dst, src), reads=list(cast_after), writes=[b], dma=True, track=False)

    cast_layer_urgent(0)

    AR = Arena(nc, 106400)
    psW = [nc.alloc_psum_tensor(f"psw{i}", [128, 1024], F32) for i in range(4)]
    ps = [psW[i // 2][:, (i % 2) * 512:(i % 2 + 1) * 512] for i in range(8)]
    psb = [Buf(f"ps{i}") for i in range(8)]

    colsT = Tl(AR, 1, NCOL, F32, "cols")
    dcols = Tl(AR, 1, L * 64, F32, "dcols")
    identF = Tl(AR, 1, 128, F32, "ident")
    onesD = Tl(AR, 1, 128, BF16, "onesD")
    ones256 = Tl(AR, 1, 128, BF16, "ones256")
    ones128 = Tl(AR, 1, 128, BF16, "ones128")
    ones1 = Tl(AR, 1, 128, BF16, "ones1")
    consts = Buf("consts")
    onecol = Tl(AR, 1, 2, F32, "onecol")

    def col(name, i=0, p0=0, p1=128):
        o, n = COL_LAY[name]
        return colsT.ap[p0:p1, o + i:o + i + 1]

    def dcol(l, i, p0=0, p1=128):
        return dcols.ap[p0:p1, l * 64 + i:l * 64 + i + 1]

    P.add("sp", DMA(colsT.ap, cols_d), writes=[consts], dma=True)
    P.add("sp", DMA(identF.ap, ident_d), writes=[consts], dma=True)
    P.add("pool", MSET(onesD.ap, 1.0 / 1024), writes=[consts])
    P.add("pool", MSET(ones256.ap, 1.0 / 256), writes=[consts])
    P.add("pool", MSET(ones128.ap, 1.0 / 128), writes=[consts])
    P.add("pool", MSET(ones1.ap, 1.0), writes=[consts])
    P.add("pool", MSET(onecol.ap, 1.0), writes=[consts])
    for l in range(L):
        o, _ = COL_LAY[f"lam{l}"]
        lam = colsT.ap[:, o:o + 16]
        d0 = dcols.ap[:, l * 64:l * 64 + 16]
        d1 = dcols.ap[:, l * 64 + 16:l * 64 + 32]
        P.add("act", ACTF(d0, lam, AF.Exp, scale=-1.0), reads=[consts], writes=[consts])
        P.add("dve", TS1(d0, d0, 1.0, ALU.add), reads=[consts], writes=[consts])
        P.add("act", ACTF(d0, d0, AF.Ln), reads=[consts], writes=[consts])
        P.add("dve", TS1(d1, d0, -4.0, ALU.mult), reads=[consts], writes=[consts])
        P.add("dve", TS1(d0, d0, -8.0, ALU.mult), reads=[consts], writes=[consts])
        o, _ = COL_LAY[f"brg{l}"]
        P.add("dve", TS1(dcols.ap[:, l * 64 + 32:l * 64 + 48], colsT.ap[:, o:o + 16], 0.5, ALU.mult), reads=[consts], writes=[consts])
        o, _ = COL_LAY[f"big{l}"]
        P.add("dve", TS1(dcols.ap[:, l * 64 + 48:l * 64 + 64], colsT.ap[:, o:o + 16], 0.5, ALU.mult), reads=[consts], writes=[consts])

    PERSIST = AR.mark()

    def ln_parts(y, n, gname, bname, ybf, ysq, st, psA, psB, hbf):
        yb = y.b[0]
        if isinstance(psA, int):
            apA, bA, apB, bB = ps[psA][:, 0:n], psb[psA], ps[psB][:, 0:n], psb[psB]
        else:
            (apA, bA), (apB, bB) = psA, psB

        def pre():
            P.add("act", ACTF(ybf.v3(n), y.v3(n), AF.Identity), reads=[yb], writes=[ybf.b[0]])
            P.add("act", ACTF(ysq.v3(n), y.v3(n), AF.Square), reads=[yb], writes=[ysq.b[0]])

        def pe():
            for k in range(KC):
                P.add("pe", MM(apA, onesD.ap, ybf.s(k, 0, n), k == 0, k == KC - 1), reads=[ybf.b[0], consts], writes=[bA])
            for k in range(KC):
                P.add("pe", MM(apB, onesD.ap, ysq.s(k, 0, n), k == 0, k == KC - 1), reads=[ysq.b[0], consts], writes=[bB])

        def post():
            mean, msq, var, rstd = (st.s(i, 0, n) for i in range(4))
            sb = st.b[0]
            P.add("dve", CP(mean, apA), reads=[bA], writes=[sb])
            P.add("dve", TT(msq, mean, mean, ALU.mult), reads=[sb], writes=[sb])
            P.add("dve", STT(var, apB, LN_EPS, msq, ALU.add, ALU.subtract), reads=[bB, sb], writes=[sb])
            P.add("act", ACTF(var, var, AF.Sqrt), reads=[sb], writes=[sb])
            P.add("dve", RCP(rstd, var), reads=[sb], writes=[sb])
            mean3 = mean.unsqueeze(1).broadcast_to([128, KC, n])
            rstd3 = rstd.unsqueeze(1).broadcast_to([128, KC, n])
            P.add("dve", TT(y.v3(n), y.v3(n), mean3, ALU.subtract), reads=[sb], writes=[yb])
            P.add("dve", TT(y.v3(n), y.v3(n), rstd3, ALU.mult), reads=[sb], writes=[yb])
            for k in range(KC):
                if hbf is not None:
                    P.add("act", ACTF(hbf.s(k, 0, n), y.s(k, 0, n), AF.Identity, scale=col(gname, k), bias=col(bname, k)), reads=[consts, yb], writes=[hbf.b[0]])
                P.add("act", ACTF(y.s(k, 0, n), y.s(k, 0, n), AF.Identity, scale=col(gname, k), bias=col(bname, k)), reads=[consts], writes=[yb])

        return pre, pe, post

    def ln_chunk(*a):
        pre, pe, post = ln_parts(*a)
        pre()
        pe()
        post()

    ys = [Tl(AR, KC, 512, F32, f"y{i}") for i in range(2)]
    hb = [Tl(AR, KC, 512, BF16, f"hb{i}") for i in range(2)]
    ybf0 = [Tl(AR, KC, 512, BF16, f"ybf{i}") for i in range(2)]
    ysq0 = [Tl(AR, KC, 512, BF16, f"ysq{i}") for i in range(2)]
    st0 = [Tl(AR, 4, 512, F32, f"st{i}") for i in range(2)]
    s0_parts = {}

    def s0_load_pre(ci):
        t0, n = TCH[ci]
        y = ys[ci % 2]
        if t0 == 0:
            P.add("sp", DMA(y.v3()[:, :, 0:NMETA], fm(metaT)), writes=[y.b[0]], dma=True)
            P.add("sp", DMA(y.v3()[:, :, NMETA:n], fm(xT)[:, :, 0:n - NMETA]), writes=[y.b[0]], dma=True)
        else:
            P.add("sp", DMA(y.v3(n), fm(xT)[:, :, t0 - NMETA:t0 - NMETA + n]), writes=[y.b[0]], dma=True)
        pre, pe_, post = ln_parts(y, n, "ln0_g", "ln0_b", ybf0[ci % 2], ysq0[ci % 2], st0[ci % 2], 2 * (ci % 2), 2 * (ci % 2) + 1, hb[ci % 2])
        pre()
        s0_parts[ci] = (pe_, post)

    def s0_finish(ci):
        t0, n = TCH[ci]
        y = ys[ci % 2]
        hbt = hb[ci % 2]
        pe_, post = s0_parts.pop(ci)
        pe_()
        post()
        P.add("pool", DMA(fm(hres_d)[:, :, t0:t0 + n], y.v3(n)), reads=[y.b[0]], dma=True)
        P.add("pool", DMA(fm(hbfA_d)[:, :, t0:t0 + n], hbt.v3(n)), reads=[hbt.b[0]], dma=True)
        if dbg:
            P.add("pool", DMA(fm(dbg_outs["d_h0"])[:, :, t0:t0 + n], y.v3(n)), reads=[y.b[0]], dma=True)

    s0_load_pre(0)
    for ci in range(1, len(TCH)):
        s0_load_pre(ci)
        s0_finish(ci - 1)
    s0_finish(len(TCH) - 1)
    P.barrier()
    AR.reset(PERSIST)

    for l in range(L):
        last = l == L - 1
        hT = Tl(AR, KC, T, BF16, "hT", perk=False)
        hTb = [Buf(f"hT{c}") for c in range(len(TCH))]
        for ci, (t0, n) in enumerate(TCH):
            P.add("sp", DMA(hT.v3()[:, :, t0:t0 + n], fm(hbfA_d)[:, :, t0:t0 + n]), writes=[hTb[ci]], dma=True)
        LAYER = AR.mark()

        cqn = Tl(AR, 2, T, BF16, "cqn")
        ckvn = Tl(AR, 1, T, BF16, "ckvn")
        krope = Tl(AR, 1, T, BF16, "krope")
        Vt = Tl(AR, len(KT), 128, BF16, "V")
        latb = [Buf(f"lat{c}") for c in range(len(TCH))]
        Vb = Buf("V")
        kpad = Buf("kpad")
        P.add("pool", MSET(krope.s(0, 0, T, 64, 128), 0.0), writes=[kpad])
        STA = AR.mark()
        wA = Tl(AR, KC, 448, BF16, "wA")
        P.add("sp", DMA(wA.v3(), b_w_in[l][:, 0:448].rearrange("(k p) n -> p k n", p=128)), reads=WC[("inA", l)], writes=[wA.b[0]], dma=True)
        cqf = Tl(AR, 2, 512, F32, "cqf")
        cqs = Tl(AR, 2, 512, BF16, "cqs")
        ckf = Tl(AR, 1, 512, F32, "ckf")
        cks = Tl(AR, 1, 512, BF16, "cks")
        ckn = Tl(AR, 1, 512, F32, "ckn")
        krf = Tl(AR, 1, 512, F32, "krf")
        rot = Tl(AR, 1, 512, F32, "rot")
        cst = Tl(AR, 2, 512, F32, "cst")
        rq = Tl(AR, 1, 512, F32, "rq")
        rk = Tl(AR, 1, 512, F32, "rk")
        for ci, (t0, n) in enumerate(TCH):
            P.add("sp", DMA(cst.v3(n)[0:64], cs2_d[:, :, t0:t0 + n].rearrange("a p t -> p a t")), writes=[cst.b[0]], dma=True)
            for m in range(2):
                for k in range(KC):
                    P.add("pe", MM(ps[m][:, 0:n], wA.s(k, m * 128, (m + 1) * 128), hT.s(k, t0, t0 + n), k == 0, k == KC - 1),
                          reads=[wA.b[0], hTb[ci]], writes=[psb[m]])
            for k in range(KC):
                P.add("pe", MM(ps[2][:, 0:n], wA.s(k, 256, 384), hT.s(k, t0, t0 + n), k == 0, k == KC - 1), reads=[wA.b[0], hTb[ci]], writes=[psb[2]])
            for k in range(KC):
                P.add("pe", MM(ps[3][0:64, 0:n], wA.s(k, 384, 448), hT.s(k, t0, t0 + n), k == 0, k == KC - 1), reads=[wA.b[0], hTb[ci]], writes=[psb[3]])
            for m in range(2):
                P.add("act", ACTF(cqf.s(m, 0, n), ps[m][:, 0:n], AF.Identity), reads=[psb[m]], writes=[cqf.b[0]])
                P.add("act", ACTF(cqs.s(m, 0, n), ps[m][:, 0:n], AF.Square), reads=[psb[m]], writes=[cqs.b[0]])
            P.add("act", ACTF(ckf.s(0, 0, n), ps[2][:, 0:n], AF.Identity), reads=[psb[2]], writes=[ckf.b[0]])
            P.add("act", ACTF(cks.s(0, 0, n), ps[2][:, 0:n], AF.Square), reads=[psb[2]], writes=[cks.b[0]])
            P.add("dve", CP(krf.s(0, 0, n, 0, 64), ps[3][0:64, 0:n]), reads=[psb[3]], writes=[krf.b[0]])
            for m in range(2):
                P.add("pe", MM(ps[4][:, 0:n], ones256.ap, cqs.s(m, 0, n), m == 0, m == 1), reads=[cqs.b[0], consts], writes=[psb[4]])
            P.add("pe", MM(ps[5][:, 0:n], ones128.ap, cks.s(0, 0, n), True, True), reads=[cks.b[0], consts], writes=[psb[5]])
            P.add("dve", TS1(rq.s(0, 0, n), ps[4][:, 0:n], RMS_EPS, ALU.add), reads=[psb[4]], writes=[rq.b[0]])
            P.add("dve", TS1(rk.s(0, 0, n), ps[5][:, 0:n], RMS_EPS, ALU.add), reads=[psb[5]], writes=[rk.b[0]])
            P.add("act", ACTF(rq.s(0, 0, n), rq.s(0, 0, n), AF.Sqrt), reads=[], writes=[rq.b[0]])
            P.add("act", ACTF(rk.s(0, 0, n), rk.s(0, 0, n), AF.Sqrt), reads=[], writes=[rk.b[0]])
            P.add("dve", RCP(rq.s(0, 0, n), rq.s(0, 0, n)), reads=[], writes=[rq.b[0]])
            P.add("dve", RCP(rk.s(0, 0, n), rk.s(0, 0, n)), reads=[], writes=[rk.b[0]])
            for m in range(2):
                P.add("dve", STT(cqn.s(m, t0, t0 + n), cqf.s(m, 0, n), col(f"qn{l}", m), rq.s(0, 0, n), ALU.mult, ALU.mult),
                      reads=[cqf.b[0], rq.b[0], consts], writes=[latb[ci]])
            P.add("dve", STT(ckn.s(0, 0, n), ckf.s(0, 0, n), col(f"kvn{l}", 0), rk.s(0, 0, n), ALU.mult, ALU.mult),
                  reads=[ckf.b[0], rk.b[0], consts], writes=[ckn.b[0]])
            P.add("act", ACTF(ckvn.s(0, t0, t0 + n), ckn.s(0, 0, n), AF.Identity), reads=[ckn.b[0]], writes=[latb[ci]])
            P.add("act", ACTF(rot.s(0, 0, n, 0, 32), krf.s(0, 0, n, 32, 64), AF.Identity, scale=-1.0), reads=[krf.b[0]], writes=[rot.b[0]])
            P.add("act", ACTF(rot.s(0, 0, n, 32, 64), krf.s(0, 0, n, 0, 32), AF.Identity), reads=[krf.b[0]], writes=[rot.b[0]])
            P.add("dve", TT(krf.s(0, 0, n, 0, 64), krf.s(0, 0, n, 0, 64), cst.s(0, 0, n, 0, 64), ALU.mult), reads=[cst.b[0], rot.b[0]], writes=[krf.b[0]])
            P.add("dve", TT(rot.s(0, 0, n, 0, 64), rot.s(0, 0, n, 0, 64), cst.s(1, 0, n, 0, 64), ALU.mult), reads=[cst.b[0]], writes=[rot.b[0]])
            P.add("dve", TT(krope.s(0, t0, t0 + n, 0, 64), krf.s(0, 0, n, 0, 64), rot.s(0, 0, n, 0, 64), ALU.add), reads=[krf.b[0], rot.b[0]], writes=[latb[ci]])
            for j, (kt0, kn) in enumerate(KT):
                if kt0 < t0 or kt0 >= t0 + n:
                    continue
                a = kt0 - t0
                P.add("pe", TR(ps[6][0:kn, 0:128], ckn.s(0, a, a + kn), identF.ap), reads=[ckn.b[0], consts], writes=[psb[6]])
                P.add("act", ACTF(Vt.s(j, 0, 128, 0, kn), ps[6][0:kn, 0:128], AF.Identity), reads=[psb[6]], writes=[Vb])
        P.barrier()
        AR.reset(STA)

        wuq = Tl(AR, 2, 1536, BF16, "wuq")
        wukT = Tl(AR, H, 128, BF16, "wukT")
        wuv = Tl(AR, 1, 1024, BF16, "wuv")
        wom = Tl(AR, KC, 1024, BF16, "wom")
        wB = Buf("wB")
        P.add("sp", DMA(wuq.v3(), b_w_uq[l].rearrange("(k p) n -> p k n", p=128)), reads=WC[("uq", l)], writes=[wB], dma=True)
        P.add("sp", DMA(wuv.ap, b_w_uv[l]), reads=WC[("uv", l)], writes=[wB], dma=True)
        P.add("sp", DMA(wom.v3(), b_w_o_mla[l].rearrange("(k p) n -> p k n", p=128)), reads=WC[("om", l)], writes=[wB], dma=True)
        qn_t = [Tl(AR, 1, 512, BF16, f"qn{i}") for i in range(2)]
        qabs = Tl(AR, H, 512, BF16, "qabs", perk=True)
        qrf = [Tl(AR, 1, 512, F32, f"qrf{i}") for i in range(2)]
        qrot = [Tl(AR, 1, 512, F32, f"qrot{i}") for i in range(2)]
        qrope = Tl(AR, H, 512, BF16, "qrope", perk=True)
        cstq = Tl(AR, 2, 512, F32, "cstq")
        qpad = Buf("qpad")
        P.add("pool", MSET(qrope.flat(0, H * 512, 64, 128), 0.0), writes=[qpad])
        NPT = 3
        PT = [Tl(AR, 2, 512, BF16, f"PT{i}") for i in range(NPT)]
        rl = [Tl(AR, 1, 512, F32, f"rl{i}") for i in range(2)]
        psum_t = [Tl(AR, 1, 512, BF16, f"psum{i}") for i in range(2)]
        olat = [Tl(AR, 1, 512, BF16, f"olat{i}") for i in range(2)]
        oT = qabs
        zm1 = Tl(AR, KC, 512, BF16, "zmT")
        wukF_ap = zm1.ap.bitcast(F32)[:, 0:1024]
        P.add("sp", DMA(wukF_ap, w_uk[l]), writes=[zm1.b[0]], dma=True)
        wukTb = Buf("wukT")
        for h in range(H):
            P.add("pe", TR(ps[5][:, 0:128], wukF_ap[:, h * 128:(h + 1) * 128], identF.ap), reads=[zm1.b[0], consts], writes=[psb[5]])
            P.add("act", ACTF(wukT.s(h), ps[5][:, 0:128], AF.Identity), reads=[psb[5]], writes=[wukTb])
        if l == 0:
            cast_after[:] = [wB, zm1.b[0]]
            cast_layer_small(0)
            cast_layer_ffn(0)
            cast_layer_urgent(1)
            cast_layer_small(1)
            cast_layer_ffn(1)
            cast_after[:] = []
        KG = []
        full = [(j, kt0, kn) for j, (kt0, kn) in enumerate(KT) if kn == 128]
        for i in range(0, len(full), 2):
            KG.append(full[i:i + 2])
        for j, (kt0, kn) in enumerate(KT):
            if kn != 128:
                KG.append([(j, kt0, kn)])
        NG = len(KG)
        sgb = [Buf("sg0"), Buf("sg1")]
        B_O, B_L, B_MA, B_MB = 4, 5, 6, 7
        gi = 0
        for ci, (t0, n) in enumerate(TCH):
            P.add("sp", DMA(cstq.v3(n)[0:64], cs2_d[:, :, t0:t0 + n].rearrange("a p t -> p a t")), writes=[cstq.b[0]], dma=True)
            for h in range(H):
                pa, pb_ = B_MA, B_MB
                qn = qn_t[h % 2]
                qf = qrf[h % 2]
                qo = qrot[h % 2]
                for k in range(2):
                    P.add("pe", MM(ps[pa][:, 0:n], wuq.s(k, h * 192, h * 192 + 128), cqn.s(k, t0, t0 + n), k == 0, k == 1),
                          reads=[wB, latb[ci]], writes=[psb[pa]])
                P.add("act", ACTF(qn.s(0, 0, n), ps[pa][:, 0:n], AF.Identity), reads=[psb[pa]], writes=[qn.b[0]])
                for k in range(2):
                    P.add("pe", MM(ps[pb_][0:64, 0:n], wuq.s(k, h * 192 + 128, h * 192 + 192), cqn.s(k, t0, t0 + n), k == 0, k == 1),
                          reads=[wB, latb[ci]], writes=[psb[pb_]])
                P.add("dve", CP(qf.s(0, 0, n, 0, 64), ps[pb_][0:64, 0:n]), reads=[psb[pb_]], writes=[qf.b[0]])
                P.add("pe", MM(ps[pa][:, 0:n], wukT.s(h), qn.s(0, 0, n), True, True), reads=[wukTb, qn.b[0]], writes=[psb[pa]])
                P.add("act", ACTF(qabs.s(h, 0, n), ps[pa][:, 0:n], AF.Identity), reads=[psb[pa]], writes=[qabs.b[h]])
                P.add("act", ACTF(qo.s(0, 0, n, 0, 32), qf.s(0, 0, n, 32, 64), AF.Identity, scale=-1.0), reads=[qf.b[0]], writes=[qo.b[0]])
                P.add("act", ACTF(qo.s(0, 0, n, 32, 64), qf.s(0, 0, n, 0, 32), AF.Identity), reads=[qf.b[0]], writes=[qo.b[0]])
                P.add("dve", TT(qf.s(0, 0, n, 0, 64), qf.s(0, 0, n, 0, 64), cstq.s(0, 0, n, 0, 64), ALU.mult), reads=[cstq.b[0], qo.b[0]], writes=[qf.b[0]])
                P.add("dve", TT(qo.s(0, 0, n, 0, 64), qo.s(0, 0, n, 0, 64), cstq.s(1, 0, n, 0, 64), ALU.mult), reads=[cstq.b[0]], writes=[qo.b[0]])
                P.add("dve", TT(qrope.s(h, 0, n, 0, 64), qf.s(0, 0, n, 0, 64), qo.s(0, 0, n, 0, 64), ALU.add), reads=[qf.b[0], qo.b[0]], writes=[qrope.b[h]])
            pending_vup = None

            def emit_S(h, g, gidx):
                sg = psW[gidx % 2]
                for ti, (j, kt0, kn) in enumerate(KG[g]):
                    kci = kt0 // 512
                    o_ = sg[0:kn, ti * 512:ti * 512 + n]
                    P.add("pe", MM(o_, ckvn.s(0, kt0, kt0 + kn), qabs.s(h, 0, n), True, False), reads=[latb[kci], qabs.b[h]], writes=[sgb[gidx % 2]])
                    P.add("pe", MM(o_, krope.s(0, kt0, kt0 + kn), qrope.s(h, 0, n), False, True), reads=[latb[kci], qrope.b[h], kpad, qpad], writes=[sgb[gidx % 2]])

            def emit_exp(g, gidx):
                sg = psW[gidx % 2]
                pt = PT[gidx % NPT]
                nt = len(KG[g])
                kn = KG[g][0][2]
                src = sg.rearrange("p (t c) -> p t c", t=2)[0:kn, 0:nt, 0:n]
                dst = pt.v3()[0:kn, 0:nt, 0:n]
                P.add("act", ACTF(dst, src, AF.Exp, scale=ATT_SCALE), reads=[sgb[gidx % 2]], writes=[pt.b[0]])
                if nt == 2:
                    pst = psum_t[gidx % 2]
                    P.add("dve", TT(pst.s(0, 0, n, 0, kn), pt.s(0, 0, n, 0, kn), pt.s(1, 0, n, 0, kn), ALU.add), reads=[pt.b[0]], writes=[pst.b[0]])

            def emit_PV(g, gidx):
                pt = PT[gidx % NPT]
                for ti, (j, kt0, kn) in enumerate(KG[g]):
                    first = (g == 0 and ti == 0)
                    lastt = (g == NG - 1 and ti == len(KG[g]) - 1)
                    P.add("pe", MM(ps[B_O][:, 0:n], Vt.s(j, 0, 128, 0, kn), pt.s(ti, 0, n, 0, kn), first, lastt), reads=[Vb, pt.b[0]], writes=[psb[B_O]])
                kn0 = KG[g][0][2]
                if len(KG[g]) == 2:
                    src_, sb_ = psum_t[gidx % 2].s(0, 0, n, 0, kn0), psum_t[gidx % 2].b[0]
                else:
                    src_, sb_ = pt.s(0, 0, n, 0, kn0), pt.b[0]
                P.add("pe", MM(ps[B_L][:, 0:n], ones1.s(0, 0, 128, 0, kn0), src_, g == 0, g == NG - 1), reads=[consts, sb_], writes=[psb[B_L]])

            def emit_vup(h):
                ol = olat[h % 2]
                P.add("pe", MM(ps[B_MA][:, 0:n], wuv.s(0, h * 128, (h + 1) * 128), ol.s(0, 0, n), True, True), reads=[wB, ol.b[0]], writes=[psb[B_MA]])
                P.add("act", ACTF(oT.s(h, 0, n), ps[B_MA][:, 0:n], AF.Identity), reads=[psb[B_MA]], writes=[oT.b[h]])

            for h in range(H):
                gbase = gi
                emit_S(h, 0, gbase)
                emit_exp(0, gbase)
                for g in range(1, NG):
                    emit_S(h, g, gbase + g)
                    emit_exp(g, gbase + g)
                    if g == 2 and pending_vup is not None:
                        emit_vup(pending_vup)
                        pending_vup = None
                    emit_PV(g - 1, gbase + g - 1)
                emit_PV(NG - 1, gbase + NG - 1)
                gi += NG
                r_ = rl[h % 2]
                ol = olat[h % 2]
                P.add("dve", RCP(r_.s(0, 0, n), ps[B_L][:, 0:n]), reads=[psb[B_L]], writes=[r_.b[0]])
                P.add("dve", TT(ol.s(0, 0, n), ps[B_O][:, 0:n], r_.s(0, 0, n), ALU.mult), reads=[psb[B_O], r_.b[0]], writes=[ol.b[0]])
                if pending_vup is not None:
                    emit_vup(pending_vup)
                pending_vup = h
            emit_vup(pending_vup)
            zo = zm1
            for m in range(KC):
                pa = B_MA + (m % 2)
                for hh in range(H):
                    P.add("pe", MM(ps[pa][:, 0:n], wom.s(hh, m * 128, (m + 1) * 128), oT.s(hh, 0, n), hh == 0, hh == H - 1),
                          reads=[wB] + [oT.b[hh]], writes=[psb[pa]])
                P.add("act", ACTF(zo.s(m, 0, n), ps[pa][:, 0:n], AF.Identity, scale=0.5), reads=[psb[pa]], writes=[zo.b[0]])
            P.add("pool", DMA(fm(zm_d)[:, :, t0:t0 + n], zo.v3(n)), reads=[zo.b[0]], dma=True)
            if dbg and l == 0:
                P.add("pool", DMA(fm(dbg_outs["d_zm"])[:, :, t0:t0 + n], zo.v3(n)), reads=[zo.b[0]], dma=True)
        P.barrier()
        AR.reset(LAYER)

        TP = T + 4
        PCS = chunks(T, 1024)
        NPC = len(PCS)
        xpad = Tl(AR, 1, TP, F32, "xpad")
        xc = Tl(AR, 1, T, F32, "xc")
        xcb = Tl(AR, 1, T, BF16, "xcb")
        t_a = Tl(AR, 1, T, F32, "t_a")
        t_i = Tl(AR, 1, T, F32, "t_i")
        t_g = Tl(AR, 1, T, F32, "t_g")
        hf = Tl(AR, 1, T, F32, "hf")
        hbk = t_g
        xc_b = [Buf(f"xc{p}") for p in range(NPC)]
        xcb_b = [Buf(f"xcb{p}") for p in range(NPC)]
        ta_b = [Buf(f"ta{p}") for p in range(NPC)]
        ti_b = [Buf(f"ti{p}") for p in range(NPC)]
        tg_b = [Buf(f"tg{p}") for p in range(NPC)]
        hf_b = [Buf(f"hf{p}") for p in range(NPC)]
        wlx = Tl(AR, KC, 128, BF16, "wlx")
        wlg = Tl(AR, KC, 128, BF16, "wlg")
        wgt = Tl(AR, 4, 128, BF16, "wgt")
        lg_s = [Tl(AR, 1, 512, F32, f"lgs{i}") for i in range(2)]
        ylo = Tl(AR, 1, T, BF16, "ylo")
        bk = 0
        for cc in range(KC):
            P.add("sp", DMA(wlx.v3(), b_w_in[l][:, C_LX + cc * 128:C_LX + (cc + 1) * 128].rearrange("(k p) n -> p k n", p=128)), reads=WC[("inLX", l)], writes=[wlx.b[0]], dma=True)
            P.add("sp", DMA(wlg.v3(), b_w_in[l][:, C_LG + cc * 128:C_LG + (cc + 1) * 128].rearrange("(k p) n -> p k n", p=128)), reads=WC[("inLG", l)], writes=[wlg.b[0]], dma=True)
            for d in range(2):
                P.add("sp", DMA(wgt.s(d), b_w_rg[l][(d * 8 + cc) * 128:(d * 8 + cc + 1) * 128, :]), reads=WC[("rg", l)], writes=[wgt.b[0]], dma=True)
                P.add("sp", DMA(wgt.s(2 + d), b_w_ig[l][(d * 8 + cc) * 128:(d * 8 + cc + 1) * 128, :]), reads=WC[("ig", l)], writes=[wgt.b[0]], dma=True)
            P.add("pool", MSET(xpad.s(0, 0, 2), 0.0), writes=[xpad.b[0]])
            P.add("pool", MSET(xpad.s(0, T + 2, T + 4), 0.0), writes=[xpad.b[0]])
            for ci, (t0, n) in enumerate(TCH):
                pa = ci % 2
                for k in range(KC):
                    P.add("pe", MM(ps[pa][:, 0:n], wlx.s(k), hT.s(k, t0, t0 + n), k == 0, k == KC - 1), reads=[wlx.b[0], hTb[ci]], writes=[psb[pa]])
                P.add("act", ACTF(xpad.s(0, 2 + t0, 2 + t0 + n), ps[pa][:, 0:n], AF.Identity), reads=[psb[pa]], writes=[xpad.b[0]])
            for p, (a0, pn) in enumerate(PCS):
                a1 = a0 + pn
                P.add("act", ACTF(xc.s(0, a0, a1), xpad.s(0, a0, a1), AF.Identity, scale=col(f"lcw{l}", 0 * 8 + cc), bias=col(f"lcb{l}", cc)),
                      reads=[xpad.b[0], consts], writes=[xc_b[p]])
                for j in range(1, 4):
                    P.add("dve", STT(xc.s(0, a0, a1), xpad.s(0, j + a0, j + a1), col(f"lcw{l}", j * 8 + cc), xc.s(0, a0, a1), ALU.mult, ALU.add),
                          reads=[xpad.b[0], consts], writes=[xc_b[p]])
                P.add("act", ACTF(xcb.s(0, a0, a1), xc.s(0, a0, a1), AF.Identity), reads=[xc_b[p]], writes=[xcb_b[p]])
            for d in range(2):
                hdst = hf if d == 0 else hbk
                order = list(range(NPC)) if d == 0 else list(range(NPC - 1, -1, -1))
                for p in order:
                    a0, pn = PCS[p]
                    a1 = a0 + pn
                    for (s0_, sn) in chunks(pn, 512):
                        q0 = a0 + s0_
                        pa, pb_ = 2 + (bk % 2), 4 + (bk % 2)
                        bk += 1
                        P.add("pe", MM(ps[pa][:, 0:sn], wgt.s(d), xcb.s(0, q0, q0 + sn), True, True), reads=[wgt.b[0], xcb_b[p]], writes=[psb[pa]])
                        P.add("pe", MM(ps[pb_][:, 0:sn], wgt.s(2 + d), xcb.s(0, q0, q0 + sn), True, True), reads=[wgt.b[0], xcb_b[p]], writes=[psb[pb_]])
                        P.add("act", ACTF(t_a.s(0, q0, q0 + sn), ps[pa][:, 0:sn], AF.Tanh, scale=0.5, bias=dcol(l, 32 + d * 8 + cc)),
                              reads=[psb[pa], consts], writes=[ta_b[p]])
                        P.add("act", ACTF(t_i.s(0, q0, q0 + sn), ps[pb_][:, 0:sn], AF.Tanh, scale=0.5, bias=dcol(l, 48 + d * 8 + cc)),
                              reads=[psb[pb_], consts], writes=[ti_b[p]])
                    P.add("act", ACTF(t_g.s(0, a0, a1), t_a.s(0, a0, a1), AF.Exp, scale=dcol(l, d * 8 + cc), bias=dcol(l, d * 8 + cc)), reads=[ta_b[p], consts], writes=[tg_b[p]])
                    P.add("act", ACTF(t_a.s(0, a0, a1), t_a.s(0, a0, a1), AF.Exp, scale=dcol(l, 16 + d * 8 + cc), bias=dcol(l, 16 + d * 8 + cc)), reads=[consts], writes=[ta_b[p]])
                    P.add("act", ACTF(t_g.s(0, a0, a1), t_g.s(0, a0, a1), AF.Sqrt, scale=-1.0, bias=onecol.ap[:, 0:1]), reads=[consts], writes=[tg_b[p]])
                    P.add("dve", STT(t_i.s(0, a0, a1), t_i.s(0, a0, a1), 1.0, xc.s(0, a0, a1), ALU.add, ALU.mult), reads=[xc_b[p]], writes=[ti_b[p]])
                    P.add("dve", STT(t_i.s(0, a0, a1), t_g.s(0, a0, a1), 0.5, t_i.s(0, a0, a1), ALU.mult, ALU.mult), reads=[tg_b[p]], writes=[ti_b[p]])
                    if d == 0:
                        init = 0.0 if p == 0 else hf.s(0, a0 - 1, a0)
                        rd = [ta_b[p], ti_b[p]] + ([hf_b[p - 1]] if p > 0 else [])
                        P.add("dve", SCAN(hf.s(0, a0, a1), t_a.s(0, a0, a1), t_i.s(0, a0, a1), init), reads=rd, writes=[hf_b[p]])
                    else:
                        init = 0.0 if p == NPC - 1 else hbk.s(0, a1, a1 + 1)
                        rd = [ta_b[p], ti_b[p]] + ([tg_b[p + 1]] if p < NPC - 1 else [])
                        P.add("dve", SCAN(hbk.ap[:, a0:a1][:, ::-1], t_a.ap[:, a0:a1][:, ::-1], t_i.ap[:, a0:a1][:, ::-1], init), reads=rd, writes=[tg_b[p]])
                        P.add("dve", TT(hf.s(0, a0, a1), hf.s(0, a0, a1), hbk.s(0, a0, a1), ALU.add), reads=[tg_b[p]], writes=[hf_b[p]])
                        for (s0_, sn) in chunks(pn, 512):
                            q0 = a0 + s0_
                            pa = 6 + (bk % 2)
                            gs = lg_s[bk % 2]
                            bk += 1
                            hci = q0 // 512
                            hcs = sorted({q0 // 512, (q0 + sn - 1) // 512})
                            for k in range(KC):
                                P.add("pe", MM(ps[pa][:, 0:sn], wlg.s(k), hT.s(k, q0, q0 + sn), k == 0, k == KC - 1), reads=[wlg.b[0]] + [hTb[c_] for c_ in hcs], writes=[psb[pa]])
                            P.add("act", ACTF(gs.s(0, 0, sn), ps[pa][:, 0:sn], AF.Square, scale=GELU_SQA), reads=[psb[pa]], writes=[gs.b[0]])
                            P.add("dve", STT(gs.s(0, 0, sn), gs.s(0, 0, sn), 1.0, ps[pa][:, 0:sn], ALU.add, ALU.mult), reads=[psb[pa]], writes=[gs.b[0]])
                            P.add("act", ACTF(gs.s(0, 0, sn), gs.s(0, 0, sn), AF.Tanh, scale=GELU_C), reads=[], writes=[gs.b[0]])
                            P.add("dve", STT(gs.s(0, 0, sn), gs.s(0, 0, sn), 1.0, ps[pa][:, 0:sn], ALU.add, ALU.mult), reads=[psb[pa]], writes=[gs.b[0]])
                            P.add("dve", STT(ylo.s(0, q0, q0 + sn), gs.s(0, 0, sn), 0.5, hf.s(0, q0, q0 + sn), ALU.mult, ALU.mult), reads=[gs.b[0], hf_b[p]], writes=[ylo.b[0]])
            P.add("pool", DMA(ylin_d[cc * 128:(cc + 1) * 128, :], ylo.s(0)), reads=[ylo.b[0]], dma=True)
            if dbg and l == 0:
                P.add("pool", DMA(dbg_outs["d_ylin"][cc * 128:(cc + 1) * 128, :], ylo.s(0)), reads=[ylo.b[0]], dma=True)
        P.barrier()
        AR.reset(LAYER)

        wol = Tl(AR, KC, 1024, BF16, "wol")
        wgl = Tl(AR, KC, 1024, BF16, "wgl")
        wot = Tl(AR, KC, 1024, BF16, "wot")
        wgm = Tl(AR, KC, 1024, BF16, "wgm")
        wD = Buf("wD")
        P.add("sp", DMA(wgm.v3(), b_w_in[l][:, C_GM:C_GM + D].rearrange("(k p) n -> p k n", p=128)), reads=WC[("inGM", l)], writes=[wD], dma=True)
        P.add("sp", DMA(wol.v3(), b_w_o_lru[l].rearrange("(k p) n -> p k n", p=128)), reads=WC[("ol", l)], writes=[wD], dma=True)
        P.add("sp", DMA(wgl.v3(), b_w_in[l][:, C_GL:C_GL + D].rearrange("(k p) n -> p k n", p=128)), reads=WC[("inGL", l)], writes=[wD], dma=True)
        P.add("sp", DMA(wot.v3(), b_w_out[l].rearrange("(k p) n -> p k n", p=128)), reads=WC[("out", l)], writes=[wD], dma=True)
        ND = 256
        DCH = chunks(T, ND)
        NDC = len(DCH)
        yl_t = [Tl(AR, KC, ND, BF16, f"yl{i}") for i in range(2)]
        zm_t = [Tl(AR, KC, ND, BF16, f"zmi{i}") for i in range(2)]
        hr_t = [Tl(AR, KC, ND, F32, f"hr{i}") for i in range(3)]
        hbo = Tl(AR, KC, ND, BF16, "hbD")
        thl = [Tl(AR, 1, ND, F32, f"thl{i}") for i in range(2)]
        zb = [Tl(AR, 1, ND, F32, f"zb{i}") for i in range(2)]
        thm2 = [Tl(AR, 1, ND, F32, f"thm2{i}") for i in range(2)]
        zb2 = [Tl(AR, 1, ND, F32, f"zb2{i}") for i in range(2)]
        zT_t = [Tl(AR, KC, ND, BF16, f"zT{i}", perk=True) for i in range(2)]
        ybfD = Tl(AR, KC, ND, BF16, "ybfD")
        ysqD = Tl(AR, KC, ND, BF16, "ysqD")
        stD = Tl(AR, 4, ND, F32, "stD")
        ln_state = {}

        def ph1(ci):
            t0, n = DCH[ci]
            hci = t0 // 512
            yl = yl_t[ci % 2]
            zmi = zm_t[ci % 2]
            hr = hr_t[ci % 3]
            zT = zT_t[ci % 2]
            P.add("sp", DMA(yl.v3(n), fm(ylin_d)[:, :, t0:t0 + n]), writes=[yl.b[0]], dma=True)
            P.add("sp", DMA(zmi.v3(n), fm(zm_d)[:, :, t0:t0 + n]), writes=[zmi.b[0]], dma=True)
            P.add("sp", DMA(hr.v3(n), fm(hres_d)[:, :, t0:t0 + n]), writes=[hr.b[0]], dma=True)
            for m in range(KC):
                pa, pb_, pc = (m % 2), 2 + (m % 2), 4 + (m % 2)
                for k in range(KC):
                    P.add("pe", MM(ps[pa][:, 0:n], wol.s(k, m * 128, (m + 1) * 128), yl.s(k, 0, n), k == 0, k == KC - 1), reads=[wD, yl.b[0]], writes=[psb[pa]])
                for k in range(KC):
                    P.add("pe", MM(ps[pb_][:, 0:n], wgl.s(k, m * 128, (m + 1) * 128), hT.s(k, t0, t0 + n), k == 0, k == KC - 1), reads=[wD, hTb[hci]], writes=[psb[pb_]])
                for k in range(KC):
                    P.add("pe", MM(ps[pc][:, 0:n], wgm.s(k, m * 128, (m + 1) * 128), hT.s(k, t0, t0 + n), k == 0, k == KC - 1), reads=[wD, hTb[hci]], writes=[psb[pc]])
                th = thl[m % 2]
                th2 = thm2[m % 2]
                z = zb[m % 2]
                z2 = zb2[m % 2]
                P.add("act", ACTF(th.s(0, 0, n), ps[pb_][:, 0:n], AF.Tanh, scale=0.5), reads=[psb[pb_]], writes=[th.b[0]])
                P.add("act", ACTF(th2.s(0, 0, n), ps[pc][:, 0:n], AF.Tanh, scale=0.5), reads=[psb[pc]], writes=[th2.b[0]])
                P.add("dve", STT(z.s(0, 0, n), th.s(0, 0, n), 1.0, ps[pa][:, 0:n], ALU.add, ALU.mult), reads=[th.b[0], psb[pa]], writes=[z.b[0]])
                P.add("dve", STT(z2.s(0, 0, n), th2.s(0, 0, n), 1.0, zmi.s(m, 0, n), ALU.add, ALU.mult), reads=[th2.b[0], zmi.b[0]], writes=[z2.b[0]])
                P.add("dve", STT(zT.s(m, 0, n), z.s(0, 0, n), 0.5, z2.s(0, 0, n), ALU.mult, ALU.add), reads=[z.b[0], z2.b[0]], writes=[zT.b[m]])

        def ph2(ci):
            t0, n = DCH[ci]
            hr = hr_t[ci % 3]
            zT = zT_t[ci % 2]
            for m in range(KC):
                for k in range(KC):
                    P.add("pe", MM(ps[6][:, 0:n], wot.s(k, m * 128, (m + 1) * 128), zT.s(k, 0, n), k == 0, k == KC - 1), reads=[wD, zT.b[k]], writes=[psb[6]])
                P.add("dve", STT(hr.s(m, 0, n), hr.s(m, 0, n), DN_ALPHA, ps[6][:, 0:n], ALU.mult, ALU.add), reads=[psb[6]], writes=[hr.b[0]])
            pre, pe_, post = ln_parts(hr, n, f"l1g{l}", f"l1b{l}", ybfD, ysqD, stD, (ps[7][:, 0:n], psb[7]), (ps[7][:, 256:256 + n], psb[7]), hbo)
            pre()
            ln_state[ci] = (pe_, post)

        def ph3(ci):
            t0, n = DCH[ci]
            hr = hr_t[ci % 3]
            pe_, post = ln_state.pop(ci)
            pe_()
            post()
            P.add("pool", DMA(fm(hres_d)[:, :, t0:t0 + n], hr.v3(n)), reads=[hr.b[0]], dma=True)
            P.add("pool", DMA(fm(hbfB_d)[:, :, t0:t0 + n], hbo.v3(n)), reads=[hbo.b[0]], dma=True)
            if dbg and l == 0:
                P.add("pool", DMA(fm(dbg_outs["d_hA1"])[:, :, t0:t0 + n], hr.v3(n)), reads=[hr.b[0]], dma=True)

        ph1(0)
        if NDC > 1:
            ph1(1)
        ph2(0)
        for ci in range(2, NDC):
            ph1(ci)
            ph3(ci - 2)
            ph2(ci - 1)
        if NDC > 1:
            ph3(NDC - 2)
            ph2(NDC - 1)
        ph3(NDC - 1)
        P.barrier()
        AR.reset(PERSIST)

        wdn = Tl(AR, NFP, 1024, BF16, "wdn")
        wE = Buf("wE")
        for j0, jn in chunks(NFP, 8):
            P.add("sp", DMA(wdn.v3()[:, j0:j0 + jn, :], b_w_down[l][j0 * 128:(j0 + jn) * 128, :].rearrange("(k p) n -> p k n", p=128)),
                  reads=WC[("dn", l)], writes=[wE], dma=True)
        dg = Tl(AR, 4 * NFC, 128, BF16, "dg")
        dgb = Buf("dg")
        id3 = identF.ap.unsqueeze(1).broadcast_to([128, NFC, 128])
        for tap in range(4):
            if tap < 3:
                o_, _ = COL_LAY[f"fcw{l}"]
                o_ += tap * NFC
            else:
                o_, _ = COL_LAY[f"fcb{l}"]
            c3 = colsT.ap[:, o_:o_ + NFC].unsqueeze(2).broadcast_to([128, NFC, 128])
            P.add("dve", TT(dg.v3()[:, tap * NFC:(tap + 1) * NFC, :], id3, c3, ALU.mult), reads=[consts], writes=[dgb])
        onesN = Tl(AR, 1, 512, BF16, "onesN")
        P.add("pool", MSET(onesN.ap, 1.0), writes=[dgb])
        NWB = 3
        wupg = [Tl(AR, KC, 128, BF16, f"wupg{i}") for i in range(NWB)]
        wupv = [Tl(AR, KC, 128, BF16, f"wupv{i}") for i in range(NWB)]
        hin = [Tl(AR, KC, 512, BF16, f"hin{i}") for i in range(2)]
        gact = Tl(AR, NFP, 510, BF16, "gact", perk=True)
        ug = [Tl(AR, 1, 512, BF16, f"ug{i}") for i in range(2)]
        uv = [Tl(AR, 1, 512, BF16, f"uv{i}") for i in range(2)]
        gq = [Tl(AR, 1, 512, F32, f"gq{i}") for i in range(2)]
        hrE = Tl(AR, KC, 512, F32, "hrE")
        hbE = Tl(AR, KC, 512, BF16, "hbE")
        ybfE = Tl(AR, KC, 512, BF16, "ybfE")
        ysqE = Tl(AR, KC, 512, BF16, "ysqE")
        stE = Tl(AR, 4, 512, F32, "stE")
        wi = 0
        deferred = []

        def emit_conv(ci, j, t0, n, off, nn):
            for (src, mcol, pb) in ((ug[j % 2], j, 4), (uv[j % 2], NFP + j, 5)):
                a = 1 if t0 == 0 else 0
                bnd = n - 1 if t0 + n == T else n
                P.add("pe", MM(ps[pb][:, 0:n], dg.s(1 * NFC + mcol), src.s(0, off, off + n), True, False), reads=[dgb, src.b[0]], writes=[psb[pb]])
                P.add("pe", MM(ps[pb][:, a:n], dg.s(0 * NFC + mcol), src.s(0, off + a - 1, off + n - 1), False, False), reads=[dgb, src.b[0]], writes=[psb[pb]])
                P.add("pe", MM(ps[pb][:, 0:bnd], dg.s(2 * NFC + mcol), src.s(0, off + 1, off + 1 + bnd), False, False), reads=[dgb, src.b[0]], writes=[psb[pb]])
                P.add("pe", MM(ps[pb][:, 0:n], dg.s(3 * NFC + mcol), onesN.s(0, 0, n), False, True), reads=[dgb], writes=[psb[pb]])
            q_ = gq[j % 2]
            P.add("act", ACTF(q_.s(0, 0, n), ps[4][:, 0:n], AF.Square, scale=GELU_SQA), reads=[psb[4]], writes=[q_.b[0]])
            P.add("dve", STT(q_.s(0, 0, n), q_.s(0, 0, n), 1.0, ps[4][:, 0:n], ALU.add, ALU.mult), reads=[psb[4]], writes=[q_.b[0]])
            P.add("act", ACTF(q_.s(0, 0, n), q_.s(0, 0, n), AF.Tanh, scale=GELU_C), reads=[], writes=[q_.b[0]])
            P.add("dve", STT(q_.s(0, 0, n), q_.s(0, 0, n), 1.0, ps[4][:, 0:n], ALU.add, ALU.mult), reads=[psb[4]], writes=[q_.b[0]])
            P.add("dve", STT(gact.s(j, 0, n), q_.s(0, 0, n), 0.5, ps[5][:, 0:n], ALU.mult, ALU.mult), reads=[q_.b[0], psb[5]], writes=[gact.b[j]])

        for ci, (t0, n) in enumerate(FCH):
            lo = max(t0 - 1, 0)
            hi = min(t0 + n + 1, T)
            nn = hi - lo
            off = t0 - lo
            hi_t = hin[ci % 2]
            P.add("sp", DMA(hi_t.v3(nn), fm(hbfB_d)[:, :, lo:hi]), writes=[hi_t.b[0]], dma=True)
            prev = None
            for j in range(NFP):
                wg = wupg[wi % NWB]
                wv = wupv[wi % NWB]
                wi += 1
                P.add("sp", DMA(wg.ap, b_w_up[l][j]), reads=WC[("up", l, j)], writes=[wg.b[0]], dma=True)
                P.add("sp", DMA(wv.ap, b_w_up[l][NFP + j]), reads=WC[("up", l, NFP + j)], writes=[wv.b[0]], dma=True)
                if j == 8:
                    P.add("sp", DMA(hrE.v3(n), fm(hres_d)[:, :, t0:t0 + n]), writes=[hrE.b[0]], dma=True)
                pg, pv = (j % 2), 2 + (j % 2)
                for k in range(KC):
                    P.add("pe", MM(ps[pg][:, 0:nn], wg.s(k), hi_t.s(k, 0, nn), k == 0, k == KC - 1), reads=[wg.b[0], hi_t.b[0]], writes=[psb[pg]])
                for k in range(KC):
                    P.add("pe", MM(ps[pv][:, 0:nn], wv.s(k), hi_t.s(k, 0, nn), k == 0, k == KC - 1), reads=[wv.b[0], hi_t.b[0]], writes=[psb[pv]])
                P.add("act", ACTF(ug[j % 2].s(0, 0, nn), ps[pg][:, 0:nn], AF.Identity), reads=[psb[pg]], writes=[ug[j % 2].b[0]])
                P.add("act", ACTF(uv[j % 2].s(0, 0, nn), ps[pv][:, 0:nn], AF.Identity), reads=[psb[pv]], writes=[uv[j % 2].b[0]])
                if j == 3 and deferred:
                    for f in deferred:
                        f()
                    deferred = []
                if prev is not None:
                    emit_conv(ci, *prev)
                prev = (j, t0, n, off, nn)
            emit_conv(ci, *prev)
            for m in range(KC):
                pa = 6 + (m % 2)
                for j in range(NFP):
                    P.add("pe", MM(ps[pa][:, 0:n], wdn.s(j, m * 128, (m + 1) * 128), gact.s(j, 0, n), j == 0, j == NFP - 1), reads=[wE, gact.b[j]], writes=[psb[pa]])
                P.add("dve", STT(hrE.s(m, 0, n), hrE.s(m, 0, n), DN_ALPHA, ps[pa][:, 0:n], ALU.mult, ALU.add), reads=[psb[pa]], writes=[hrE.b[0]])
            pre, pe_, post = ln_parts(hrE, n, f"l2g{l}", f"l2b{l}", ybfE, ysqE, stE, 6, 7, None if last else hbE)
            pre()

            def outs(t0=t0, n=n):
                if not last:
                    P.add("pool", DMA(fm(hres_d)[:, :, t0:t0 + n], hrE.v3(n)), reads=[hrE.b[0]], dma=True)
                    P.add("pool", DMA(fm(hbfA_d)[:, :, t0:t0 + n], hbE.v3(n)), reads=[hbE.b[0]], dma=True)
                    if dbg:
                        P.add("pool", DMA(fm(dbg_outs["d_hE1"])[:, :, t0:t0 + n], hrE.v3(n)), reads=[hrE.b[0]], dma=True)
                else:
                    a = max(NMETA - t0, 0)
                    if a < n:
                        P.add("pool", DMA(fm(outT)[:, :, t0 + a - NMETA:t0 + n - NMETA], hrE.v3()[:, :, a:n]), reads=[hrE.b[0]], dma=True)

            deferred = [pe_, post, outs]
        for f in deferred:
            f()
        deferred = []
        P.barrier()
        AR.reset(PERSIST)

    with nc.Block() as block:
        P.finalize_and_emit(block)
    return nc


def _colpack(v):
    v = np.asarray(v, np.float32).reshape(-1)
    return np.ascontiguousarray(v.reshape(-1, 128).T)


def pack_cols(inp):
    cols = np.zeros((128, NCOL), np.float32)

    def put(name, arr):
        o, n = COL_LAY[name]
        a = _colpack(arr)
        assert a.shape[1] == n, (name, a.shape, n)
        cols[:, o:o + n] = a

    put("ln0_g", inp["ln0_g"])
    put("ln0_b", inp["ln0_b"])
    for l in range(L):
        put(f"qn{l}", inp["q_norm"][l])
        put(f"kvn{l}", inp["kv_norm"][l])
        put(f"lcw{l}", inp["lru_conv_w"][l])
        put(f"lcb{l}", inp["lru_conv_b"][l])
        put(f"brg{l}", inp["b_rg"][l])
        put(f"big{l}", inp["b_ig"][l])
        put(f"lam{l}", inp["lru_lambda"][l])
        put(f"l1g{l}", inp["ln1_g"][l])
        put(f"l1b{l}", inp["ln1_b"][l])
        put(f"fcw{l}", inp["ffn_conv_w"][l])
        put(f"fcb{l}", inp["ffn_conv_b"][l])
        put(f"l2g{l}", inp["ln2_g"][l])
        put(f"l2b{l}", inp["ln2_b"][l])
    return cols


def rope_tables(T):
    half = 32
    inv_freq = np.exp(-math.log(10000.0) * np.arange(half, dtype=np.float32) / half).astype(np.float32)
    ang = np.arange(T, dtype=np.float32)[:, None] * inv_freq[None, :]
    cos = np.cos(ang).astype(np.float32).T
    sin = np.sin(ang).astype(np.float32).T
    return np.ascontiguousarray(np.stack([np.concatenate([cos, cos], 0), np.concatenate([sin, sin], 0)], 0))


def make_in_maps(inp, nb, SEQ):
    f = lambda a: np.ascontiguousarray(np.asarray(a, np.float32))
    T = SEQ + NMETA
    shared = {
        "metaT": f(np.asarray(inp["meta_tokens"]).T),
        "cols": pack_cols({k: np.asarray(v) for k, v in inp.items()}),
        "cs2": rope_tables(T),
        "ident": np.eye(128, dtype=np.float32),
        "w_in": f(inp["w_in"]),
        "w_uq": f(np.asarray(inp["w_uq"]).reshape(L, 256, 1536)),
        "w_uk": f(np.asarray(inp["w_uk"]).reshape(L, 128, 1024)),
        "w_uv": f(np.asarray(inp["w_uv"]).reshape(L, 128, 1024)),
        "w_o_mla": f(inp["w_o_mla"]),
        "w_rg": f(np.asarray(inp["w_rg"]).reshape(L, 2048, 128)),
        "w_ig": f(np.asarray(inp["w_ig"]).reshape(L, 2048, 128)),
        "w_o_lru": f(inp["w_o_lru"]),
        "w_out": f(inp["w_out"]),
        "w_up": f(inp["w_up"]),
        "w_down": f(inp["w_down"]),
    }
    x = np.asarray(inp["x"], np.float32)
    maps = []
    for b in range(nb):
        m = dict(shared)
        m["xT"] = np.ascontiguousarray(x[b].T)
        maps.append(m)
    return maps


_NC_CACHE = {}


def kernel(**inputs):
    x = np.asarray(inputs["x"])
    B, SEQ, _ = x.shape
    if SEQ not in _NC_CACHE:
        _NC_CACHE[SEQ] = build(SEQ)
    nc = _NC_CACHE[SEQ]
    maps = make_in_maps(inputs, B, SEQ)
    res = run_bass_kernel_spmd(nc, maps, core_ids=list(range(B)))
    out = np.stack([np.asarray(r["outT"]).T for r in res.results], 0)
    return np.ascontiguousarray(out.astype(np.float32))
```
